# Optimizing a Trainium2 kernel written in Bass

```python
import jax, jax.numpy as jnp
from jax import lax
import numpy as np

D_MODEL = 1024
BATCH = 2
SEQ = 8192
DEPTH = 1
DEC_BATCH = 16
DEC_SEQ = 2048
PAST_LEN = 128

PLE_DIM = 256
GRID_W = 64
WIN_R = 8
WIN_C = 16
Q_BLOCK_C = 16
K_BLOCK_C = 32
NA_HEADS = 8
NA_HEAD_DIM = 64
NA_WIDTH = NA_HEADS * NA_HEAD_DIM
RW_HEADS = 8
RW_HEAD_DIM = 64
RW_WIDTH = RW_HEADS * RW_HEAD_DIM
DECAY_RANK = 64
AAA_RANK = 64
GATE_RANK = 128
D_FF = 4 * D_MODEL
N_BRANCH = 2
EPS = 1e-6
GN_EPS = 64e-5
DECAY_SCALE = 0.606531
NEG_INF = -1e30

NA_COLS = 3 * NA_WIDTH
RW_COLS = 3 * RW_WIDTH + 2 * DECAY_RANK + AAA_RANK + GATE_RANK
GATE_COLS = N_BRANCH * D_MODEL
IN_COLS = NA_COLS + RW_COLS + GATE_COLS

kernel_name = "hybrid_natten_rwkv7_bidir_encoder"


def rms_norm(x, g):
    xf = x.astype(jnp.float32)
    y = xf * lax.rsqrt(jnp.mean(xf * xf, axis=-1, keepdims=True) + EPS)
    return (y * g.astype(jnp.float32)).astype(x.dtype)


def neighbourhood_attention(q, k, v, rel_bias):
    B, L, H, dh = q.shape
    rows = L // GRID_W
    kr = min(WIN_R, rows)
    n_cb = GRID_W // Q_BLOCK_C
    nk = kr * K_BLOCK_C
    row_start = np.clip(np.arange(rows) - kr // 2, 0, rows - kr)
    key_rows = row_start[:, None] + np.arange(kr)[None, :]
    blk_start = np.clip(np.arange(n_cb) * Q_BLOCK_C - WIN_C // 2, 0, GRID_W - K_BLOCK_C)
    key_cols = blk_start[:, None] + np.arange(K_BLOCK_C)[None, :]
    idx = (key_rows[:, None, :, None] * GRID_W + key_cols[None, :, None, :]).reshape(rows, n_cb, nk)
    q_cols = np.arange(n_cb)[:, None] * Q_BLOCK_C + np.arange(Q_BLOCK_C)[None, :]
    col_start = np.clip(q_cols - WIN_C // 2, 0, GRID_W - WIN_C)
    in_win = (key_cols[:, None, :] >= col_start[..., None]) & (key_cols[:, None, :] < col_start[..., None] + WIN_C)
    mask = np.broadcast_to(in_win[:, :, None, :], (n_cb, Q_BLOCK_C, kr, K_BLOCK_C))
    dr_idx = key_rows - np.arange(rows)[:, None] + WIN_R - 1
    dc_idx = np.clip(key_cols[:, None, :] - q_cols[..., None] + WIN_C - 1, 0, 2 * WIN_C - 2)
    bias = rel_bias.astype(jnp.float32)[:, dr_idx[:, None, None, :, None], dc_idx[None, :, :, None, :]]
    bias = jnp.where(mask[None, None], bias, NEG_INF).reshape(H, rows, n_cb, Q_BLOCK_C, nk)
    qb = q.reshape(B, rows, n_cb, Q_BLOCK_C, H, dh)
    kg = k[:, idx]
    vg = v[:, idx]
    s = jnp.einsum('brcqhd,brckhd->bhrcqk', qb, kg).astype(jnp.float32) * (dh ** -0.5) + bias[None]
    prob = jax.nn.softmax(s, axis=-1).astype(v.dtype)
    o = jnp.einsum('bhrcqk,brckhd->brcqhd', prob, vg)
    return o.reshape(B, L, H, dh)


def centred_shift(u, w):
    prev = jnp.pad(u[:, :-1], ((0, 0), (1, 0), (0, 0)))
    nxt = jnp.pad(u[:, 1:], ((0, 0), (0, 1), (0, 0)))
    return prev * w[0] + u * w[1] + nxt * w[2]


def rwkv7_bidir(u, w0_f, w_up_f, w0_b, w_up_b, a0, a_up, g_up, k_k, k_a, r_k, ln_x_w, ln_x_b):
    B, L, _ = u.shape
    f32 = lambda t: t.astype(jnp.float32)
    splits = np.cumsum([RW_WIDTH, RW_WIDTH, RW_WIDTH, DECAY_RANK, DECAY_RANK, AAA_RANK])
    r, k, v, dwf, dwb, da, dg = jnp.split(u, splits, axis=-1)
    decay_f = jnp.exp(-DECAY_SCALE * jax.nn.sigmoid(f32(w0_f + jnp.tanh(dwf) @ w_up_f)))
    decay_b = jnp.exp(-DECAY_SCALE * jax.nn.sigmoid(f32(w0_b + jnp.tanh(dwb) @ w_up_b)))
    a = jax.nn.sigmoid(f32(a0 + da @ a_up))
    g = jax.nn.sigmoid(dg) @ g_up
    heads = lambda t: t.reshape(B, L, RW_HEADS, RW_HEAD_DIM)
    kk = heads(f32(k * k_k))
    kk = kk / jnp.maximum(jnp.sqrt(jnp.sum(kk * kk, axis=-1, keepdims=True)), 1e-12)
    kt = f32(k) * (1.0 + (a - 1.0) * f32(k_a))
    rh, kh, vh, ah = heads(f32(r)), heads(kt), heads(f32(v)), heads(a)

    def step(S, inp):
        r_t, w_t, k_t, v_t, kk_t, a_t = inp
        sa = jnp.einsum('bhvk,bhk->bhv', S, -kk_t)
        S = S * w_t[:, :, None, :] + sa[..., None] * (kk_t * a_t)[:, :, None, :] + v_t[..., None] * k_t[:, :, None, :]
        return S, jnp.einsum('bhvk,bhk->bhv', S, r_t)

    tm = lambda t: jnp.moveaxis(t, 1, 0)
    S0 = jnp.zeros((B, RW_HEADS, RW_HEAD_DIM, RW_HEAD_DIM), jnp.float32)
    r_tm, k_tm, v_tm, kk_tm, a_tm = tm(rh), tm(kh), tm(vh), tm(kk), tm(ah)
    _, y_f = lax.scan(step, S0, (r_tm, tm(heads(decay_f)), k_tm, v_tm, kk_tm, a_tm))
    _, y_b = lax.scan(step, S0, (r_tm, tm(heads(decay_b)), k_tm, v_tm, kk_tm, a_tm), reverse=True)
    o = jnp.moveaxis(y_f + y_b, 0, 1)
    mu = jnp.mean(o, axis=-1, keepdims=True)
    var = jnp.mean(jnp.square(o - mu), axis=-1, keepdims=True)
    o = ((o - mu) * lax.rsqrt(var + GN_EPS)).reshape(B, L, RW_WIDTH) * f32(ln_x_w) + f32(ln_x_b)
    bonus = jnp.sum(rh * kh * f32(r_k), axis=-1, keepdims=True) * vh
    out = (o + bonus.reshape(B, L, RW_WIDTH)) * f32(g)
    return out.astype(u.dtype)


def encoder_layer(x, p, g_mix, w_in, conv_w, q_gain, k_gain, rel_bias, w0_f, w_up_f, w0_b, w_up_b,
                  a0, a_up, g_up, k_k, k_a, r_k, ln_x_w, ln_x_b, w_a_out, w_b_out, w_o,
                  g_ffn, w_ff1, w_ff2, w_ple, g_ple, w_pgate):
    B, L, _ = x.shape
    n = rms_norm(x, g_mix)
    z = n @ w_in
    z_na, z_rw, z_gate = jnp.split(z, [NA_COLS, NA_COLS + RW_COLS], axis=-1)
    q, k, v = [t.reshape(B, L, NA_HEADS, NA_HEAD_DIM) for t in jnp.split(z_na, 3, axis=-1)]
    q = rms_norm(q, q_gain)
    k = rms_norm(k, k_gain)
    y_a = neighbourhood_attention(q, k, v, rel_bias).reshape(B, L, NA_WIDTH)
    y_b = rwkv7_bidir(centred_shift(z_rw, conv_w), w0_f, w_up_f, w0_b, w_up_b, a0, a_up, g_up,
                      k_k, k_a, r_k, ln_x_w, ln_x_b)
    gate_a, gate_b = jnp.split(z_gate, N_BRANCH, axis=-1)
    merged = jax.nn.sigmoid(gate_a) * (y_a @ w_a_out) + jax.nn.sigmoid(gate_b) * (y_b @ w_b_out)
    h = x + merged @ w_o
    h = h + jnp.square(jax.nn.relu(rms_norm(h, g_ffn) @ w_ff1)) @ w_ff2
    h = h + jax.nn.sigmoid(rms_norm(h, g_ple) @ w_pgate) * (p @ w_ple)
    return h


def setup_inputs(seed: int = 0) -> dict:
    key = jax.random.key(seed)
    ks = jax.random.split(key, 32)
    nrm = lambda k, shape, scale: jax.random.normal(k, shape, jnp.float32) * scale
    conv_base = jnp.array([0.25, 0.5, 0.25], jnp.float32)[None, :, None]
    return {
        'x_prompt': nrm(ks[0], (BATCH, SEQ, D_MODEL), 1.0),
        'x_sample': nrm(ks[1], (DEC_BATCH, DEC_SEQ, D_MODEL), 1.0),
        'p_prompt': nrm(ks[2], (DEPTH, BATCH, SEQ, PLE_DIM), 1.0),
        'p_sample': nrm(ks[3], (DEPTH, DEC_BATCH, DEC_SEQ, PLE_DIM), 1.0),
        'g_mix': 1.0 + nrm(ks[4], (DEPTH, D_MODEL), 0.02),
        'w_in': nrm(ks[5], (DEPTH, D_MODEL, IN_COLS), D_MODEL ** -0.5),
        'conv_w': conv_base + nrm(ks[6], (DEPTH, 3, RW_COLS), 0.05),
        'q_gain': 1.0 + nrm(ks[7], (DEPTH, NA_HEAD_DIM), 0.02),
        'k_gain': 1.0 + nrm(ks[8], (DEPTH, NA_HEAD_DIM), 0.02),
        'rel_bias': nrm(ks[9], (DEPTH, NA_HEADS, 2 * WIN_R - 1, 2 * WIN_C - 1), 0.02),
        'w0_f': -1.0 + nrm(ks[10], (DEPTH, RW_WIDTH), 0.5),
        'w_up_f': nrm(ks[11], (DEPTH, DECAY_RANK, RW_WIDTH), 0.1 * DECAY_RANK ** -0.5),
        'w0_b': -1.0 + nrm(ks[12], (DEPTH, RW_WIDTH), 0.5),
        'w_up_b': nrm(ks[13], (DEPTH, DECAY_RANK, RW_WIDTH), 0.1 * DECAY_RANK ** -0.5),
        'a0': nrm(ks[14], (DEPTH, RW_WIDTH), 0.1),
        'a_up': nrm(ks[15], (DEPTH, AAA_RANK, RW_WIDTH), 0.1 * AAA_RANK ** -0.5),
        'g_up': nrm(ks[16], (DEPTH, GATE_RANK, RW_WIDTH), GATE_RANK ** -0.5),
        'k_k': 0.85 + nrm(ks[17], (DEPTH, RW_WIDTH), 0.02),
        'k_a': 1.0 + nrm(ks[18], (DEPTH, RW_WIDTH), 0.02),
        'r_k': nrm(ks[19], (DEPTH, RW_HEADS, RW_HEAD_DIM), 0.1),
        'ln_x_w': 1.0 + nrm(ks[20], (DEPTH, RW_WIDTH), 0.02),
        'ln_x_b': nrm(ks[21], (DEPTH, RW_WIDTH), 0.01),
        'w_a_out': nrm(ks[22], (DEPTH, NA_WIDTH, D_MODEL), NA_WIDTH ** -0.5),
        'w_b_out': nrm(ks[23], (DEPTH, RW_WIDTH, D_MODEL), RW_WIDTH ** -0.5),
        'w_o': nrm(ks[24], (DEPTH, D_MODEL, D_MODEL), D_MODEL ** -0.5),
        'g_ffn': 1.0 + nrm(ks[25], (DEPTH, D_MODEL), 0.02),
        'w_ff1': nrm(ks[26], (DEPTH, D_MODEL, D_FF), D_MODEL ** -0.5),
        'w_ff2': nrm(ks[27], (DEPTH, D_FF, D_MODEL), D_FF ** -0.5),
        'w_ple': nrm(ks[28], (DEPTH, PLE_DIM, D_MODEL), PLE_DIM ** -0.5),
        'g_ple': 1.0 + nrm(ks[29], (DEPTH, D_MODEL), 0.02),
        'w_pgate': nrm(ks[30], (DEPTH, D_MODEL, D_MODEL), D_MODEL ** -0.5),
    }


def reference(x_prompt, x_sample, p_prompt, p_sample, g_mix, w_in, conv_w, q_gain, k_gain, rel_bias,
              w0_f, w_up_f, w0_b, w_up_b, a0, a_up, g_up, k_k, k_a, r_k, ln_x_w, ln_x_b,
              w_a_out, w_b_out, w_o, g_ffn, w_ff1, w_ff2, w_ple, g_ple, w_pgate):
    def run(h, p):
        for i in range(DEPTH):
            h = encoder_layer(h, p[i], g_mix[i], w_in[i], conv_w[i], q_gain[i], k_gain[i], rel_bias[i],
                              w0_f[i], w_up_f[i], w0_b[i], w_up_b[i], a0[i], a_up[i], g_up[i],
                              k_k[i], k_a[i], r_k[i], ln_x_w[i], ln_x_b[i], w_a_out[i], w_b_out[i], w_o[i],
                              g_ffn[i], w_ff1[i], w_ff2[i], w_ple[i], g_ple[i], w_pgate[i])
        return h
    y_prompt = run(x_prompt, p_prompt)
    y_sample = run(x_sample, p_sample)
    return (y_prompt, y_sample)
```

```python
import contextlib
import numpy as np
import concourse.bass as bass
import concourse.mybir as mybir
from concourse.bass_utils import run_bass_kernel_spmd

F32 = mybir.dt.float32
BF16 = mybir.dt.bfloat16
ALU = mybir.AluOpType
AF = mybir.ActivationFunctionType
AX = mybir.AxisListType

D = 1024
DS = 0.606531
EPS = 1e-6
GN_EPS = 64e-5
RWC = 1856
NCORE = 8


class Buf:
    __slots__ = ("t", "lw", "rd", "name")

    def __init__(self, t, name=""):
        self.t = t
        self.lw = None
        self.rd = {}
        self.name = name

    def __getitem__(self, k):
        return self.t[k]


class Prog:
    ENG = ("pe", "act", "dve", "pool", "sp")

    def __init__(self, nc, stack):
        self.nc = nc
        self.stack = stack
        self.q = {e: [] for e in self.ENG}
        self.sems = {}
        self.cnt = {}
        self.seen = {e: {} for e in self.ENG}
        for e in self.ENG:
            self.sems[e] = stack.enter_context(nc.semaphore("s_" + e))
            self.cnt[e] = 0
        self.n_inst = 0
        self.rr = 0
        self.dma_pool = {}
        self.dma_rr = {}
        self.DMA_POOL_SIZE = {"sp": 32, "act": 16, "pool": 8}

    def sb(self, shape, dt, name, stack=None):
        self.uid = getattr(self, "uid", 0) + 1
        name = "%s_%d" % (name, self.uid)
        t = (stack or self.stack).enter_context(self.nc.sbuf_tensor(name, list(shape), dt))
        return Buf(t, name)

    def ps(self, shape, dt, name, stack=None):
        self.uid = getattr(self, "uid", 0) + 1
        name = "%s_%d" % (name, self.uid)
        t = (stack or self.stack).enter_context(self.nc.psum_tensor(name, list(shape), dt))
        return Buf(t, name)

    def dma_sem(self, name):
        s = self.stack.enter_context(self.nc.semaphore(name))
        self.sems[name] = s
        self.cnt[name] = 0
        return name

    def _waits(self, eng, reads, writes):
        need = {}
        for b in reads:
            if b.lw is not None:
                k, v = b.lw
                need[k] = max(need.get(k, 0), v)
        for b in writes:
            if b.lw is not None:
                k, v = b.lw
                need[k] = max(need.get(k, 0), v)
            for k, v in b.rd.items():
                need[k] = max(need.get(k, 0), v)
        out = []
        seen = self.seen[eng]
        for k, v in need.items():
            if k == eng and eng in ("pe", "sp"):
                continue
            if seen.get(k, 0) < v:
                seen[k] = v
                out.append((k, v))
        return out

    def op(self, eng, fn, reads=(), writes=(), sig=True):
        waits = self._waits(eng, reads, writes)
        if sig:
            self.cnt[eng] += 1
            val = self.cnt[eng]
        else:
            val = self.cnt[eng] + 1
        sem = self.sems[eng]
        sems = self.sems

        def run(e):
            for k, v in waits:
                e.wait_ge(sems[k], v)
            ins = fn(e)
            if sig:
                ins.then_inc(sem, 1)
        self.q[eng].append(run)
        self.n_inst += 1
        for b in reads:
            b.rd[eng] = max(b.rd.get(eng, 0), val)
        for b in writes:
            b.lw = (eng, val)
            b.rd = {}

    def dma(self, eng, out_ap, in_ap, reads=(), writes=(), sem=None):
        pool = self.dma_pool.setdefault(eng, [])
        if not pool:
            for i in range(self.DMA_POOL_SIZE.get(eng, 8)):
                pool.append(self.dma_sem("dp_%s_%d" % (eng, i)))
            self.dma_rr[eng] = 0
        sem = pool[self.dma_rr[eng] % len(pool)]
        self.dma_rr[eng] += 1
        waits = self._waits(eng, reads, writes)
        prev = self.cnt[sem]
        if prev > 0 and self.seen[eng].get(sem, 0) < prev:
            self.seen[eng][sem] = prev
            waits = [w for w in waits if w[0] != sem] + [(sem, prev)]
        self.cnt[sem] += 16
        val = self.cnt[sem]
        s = self.sems[sem]
        sems = self.sems

        def run(e):
            for k, v in waits:
                e.wait_ge(sems[k], v)
            e.dma_start(out=out_ap, in_=in_ap).then_inc(s, 16)
        self.q[eng].append(run)
        self.n_inst += 1
        for b in reads:
            b.rd[sem] = max(b.rd.get(sem, 0), val)
        for b in writes:
            b.lw = (sem, val)
            b.rd = {}

    def barrier(self):
        tot = dict(self.cnt)
        sems = self.sems
        for eng in self.ENG:
            waits = []
            for k, v in tot.items():
                if v > 0 and k != eng and self.seen[eng].get(k, 0) < v:
                    self.seen[eng][k] = v
                    waits.append((k, v))

            def run(e, waits=waits):
                for k, v in waits:
                    e.wait_ge(sems[k], v)
            self.q[eng].append(run)

    def emit(self):
        nc = self.nc
        q = self.q
        with nc.Block() as block:
            @block.tensor
            def _(e):
                for f in q["pe"]:
                    f(e)

            @block.scalar
            def _(e):
                for f in q["act"]:
                    f(e)

            @block.vector
            def _(e):
                for f in q["dve"]:
                    f(e)

            @block.gpsimd
            def _(e):
                for f in q["pool"]:
                    f(e)

            @block.sync
            def _(e):
                for f in q["sp"]:
                    f(e)
        self.q = {e: [] for e in self.ENG}

    def tt(self, eng, out, a, b, op, R, W):
        self.op(eng, lambda e: e.tensor_tensor(out, a, b, op), R, W)

    def ts(self, eng, out, a, s1, s2, op0, op1, R, W):
        if s2 is None:
            self.op(eng, lambda e: e.tensor_scalar(out, a, s1, None, op0), R, W)
        else:
            self.op(eng, lambda e: e.tensor_scalar(out, a, s1, s2, op0, op1), R, W)

    def stt(self, eng, out, a, s, b, op0, op1, R, W):
        self.op(eng, lambda e: e.scalar_tensor_tensor(out, a, s, b, op0, op1), R, W)

    def act(self, out, a, func, R, W, scale=1.0, accum=None):
        if accum is None:
            self.op("act", lambda e: e.activation(out, a, func, scale=scale), R, W)
        else:
            self.op("act", lambda e: e.activation(out, a, func, scale=scale, accum_out=accum), R, W)

    def cp(self, eng, out, a, R, W):
        if eng == "act":
            self.op("act", lambda e: e.activation(out, a, AF.Copy), R, W)
        else:
            self.op(eng, lambda e: e.tensor_copy(out, a), R, W)

    def red(self, eng, out, a, R, W):
        self.op(eng, lambda e: e.reduce_sum(out, a, AX.X), R, W)

    def rcp(self, out, a, R, W):
        self.op("dve", lambda e: e.reciprocal(out, a), R, W)

    def mm(self, out, lhsT, rhs, start, stop, R, W, sig=None):
        if sig is None:
            sig = stop
        self.op("pe", lambda e: e.matmul(out, lhsT, rhs, start=start, stop=stop), R, W, sig=sig)

    def tr(self, out, a, ident, R, W, sig=True):
        self.op("pe", lambda e: e.transpose(out, a, ident), R, W, sig=sig)


def na_slots(i, nt, seg_t):
    m = i % seg_t
    if 2 <= m <= seg_t - 3:
        js = [i + 2 - s for s in range(5)]
        return js, "T", -4
    js = [i + 3 - s for s in range(7)]
    return js, "G", -6


def boundary_tiles(nt, seg_t):
    return [i for i in range(nt) if not (2 <= i % seg_t <= seg_t - 3)]


def host_consts(nt, seg_t, prompt):
    j = np.arange(128)[:, None]
    t = np.arange(128)[None, :]
    lt = (j < t).astype(np.float32)
    le = (j <= t).astype(np.float32)
    gt = (j > t).astype(np.float32)
    ge = (j >= t).astype(np.float32)
    cum = np.stack([lt, le, gt, ge], 1)
    amask = np.stack([np.concatenate([-lt, -le, lt, le], 1),
                      np.concatenate([-gt, -ge, gt, ge], 1)], 1)
    nmask = np.stack([-gt, -lt], 1)
    kc = np.arange(64)[:, None]
    qc = np.arange(64)[None, :]
    cs = np.clip(qc - 8, 0, 48)
    colm = ((kc >= cs) & (kc < cs + 16)).astype(np.float32)
    par = (np.arange(128) // 64)
    gm = np.zeros((128, 16, 64), np.float32)
    tm = np.zeros((128, 10, 64), np.float32)
    for p in range(128):
        for a, idx in enumerate(range(-7, 9)):
            dr = par[p] - idx
            if -7 <= dr <= 7:
                gm[p, a] = colm[p % 64]
        for a, idx in enumerate(range(-4, 6)):
            dr = par[p] - idx
            if -4 <= dr <= 3:
                tm[p, a] = colm[p % 64]
    bts = boundary_tiles(nt, seg_t)
    val = np.zeros((128, len(bts), 7, 2), np.float32)
    seq_t = nt if prompt else seg_t
    rows = 2 * seq_t
    for bi, i in enumerate(bts):
        s0 = (i // seq_t) * seq_t
        for s in range(7):
            jt = i + 3 - s
            if jt < 0 or jt >= nt or jt // seq_t != i // seq_t:
                continue
            for pr in range(2):
                for qp in range(2):
                    r = 2 * (i - s0) + qp
                    kr_ = 2 * (jt - s0) + pr
                    rs = min(max(r - 4, 0), rows - 8)
                    if rs <= kr_ < rs + 8:
                        val[pr * 64:(pr + 1) * 64, bi, s, qp] = 1.0
    return dict(cum=cum, amask=amask, nmask=nmask, gm=gm, tm=tm, val=val,
                ident=np.eye(128, dtype=np.float32))


def expand_rel_bias(rb):
    par = (np.arange(128) // 64)[:, None, None]
    kc = (np.arange(128) % 64)[:, None, None]
    qc = np.arange(64)[None, None, :]
    dc = np.clip(kc - qc + 15, 0, 30)

    def tab(idxs):
        idx = np.array(list(idxs))[None, :, None]
        dr = np.clip(par - idx + 7, 0, 14)
        return np.ascontiguousarray(np.transpose(rb[:, dr, dc], (1, 0, 2, 3)))
    return tab(range(-7, 9)), tab(range(-4, 6))


def rr(gens):
    gens = [g for g in gens if g is not None]
    while gens:
        nxt = []
        for g in gens:
            try:
                next(g)
                nxt.append(g)
            except StopIteration:
                pass
        gens = nxt


def fast(gen, n):
    while True:
        for _ in range(n):
            try:
                next(gen)
            except StopIteration:
                return
        yield


def delayed_start(gen, n):
    for _ in range(n):
        yield
    yield from gen


def chain_gens(fn, items):
    for it_ in items:
        yield from fn(it_)


def build(nt, seg_t, dbg=False):
    S = nt * 128
    nc = bass.Bass("TRN2", target_bir_lowering=False)
    dr = lambda n, sh, kind="ExternalInput": nc.dram_tensor(n, list(sh), F32, kind=kind).ap()
    xs = dr("xs", [S, D])
    pp = dr("pp", [S, 256])
    flag = dr("flag", [128, 1])
    w_in = dr("w_in", [D, 5440])
    conv_w = dr("conv_w", [3, RWC])
    vec512 = dr("vec512", [9, 512])
    vec512b = dr("vec512b", [1, 512])
    gvec = dr("gvec", [3, D])
    w_up = dr("w_up", [3, 64, 512])
    g_up = dr("g_up", [128, 512])
    w_a_out = dr("w_a_out", [512, D])
    w_b_out = dr("w_b_out", [512, D])
    w_o = dr("w_o", [D, D])
    w_ff1 = dr("w_ff1", [D, 4096])
    w_ff2 = dr("w_ff2", [4096, D])
    w_ple = dr("w_ple", [256, D])
    w_pgate = dr("w_pgate", [D, D])
    nb_t = len(boundary_tiles(nt, seg_t))
    c_ident = dr("c_ident", [128, 128])
    c_cum = dr("c_cum", [128, 4, 128])
    c_amask = dr("c_amask", [128, 2, 512])
    c_nmask = dr("c_nmask", [128, 2, 128])
    c_gm = dr("c_gm", [128, 16, 64])
    c_tm = dr("c_tm", [128, 10, 64])
    c_val = dr("c_val", [128, nb_t, 7, 2])
    rbG = dr("rbG", [128, 8, 16, 64])
    rbT = dr("rbT", [128, 8, 10, 64])
    yout = dr("yout", [S, D], "ExternalOutput")
    scr_u = dr("scr_u", [nt, 128, RWC], "ExternalOutput" if dbg else "Internal")
    scr_y = dr("scr_y", [nt, 128, 512], "Internal")
    scr_h = dr("scr_h", [nt, 128, D], "Internal")
    scr_yrw = dr("scr_yrw", [nt, 128, 512], "Internal")
    scr_ya = dr("scr_ya", [nt, 128, 512], "Internal")

    with contextlib.ExitStack() as st0:
        P = Prog(nc, st0)
        ldq = [None] * 4
        cq = wq = stq = None
        s2q = [None] * 2

        def newsem():
            return None

        identf = P.sb([128, 128], F32, "identf")
        identb = P.sb([128, 128], BF16, "identb")
        ones1 = P.sb([128, 1], F32, "ones1")
        flg = P.sb([128, 1], F32, "flg")
        wup = gup = None
        P.dma("sp", identf[:], c_ident, writes=[identf], sem=cq)
        P.dma("sp", flg[:], flag, writes=[flg], sem=cq)
        P.cp("dve", identb[:], identf[:], [identf], [identb])
        P.op("pool", lambda e: e.memset(ones1[:], 1.0), (), [ones1])
        W0F, W0B, A0, KK, KA, RK, LNW, LNB, QG = range(9)
        stages = [None, None]
        gv = None
        vb = vb2 = cum = amask = nmask = None

        def alloc_stages(st_, n=2):
            del stages[:]
            for i_ in range(n):
                stages.append(P.sb([128, 2048], F32, "stage", st_))

        def alloc_gv(st_, rows=(0, 1, 2)):
            g_ = P.sb([128, len(rows), D], F32, "gv", st_)
            for i_, r in enumerate(rows):
                P.dma("sp", g_[:, i_, :], gvec[r, :].partition_broadcast(128), writes=[g_], sem=cq)
            return g_

        def small_consts(st_, need2=False, rw=True):
            vb_ = P.sb([128, 9, 512], F32, "vb", st_)
            vb2_ = P.sb([128, 512 if need2 else 1], F32, "vb2", st_)
            cum_ = amask_ = nmask_ = None
            if rw:
                cum_ = P.sb([128, 4, 128], F32, "cum", st_)
                amask_ = P.sb([128, 2, 512], F32, "amask", st_)
                nmask_ = P.sb([128, 2, 128], F32, "nmask", st_)
                P.dma("sp", cum_[:], c_cum, writes=[cum_], sem=cq)
                P.dma("sp", amask_[:], c_amask, writes=[amask_], sem=cq)
                P.dma("sp", nmask_[:], c_nmask, writes=[nmask_], sem=cq)
            for r in range(9):
                P.dma("sp", vb_[:, r, :], vec512[r, :].partition_broadcast(128), writes=[vb_], sem=cq)
            if need2:
                P.dma("sp", vb2_[:], vec512b[0, :].partition_broadcast(128), writes=[vb2_], sem=cq)
            return vb_, vb2_, cum_, amask_, nmask_

        wcnt = [0]

        def load_w(dst, _unused, src, kchunks, ncols, col0=0):
            for k in range(kchunks):
                for c0 in range(0, ncols, 2048):
                    cwid = min(2048, ncols - c0)
                    sg = stages[wcnt[0] % len(stages)]
                    wcnt[0] += 1
                    P.dma("sp" if wcnt[0] % 2 else "act", sg[:, 0:cwid], src[k * 128:(k + 1) * 128, col0 + c0:col0 + c0 + cwid], writes=[sg], sem=wq)
                    eng = "act" if (wcnt[0] % 2) else "dve"
                    P.cp(eng, dst[:, k, c0:c0 + cwid], sg[:, 0:cwid], [sg], [dst])

        def rmsnorm_T(st_bufs, xt, grow, nT):
            sq, ss, nb, pT = st_bufs
            P.act(nb[:], xt[:], AF.Square, [xt], [nb, ss], accum=ss[:])
            P.ts("dve", ss[:], ss[:], 1.0 / D, EPS, ALU.mult, ALU.add, [ss], [ss])
            P.act(ss[:], ss[:], AF.Ln, [ss], [ss])
            P.act(ss[:], ss[:], AF.Exp, [ss], [ss], scale=-0.5)
            P.stt("dve", nb[:], xt[:], ss[:], gv[:, grow, :], ALU.mult, ALU.mult, [xt, ss, gv], [nb])
            for c in range(8):
                P.tr(pT[:, c * 128:(c + 1) * 128], nb[:, c * 128:(c + 1) * 128], identb[:], [nb, identb], [pT], sig=(c == 7))
            P.cp("act", nT[:].rearrange("p a b -> p (a b)"), pT[:], [pT], [nT])

        with contextlib.ExitStack() as st:
            alloc_stages(st)
            gv = alloc_gv(st)
            wrw = P.sb([128, 8, RWC], BF16, "wrw", st)
            load_w(wrw, None, w_in, 8, RWC, col0=1536)
            cw = P.sb([128, 3, RWC], F32, "cw", st)
            for r in range(3):
                P.dma("sp", cw[:, r, :], conv_w[r, :].partition_broadcast(128), writes=[cw], sem=cq)
            xt = [P.sb([128, D], F32, "xt%d" % i, st) for i in range(2)]
            ssA = [P.sb([128, 1], F32, "ss%d" % i, st) for i in range(2)]
            nbA = [P.sb([128, D], BF16, "nb%d" % i, st) for i in range(2)]
            nTA = [P.sb([128, 8, 128], BF16, "nT%d" % i, st) for i in range(2)]
            z = [P.sb([128, RWC], F32, "z%d" % i, st) for i in range(6)]
            zp = [P.sb([128, RWC], F32, "zp%d" % i, st) for i in range(2)]
            zn = [P.sb([128, RWC], F32, "zn%d" % i, st) for i in range(2)]
            tB = [P.sb([1, RWC], F32, "tB%d" % i, st) for i in range(2)]
            pTA = [P.ps([128, 1024], BF16, "pTA%d" % i, st) for i in range(2)]
            pzA = [[P.ps([128, 512], F32, "pzA%d_%d" % (i, j), st) for j in range(3)] for i in range(2)]
            xq = [newsem() for _ in range(2)]
            shs = [newsem() for _ in range(2)]
            uq = [newsem() for _ in range(2)]

            def tileA(c):
                p = c % 2
                x_ = xt[p]
                P.dma("sp", x_[:], xs[c * 128:(c + 1) * 128, :], (), [x_], sem=xq[p])
                yield
                ss, nb, nT, pT = ssA[p], nbA[p], nTA[p], pTA[p]
                P.act(nb[:], x_[:], AF.Square, [x_], [nb, ss], accum=ss[:])
                P.ts("dve", ss[:], ss[:], 1.0 / D, EPS, ALU.mult, ALU.add, [ss], [ss])
                yield
                P.act(ss[:], ss[:], AF.Ln, [ss], [ss])
                P.act(ss[:], ss[:], AF.Exp, [ss], [ss], scale=-0.5)
                yield
                P.stt("dve", nb[:], x_[:], ss[:], gv[:, 0, :], ALU.mult, ALU.mult, [x_, ss, gv], [nb])
                for k in range(8):
                    P.tr(pT[:, k * 128:(k + 1) * 128], nb[:, k * 128:(k + 1) * 128], identb[:], [nb, identb], [pT], sig=(k == 7))
                yield
                P.cp("act", nT[:].rearrange("p a b -> p (a b)"), pT[:], [pT], [nT])
                zc = z[c % 6]
                for gi, g0 in enumerate(range(0, RWC, 512)):
                    gw = min(512, RWC - g0)
                    pz = pzA[p][gi % 3]
                    for k in range(8):
                        P.mm(pz[:, 0:gw], nT[:, k, :], wrw[:, k, g0:g0 + gw], k == 0, k == 7, [nT, wrw], [pz])
                    yield
                    P.cp("act", zc[:, g0:g0 + gw], pz[:, 0:gw], [pz], [zc])

            zpT = [[Buf(zp[i].t, "zpT") for _ in range(3)] for i in range(2)]
            znT = [[Buf(zn[i].t, "znT") for _ in range(3)] for i in range(2)]

            def convA(c):
                p = c % 2
                zc = z[c % 6]
                zp_, zn_, tB_ = zp[p], zn[p], tB[p]
                zpt, znt = zpT[p], znT[p]
                first = (c == 0)
                last = (c == nt - 1)
                segs = (c % seg_t == 0)
                sege = (c % seg_t == seg_t - 1)
                P.dma("sp", zp_[1:113, :], zc[0:112, :], [zc], [zpt[0]])
                P.dma("sp", zp_[113:128, :], zc[112:127, :], [zc], [zpt[1]])
                if first:
                    P.op("pool", lambda e: e.memset(zp_[0:1, :], 0.0), (), [zpt[2]])
                else:
                    P.dma("sp", zp_[0:1, :], z[(c - 1) % 6][127:128, :], [z[(c - 1) % 6]], [zpt[2]])
                P.dma("sp", zn_[0:112, :], zc[1:113, :], [zc], [znt[0]])
                P.dma("sp", zn_[112:127, :], zc[113:128, :], [zc], [znt[1]])
                if last:
                    P.op("pool", lambda e: e.memset(tB_[:], 0.0), (), [tB_])
                elif sege:
                    P.ts("dve", tB_[:], z[(c + 1) % 6][0:1, :], flg[0:1, 0:1], None, ALU.mult, None, [z[(c + 1) % 6], flg], [tB_])
                else:
                    P.cp("dve", tB_[:], z[(c + 1) % 6][0:1, :], [z[(c + 1) % 6]], [tB_])
                P.dma("sp", zn_[127:128, :], tB_[:], [tB_], [znt[2]])
                yield
                if segs and not first:
                    P.ts("dve", zp_[0:1, :], zp_[0:1, :], flg[0:1, 0:1], None, ALU.mult, None, [zpt[2], flg], [zpt[2]])
                P.tt("dve", zn_[:], zn_[:], cw[:, 2, :], ALU.mult, znt + [cw], znt)
                P.tt("dve", zp_[:], zp_[:], cw[:, 0, :], ALU.mult, zpt + [cw], zpt)
                yield
                P.tt("dve", zn_[:], zn_[:], zp_[:], ALU.add, znt + zpt, znt)
                P.tt("dve", zp_[:], zc[:], cw[:, 1, :], ALU.mult, [zc, cw] + zpt, zpt)
                yield
                P.tt("dve", zn_[:], zn_[:], zp_[:], ALU.add, znt + zpt, znt)
                P.dma("act", scr_u[c], zn_[:], znt, ())

            tA = lambda c: tileA(c) if 0 <= c < nt else None
            cA = lambda c: convA(c) if 0 <= c < nt else None
            rr([tA(0), tA(1)])
            rr([tA(2)])
            for t0 in range(0, nt, 2):
                rr([tA(t0 + 3), tA(t0 + 4), cA(t0), cA(t0 + 1)])
            P.barrier()
            P.emit()

        half = nt // 2
        with contextlib.ExitStack() as st:
            vb, vb2, cum, amask, nmask = small_consts(st)
            wup = P.sb([64, 3, 512], BF16, "wup", st)
            gup = P.sb([128, 512], BF16, "gup", st)
            with contextlib.ExitStack() as st_w:
                alloc_stages(st_w)
                P.dma("sp", stages[0][0:64, 0:1536].rearrange("p (a b) -> p a b", a=3), w_up.rearrange("a p b -> p a b"), writes=[stages[0]], sem=wq)
                P.cp("dve", wup[:].rearrange("p a b -> p (a b)"), stages[0][0:64, 0:1536], [stages[0]], [wup])
                P.dma("sp", stages[1][:, 0:512], g_up, writes=[stages[1]], sem=wq)
                P.cp("dve", gup[:], stages[1][:, 0:512], [stages[1]], [gup])
                P.barrier()
                P.emit()
            SD = []
            for d in range(2):
                X = {}
                X["u"] = P.sb([128, RWC], F32, "u", st)
                X["t320"] = P.sb([128, 256], BF16, "t320", st)
                X["lT"] = P.sb([128, 3, 128], BF16, "lT", st)
                X["sg"] = P.sb([128, 512], F32, "sg", st)
                X["a"] = P.sb([128, 512], F32, "a_", st)
                X["ex"] = [P.sb([128, 512], F32, "ex%d" % i, st) for i in range(2)]
                X["kk"] = P.sb([128, 512], F32, "kk", st)
                X["tmp"] = P.sb([128, 512], F32, "tmp", st)
                X["s8"] = P.sb([128, 8], F32, "s8", st)
                X["bs"] = P.sb([128, 8], F32, "bs", st)
                X["kt"] = P.sb([128, 512], F32, "kt", st)
                X["beta"] = P.sb([128, 512], F32, "beta", st)
                X["tmb"] = P.sb([128, 4, 512], BF16, "tmb", st)
                X["fm"] = P.sb([64, 8, 4, 128], BF16, "fm", st)
                X["Vb"] = [P.sb([128, 512], BF16, "Vb%d" % i, st) for i in range(2)]
                X["KH"] = [P.sb([128, 512], BF16, "KH%d" % i, st) for i in range(2)]
                X["BH"] = [P.sb([128, 512], BF16, "BH%d" % i, st) for i in range(2)]
                X["eLC"] = [P.sb([64, 8], F32, "eLC%d" % i, st) for i in range(2)]
                X["g"] = [P.sb([128, 512], F32, "g%d" % i, st) for i in range(2)]
                X["bv"] = [P.sb([128, 512], F32, "bv%d" % i, st) for i in range(2)]
                yt = [P.sb([128, 512], F32, "y%d" % i, st) for i in range(2)]
                X["y"] = yt
                X["yh"] = [[Buf(yt[i].t, "yh") for _ in range(2)] for i in range(2)]
                X["yo"] = P.sb([128, 512], F32, "yo", st)
                X["oc"] = P.sb([128, 512], F32, "oc", st)
                X["s8c"] = P.sb([128, 8], F32, "s8c", st)
                X["s8d"] = P.sb([128, 8], F32, "s8d", st)
                X["yrw"] = X["oc"]
                X["prp"] = P.ps([128, 512], F32, "prp", st)
                X["pT"] = P.ps([128, 1024], BF16, "pTd", st)
                X["uq"] = newsem()
                X["yq"] = newsem()
                X["sq"] = newsem()
                X["rq"] = newsem()
                X["hg"] = []
                for hg in range(2):
                    G = {}
                    G["AG"] = P.sb([128, 4, 4, 128], BF16, "AG", st)
                    G["PN"] = [P.sb([128, 4, 128], BF16, "PN%d" % i, st) for i in range(2)]
                    G["PX"] = [P.sb([128, 4, 128], BF16, "PX%d" % i, st) for i in range(2)]
                    G["ZN"] = P.sb([128, 4, 128], BF16, "ZN", st)
                    G["ZX"] = P.sb([128, 4, 128], BF16, "ZX", st)
                    G["RHS"] = P.sb([128, 4, 64], BF16, "RHS", st)
                    G["U"] = P.sb([128, 4, 64], BF16, "U", st)
                    G["Hf"] = P.sb([64, 4, 64], F32, "Hf", st)
                    G["Hb"] = P.sb([64, 4, 64], BF16, "Hb", st)
                    G["ps"] = P.ps([128, 512], F32, "pinv", st)
                    X["hg"].append(G)
                SD.append(X)
            scrY = [Buf(None, "scrY%d" % c) for c in range(nt)]

            def tile_of(d, k):
                return k if d == 0 else nt - 1 - k

            def delayed(gen, n):
                for _ in range(n):
                    yield
                yield from gen

            def is_second(d, c):
                return (c >= half) if d == 0 else (c < half)

            def prepA(d, k):
                X = SD[d]
                c = tile_of(d, k)
                par = k % 2
                comb = is_second(d, c)
                u, t320, lT, sg, a_, ex = X["u"], X["t320"], X["lT"], X["sg"], X["a"], X["ex"]
                kk, tmp, s8, kt, beta, tmb = X["kk"], X["tmp"], X["s8"], X["kt"], X["beta"], X["tmb"]
                prp, pT = X["prp"], X["pT"]
                P.dma("sp", u[:], scr_u[c], (), [u], sem=X["uq"])
                yield
                r_ = u[:, 0:512]
                k_ = u[:, 512:1024]
                v_ = u[:, 1024:1536]
                P.act(t320[:, 0:64], u[:, 1536 + 64 * d:1600 + 64 * d], AF.Tanh, [u], [t320])
                P.act(t320[:, 64:128], u[:, 1664:1728], AF.Copy, [u], [t320])
                if comb:
                    P.act(t320[:, 128:256], u[:, 1728:1856], AF.Sigmoid, [u], [t320])
                yield
                P.tr(pT[0:64, 0:128], t320[:, 0:64], identb[:], [t320, identb], [pT], sig=False)
                P.tr(pT[0:64, 128:256], t320[:, 64:128], identb[:], [t320, identb], [pT], sig=not comb)
                if comb:
                    P.tr(pT[:, 256:384], t320[:, 128:256], identb[:], [t320, identb], [pT])
                yield
                if comb:
                    P.cp("dve", lT[:].rearrange("p a b -> p (a b)"), pT[:, 0:384], [pT], [lT])
                else:
                    P.cp("dve", lT[0:64, 0:2, :].rearrange("p a b -> p (a b)"), pT[0:64, 0:256], [pT], [lT])
                yield
                P.mm(prp[:], lT[0:64, 0, :], wup[:, d, :], True, True, [lT, wup], [prp])
                yield
                P.tt("dve", tmp[:], prp[:], vb[:, W0F + d, :], ALU.add, [prp, vb], [tmp])
                yield
                P.act(sg[:], tmp[:], AF.Sigmoid, [tmp], [sg])
                P.mm(prp[:], lT[0:64, 1, :], wup[:, 2, :], True, True, [lT, wup], [prp])
                yield
                P.tt("dve", tmp[:], prp[:], vb[:, A0, :], ALU.add, [prp, vb], [tmp])
                yield
                P.act(a_[:], tmp[:], AF.Sigmoid, [tmp], [a_])
                P.tt("dve", kk[:], k_, vb[:, KK, :], ALU.mult, [u, vb], [kk])
                yield
                P.act(tmp[:], kk[:], AF.Square, [kk], [tmp])
                yield
                P.red("dve", s8[:], tmp[:].rearrange("p (a b) -> p a b", a=8), [tmp], [s8])
                yield
                P.act(s8[:], s8[:], AF.Ln, [s8], [s8])
                P.act(s8[:], s8[:], AF.Exp, [s8], [s8], scale=-0.5)
                yield
                P.ts("dve", s8[:], s8[:], 1e12, None, ALU.min, None, [s8], [s8])
                yield
                kk3 = kk[:].rearrange("p (a b) -> p a b", a=8)
                P.tt("dve", kk3, kk3, s8[:].unsqueeze(2).to_broadcast([128, 8, 64]), ALU.mult, [kk, s8], [kk])
                P.stt("dve", tmp[:], a_[:], -1.0, vb[:, KA, :], ALU.add, ALU.mult, [a_, vb], [tmp])
                yield
                P.stt("dve", kt[:], tmp[:], 1.0, k_, ALU.add, ALU.mult, [tmp, u], [kt])
                P.tt("dve", beta[:], kk[:], a_[:], ALU.mult, [kk, a_], [beta])
                yield
                m1, m2, m4 = (0, 1, 2) if d == 0 else (2, 3, 0)
                P.mm(prp[:], cum[:, m1, :], sg[:], True, True, [cum, sg], [prp])
                yield
                P.act(ex[0][:], prp[:], AF.Exp, [prp], [ex[0]], scale=-DS)
                yield
                P.mm(prp[:], cum[:, m2, :], sg[:], True, True, [cum, sg], [prp])
                P.tt("dve", tmb[:, 0, :], kk[:], ex[0][:], ALU.mult, [kk, ex[0]], [tmb])
                yield
                P.act(ex[1][:], prp[:], AF.Exp, [prp], [ex[1]], scale=-DS)
                P.act(ex[0][:], prp[:], AF.Exp, [prp], [ex[0]], scale=DS)
                yield
                P.mm(prp[:], cum[:, m4, :], sg[:], True, True, [cum, sg], [prp])
                P.tt("dve", tmb[:, 1, :], r_, ex[1][:], ALU.mult, [u, ex[1]], [tmb])
                P.tt("dve", tmb[:, 2, :], kt[:], ex[0][:], ALU.mult, [kt, ex[0]], [tmb])
                yield
                P.tt("dve", tmb[:, 3, :], beta[:], ex[0][:], ALU.mult, [beta, ex[0]], [tmb])
                P.act(ex[1][:], prp[:], AF.Exp, [prp], [ex[1]], scale=-DS)
                yield
                for h in range(8):
                    P.mm(prp[0:64, h:h + 1], sg[:, h * 64:(h + 1) * 64], ones1[:], True, True, [sg, ones1], [prp], sig=(h == 7))
                P.tt("dve", X["KH"][par][:], kt[:], ex[1][:], ALU.mult, [kt, ex[1]], [X["KH"][par]])
                yield
                P.act(X["eLC"][par][:], prp[0:64, 0:8], AF.Exp, [prp], [X["eLC"][par]], scale=-DS)
                P.stt("dve", X["BH"][par][:], beta[:], -1.0, ex[1][:], ALU.mult, ALU.mult, [beta, ex[1]], [X["BH"][par]])
                yield
                P.cp("act", X["Vb"][par][:], v_, [u], [X["Vb"][par]])
                if comb:
                    P.tt("dve", tmp[:], r_, kt[:], ALU.mult, [u, kt], [tmp])
                    yield
                    P.tt("dve", tmp[:], tmp[:], vb[:, RK, :], ALU.mult, [tmp, vb], [tmp])
                    yield
                    P.red("dve", X["bs"][:], tmp[:].rearrange("p (a b) -> p a b", a=8), [tmp], [X["bs"]])

            def commit(d, k):
                X = SD[d]
                c = tile_of(d, k)
                par = k % 2
                comb = is_second(d, c)
                tmb, fm, pT, prp = X["tmb"], X["fm"], X["pT"], X["prp"]
                for hp in range(4):
                    for hh in range(2):
                        h = hp * 2 + hh
                        for m in range(4):
                            P.tr(pT[0:64, (hh * 4 + m) * 128:(hh * 4 + m + 1) * 128], tmb[:, m, h * 64:(h + 1) * 64],
                                 identb[:], [tmb, identb], [pT], sig=(hh == 1 and m == 3))
                    yield
                    P.cp("act" if hp % 2 else "dve", fm[:, hp * 2:hp * 2 + 2, :, :].rearrange("p a b c -> p (a b c)"),
                         pT[0:64, :], [pT], [fm])
                if comb:
                    P.mm(prp[:], X["lT"][:, 2, :], gup[:], True, True, [X["lT"], gup], [prp])
                    P.cp("act", X["g"][par][:], prp[:], [prp], [X["g"][par]])
                    P.tt("dve", X["bv"][par][:].rearrange("p (a b) -> p a b", a=8),
                         X["u"][:, 1024:1536].rearrange("p (a b) -> p a b", a=8),
                         X["bs"][:].unsqueeze(2).to_broadcast([128, 8, 64]), ALU.mult, [X["u"], X["bs"]], [X["bv"][par]])

            def invchain(d, hg, k):
                X = SD[d]
                G = X["hg"][hg]
                c = tile_of(d, k)
                par = k % 2
                fm, Vb, KH, BH, eLC = X["fm"], X["Vb"][par], X["KH"][par], X["BH"][par], X["eLC"][par]
                AG, PNs, PXs, ZN, ZX, RHS, U = G["AG"], G["PN"], G["PX"], G["ZN"], G["ZX"], G["RHS"], G["U"]
                Hf, Hb, ps = G["Hf"], G["Hb"], G["ps"]
                yb = X["yh"][par][hg]
                ytile = X["y"][par]
                hs = [hg * 4 + i for i in range(4)]
                if k == 0:
                    P.op("pool", lambda e: e.memset(Hf[:], 0.0), (), [Hf])
                    P.op("pool", lambda e: e.memset(Hb[:], 0.0), (), [Hb])
                else:
                    joint = (c % seg_t == 0) if d == 0 else (c % seg_t == seg_t - 1)
                    if joint:
                        P.ts("dve", Hf[:], Hf[:], flg[0:64, 0:1], None, ALU.mult, None, [Hf, flg], [Hf])
                        P.cp("act", Hb[:], Hf[:], [Hf], [Hb])
                for i, h in enumerate(hs):
                    P.mm(ps[:, 0:256], fm[:, h, 3, :], fm[:, h, 0:2, :].rearrange("p a b -> p (a b)"), True, True, [fm], [ps], sig=False)
                    P.mm(ps[:, 256:512], fm[:, h, 2, :], fm[:, h, 0:2, :].rearrange("p a b -> p (a b)"), True, True, [fm], [ps])
                    yield
                    P.tt("dve", AG[:, i, :, :].rearrange("p a b -> p (a b)"), ps[:], amask[:, d, :], ALU.mult, [ps, amask], [AG])
                for i, h in enumerate(hs):
                    P.mm(ps[:, i * 128:(i + 1) * 128], fm[:, h, 0, :], fm[:, h, 3, :], True, True, [fm], [ps], sig=(i == 3))
                yield
                curN = PNs[0]
                P.tt("dve", curN[:], ps[:].rearrange("p (a b) -> p a b", a=4),
                     nmask[:, d, :].unsqueeze(1).to_broadcast([128, 4, 128]), ALU.mult, [ps, nmask], [curN])
                P.tt("dve", ZX[:], AG[:, :, 0, :], identb[:].unsqueeze(1).to_broadcast([128, 4, 128]), ALU.add, [AG, identb], [ZX])
                yield
                for i, h in enumerate(hs):
                    P.mm(ps[:, i * 64:(i + 1) * 64], fm[:, h, 0, :], Hb[:, i, :], True, False, [fm, Hb], [ps], sig=False)
                    P.mm(ps[:, i * 64:(i + 1) * 64], AG[:, i, 2, :], Vb[:, h * 64:(h + 1) * 64], False, True, [AG, Vb], [ps], sig=False)
                for i, h in enumerate(hs):
                    P.mm(ps[:, 256 + i * 64:256 + (i + 1) * 64], fm[:, h, 1, :], Hb[:, i, :], True, False, [fm, Hb], [ps], sig=False)
                    P.mm(ps[:, 256 + i * 64:256 + (i + 1) * 64], AG[:, i, 3, :], Vb[:, h * 64:(h + 1) * 64], False, True, [AG, Vb], [ps], sig=(i == 3))
                yield
                P.cp("act", RHS[:].rearrange("p a b -> p (a b)"), ps[:, 0:256], [ps], [RHS])
                P.cp("act", ytile[:, hg * 256:(hg + 1) * 256], ps[:, 256:512], [ps], [yb])
                yield
                for i, h in enumerate(hs):
                    P.mm(ps[0:64, i * 64:(i + 1) * 64], KH[:, h * 64:(h + 1) * 64], Vb[:, h * 64:(h + 1) * 64], True, True, [KH, Vb], [ps], sig=(i == 3))
                P.tt("dve", Hf[:], Hf[:], eLC[:, hg * 4:(hg + 1) * 4].unsqueeze(2).to_broadcast([64, 4, 64]), ALU.mult, [Hf, eLC], [Hf])
                yield
                P.tt("dve", Hf[:], Hf[:], ps[0:64, 0:256].rearrange("p (a b) -> p a b", a=4), ALU.add, [Hf, ps], [Hf])
                yield
                curX_ap = lambda i: AG[:, i, 0, :]
                curX_buf = AG
                for lvl in range(1, 7):
                    both = lvl < 6
                    need_side = "N" if (lvl % 2 == 1) else "X"
                    newN = PNs[lvl % 2]
                    newX = PXs[lvl % 2]
                    doX = both or need_side == "X"
                    doN = both or need_side == "N"
                    if doX:
                        for i in range(4):
                            P.mm(ps[:, i * 128:(i + 1) * 128], curN[:, i, :], curX_ap(i), True, True, [curN, curX_buf], [ps], sig=(i == 3))
                        yield
                        P.cp("act", newX[:].rearrange("p a b -> p (a b)"), ps[:], [ps], [newX])
                    if doN:
                        for i in range(4):
                            P.mm(ps[:, i * 128:(i + 1) * 128], curX_ap(i), curN[:, i, :], True, True, [curN, curX_buf], [ps], sig=(i == 3))
                        yield
                        P.cp("dve", newN[:].rearrange("p a b -> p (a b)"), ps[:], [ps], [newN])
                    curN = newN
                    curX_buf = newX
                    curX_ap = (lambda nx: (lambda i: nx[:, i, :]))(newX)
                    if need_side == "N":
                        for i in range(4):
                            P.mm(ps[:, i * 128:(i + 1) * 128], ZX[:, i, :], newN[:, i, :], True, False, [ZX, newN], [ps], sig=False)
                            P.mm(ps[:, i * 128:(i + 1) * 128], ZX[:, i, :], identb[:], False, True, [ZX, identb], [ps], sig=(i == 3))
                        yield
                        P.cp("act", ZN[:].rearrange("p a b -> p (a b)"), ps[:], [ps], [ZN])
                    else:
                        for i in range(4):
                            P.mm(ps[:, i * 128:(i + 1) * 128], ZN[:, i, :], newX[:, i, :], True, False, [ZN, newX], [ps], sig=False)
                            P.mm(ps[:, i * 128:(i + 1) * 128], ZN[:, i, :], identb[:], False, True, [ZN, identb], [ps], sig=(i == 3))
                        yield
                        P.cp("dve", ZX[:].rearrange("p a b -> p (a b)"), ps[:], [ps], [ZX])
                for i in range(4):
                    P.mm(ps[:, i * 64:(i + 1) * 64], ZX[:, i, :], RHS[:, i, :], True, True, [ZX, RHS], [ps], sig=(i == 3))
                yield
                P.cp("dve", U[:].rearrange("p a b -> p (a b)"), ps[:, 0:256], [ps], [U])
                yield
                for i, h in enumerate(hs):
                    P.mm(ps[:, i * 64:(i + 1) * 64], AG[:, i, 1, :], U[:, i, :], True, True, [AG, U], [ps], sig=False)
                for i, h in enumerate(hs):
                    P.mm(ps[0:64, 256 + i * 64:256 + (i + 1) * 64], BH[:, h * 64:(h + 1) * 64], U[:, i, :], True, True, [BH, U], [ps], sig=(i == 3))
                yield
                P.tt("dve", ytile[:, hg * 256:(hg + 1) * 256], ytile[:, hg * 256:(hg + 1) * 256], ps[:, 0:256], ALU.add, [ps, yb], [yb])
                P.tt("dve", Hf[:], Hf[:], ps[0:64, 256:512].rearrange("p (a b) -> p a b", a=4), ALU.add, [Hf, ps], [Hf])
                yield
                P.cp("act", Hb[:], Hf[:], [Hf], [Hb])

            def combine(d, k):
                X = SD[d]
                c = tile_of(d, k)
                par = k % 2
                y = X["y"][par]
                ybs = X["yh"][par]
                if not is_second(d, c):
                    P.dma("act", scr_y[c], y[:], ybs, [scrY[c]], sem=X["sq"])
                    return
                yo, oc, s8, s8b, yrw = X["yo"], X["oc"], X["s8c"], X["s8d"], X["yrw"]
                g_, bv = X["g"][par], X["bv"][par]
                P.dma("sp", yo[:], scr_y[c], [scrY[c]], [yo], sem=X["yq"])
                yield
                P.tt("dve", yo[:], yo[:], y[:], ALU.add, [yo] + ybs, [yo])
                yield
                y3 = yo[:].rearrange("p (a b) -> p a b", a=8)
                oc3 = oc[:].rearrange("p (a b) -> p a b", a=8)
                P.red("dve", s8[:], y3, [yo], [s8])
                yield
                P.ts("dve", s8[:], s8[:], 1.0 / 64, None, ALU.mult, None, [s8], [s8])
                yield
                P.tt("dve", oc3, y3, s8[:].unsqueeze(2).to_broadcast([128, 8, 64]), ALU.subtract, [yo, s8], [oc])
                yield
                P.act(yo[:], oc[:], AF.Square, [oc], [yo])
                yield
                P.red("dve", s8b[:], yo[:].rearrange("p (a b) -> p a b", a=8), [yo], [s8b])
                yield
                P.ts("dve", s8b[:], s8b[:], 1.0 / 64, GN_EPS, ALU.mult, ALU.add, [s8b], [s8b])
                yield
                P.act(s8b[:], s8b[:], AF.Ln, [s8b], [s8b])
                yield
                P.act(s8b[:], s8b[:], AF.Exp, [s8b], [s8b], scale=-0.5)
                yield
                P.tt("dve", oc3, oc3, s8b[:].unsqueeze(2).to_broadcast([128, 8, 64]), ALU.mult, [oc, s8b], [oc])
                yield
                P.tt("dve", oc[:], oc[:], vb[:, LNW, :], ALU.mult, [oc, vb], [oc])
                yield
                P.tt("dve", oc[:], oc[:], vb[:, LNB, :], ALU.add, [oc, vb], [oc])
                yield
                P.tt("dve", oc[:], oc[:], bv[:], ALU.add, [oc, bv], [oc])
                yield
                P.tt("dve", yrw[:], oc[:], g_[:], ALU.mult, [oc, g_], [yrw])
                P.dma("act", scr_yrw[c], yrw[:], [yrw], (), sem=X["rq"])

            def prep_commit(d, k):
                yield from prepA(d, k)
                yield from commit(d, k)

            rr([prep_commit(0, 0), prep_commit(1, 0)])
            for k in range(nt):
                streams = []
                if k + 1 < nt:
                    streams += [prep_commit(0, k + 1), prep_commit(1, k + 1)]
                streams += [invchain(d, hg, k) for d in range(2) for hg in range(2)]
                if k >= 1:
                    streams += [combine(0, k - 1), combine(1, k - 1)]
                rr(streams)
            rr([combine(0, nt - 1), combine(1, nt - 1)])
            P.barrier()
            P.emit()

        bts = boundary_tiles(nt, seg_t)
        with contextlib.ExitStack() as st:
            alloc_stages(st, 4)
            gv = alloc_gv(st, (0,))
            vb, vb2, cum, amask, nmask = small_consts(st, True, rw=False)
            wna = P.sb([128, 8, 1536], BF16, "wna", st)
            load_w(wna, None, w_in, 8, 1536, col0=0)
            tabG = P.sb([128, 8, 16, 64], BF16, "tabG", st)
            tabT = P.sb([128, 8, 10, 64], BF16, "tabT", st)
            valid = P.sb([128, nb_t, 7, 2], F32, "valid", st)
            P.dma("sp", valid[:], c_val, (), [valid], sem=cq)
            with contextlib.ExitStack() as st_t:
                gm = P.sb([128, 16, 64], F32, "gm", st_t)
                tmk = P.sb([128, 10, 64], F32, "tmk", st_t)
                P.dma("sp", gm[:], c_gm, (), [gm], sem=cq)
                P.dma("sp", tmk[:], c_tm, (), [tmk], sem=cq)
                for h in range(8):
                    sgb = stages[h % 2]
                    P.dma("sp", sgb[:, 0:1024].rearrange("p (a b) -> p a b", a=16), rbG[:, h, :, :], (), [sgb], sem=wq)
                    P.act(sgb[:, 0:1024], sgb[:, 0:1024], AF.Exp, [sgb], [sgb])
                    P.tt("dve", tabG[:, h, :, :], sgb[:, 0:1024].rearrange("p (a b) -> p a b", a=16), gm[:], ALU.mult, [sgb, gm], [tabG])
                    P.dma("sp", sgb[:, 1024:1664].rearrange("p (a b) -> p a b", a=10), rbT[:, h, :, :], (), [sgb], sem=wq)
                    P.act(sgb[:, 1024:1664], sgb[:, 1024:1664], AF.Exp, [sgb], [sgb])
                    P.tt("dve", tabT[:, h, :, :], sgb[:, 1024:1664].rearrange("p (a b) -> p a b", a=10), tmk[:], ALU.mult, [sgb, tmk], [tabT])
                P.barrier()
                P.emit()
            xt = [P.sb([128, D], F32, "xt%d" % i, st) for i in range(2)]
            ss = P.sb([128, 1], F32, "ss", st)
            nb = P.sb([128, D], BF16, "nb", st)
            nT = P.sb([128, 8, 128], BF16, "nT", st)
            NR = 5
            qTr = [P.sb([64, 8, 128], BF16, "qTr%d" % i, st) for i in range(NR)]
            KR = 8
            kTr = [P.sb([64, 8, 128], BF16, "kTr%d" % i, st) for i in range(KR)]
            v1r = [P.sb([128, 8, 65], BF16, "v1r%d" % i, st) for i in range(KR)]
            zna = P.sb([128, 1536], F32, "zna", st)
            tmp2 = P.sb([128, 512], F32, "tmp2", st)
            qk = P.sb([128, 2, 512], BF16, "qk", st)
            s16 = P.sb([128, 16], F32, "s16", st)
            ESs = [P.sb([128, 896], F32, "ES%d" % i, st) for i in range(2)]
            PTs = [P.sb([128, 896], BF16, "PT%d" % i, st) for i in range(2)]
            yas = [P.sb([128, 512], F32, "ya%d" % i, st) for i in range(2)]
            rden = P.sb([128, 8], F32, "rden", st)
            pTb = P.ps([128, 1024], BF16, "pTb", st)
            pss = [P.ps([128, 512], F32, "ps%d" % i, st) for i in range(7)]
            yaq = [newsem() for _ in range(2)]
            for r_ in v1r:
                P.op("pool", lambda e, r_=r_: e.memset(r_[:], 1.0), (), [r_])

            znab = [zna, P.sb([128, 1536], F32, "znab", st)]
            pTx = Buf(pTb.t, "pTx")
            pTq = Buf(pTb.t, "pTq")

            def xz(c):
                x_ = xt[c % 2]
                zc = znab[c % 2]
                P.dma("sp", x_[:], xs[c * 128:(c + 1) * 128, :], (), [x_])
                yield
                P.act(nb[:], x_[:], AF.Square, [x_], [nb, ss], accum=ss[:])
                yield
                P.ts("dve", ss[:], ss[:], 1.0 / D, EPS, ALU.mult, ALU.add, [ss], [ss])
                yield
                P.act(ss[:], ss[:], AF.Ln, [ss], [ss])
                yield
                P.act(ss[:], ss[:], AF.Exp, [ss], [ss], scale=-0.5)
                yield
                P.stt("dve", nb[:], x_[:], ss[:], gv[:, 0, :], ALU.mult, ALU.mult, [x_, ss, gv], [nb])
                yield
                for r in range(2):
                    for j in range(4):
                        k = r * 4 + j
                        P.tr(pTb[:, j * 128:(j + 1) * 128], nb[:, k * 128:(k + 1) * 128], identb[:], [nb, identb], [pTx], sig=(j == 3))
                    yield
                    P.cp("act", nT[:, r * 4:(r + 1) * 4, :].rearrange("p a b -> p (a b)"), pTb[:, 0:512], [pTx], [nT])
                    yield
                for g0 in range(3):
                    pz = pss[4]
                    for k in range(8):
                        P.mm(pz[:], nT[:, k, :], wna[:, k, g0 * 512:(g0 + 1) * 512], k == 0, k == 7, [nT, wna], [pz])
                    yield
                    P.cp("act", zc[:, g0 * 512:(g0 + 1) * 512], pz[:], [pz], [zc])
                    yield

            def qk_(c):
                zc = znab[c % 2]
                P.act(tmp2[:], zc[:, 0:512], AF.Square, [zc], [tmp2])
                yield
                P.red("dve", s16[:, 0:8], tmp2[:].rearrange("p (a b) -> p a b", a=8), [tmp2], [s16])
                yield
                P.act(tmp2[:], zc[:, 512:1024], AF.Square, [zc], [tmp2])
                yield
                P.red("dve", s16[:, 8:16], tmp2[:].rearrange("p (a b) -> p a b", a=8), [tmp2], [s16])
                yield
                P.ts("dve", s16[:], s16[:], 1.0 / 64, EPS, ALU.mult, ALU.add, [s16], [s16])
                yield
                P.act(s16[:], s16[:], AF.Ln, [s16], [s16])
                yield
                P.act(s16[:], s16[:], AF.Exp, [s16], [s16], scale=-0.5)
                yield
                for w_ in range(2):
                    z3 = zc[:, w_ * 512:(w_ + 1) * 512].rearrange("p (a b) -> p a b", a=8)
                    t3 = tmp2[:].rearrange("p (a b) -> p a b", a=8)
                    P.tt("dve", t3, z3, s16[:, w_ * 8:(w_ + 1) * 8].unsqueeze(2).to_broadcast([128, 8, 64]), ALU.mult, [zc, s16], [tmp2])
                    yield
                    gsrc = vb[:, QG, :] if w_ == 0 else vb2[:]
                    P.tt("pool", qk[:, w_, :], tmp2[:], gsrc, ALU.mult, [tmp2, vb, vb2], [qk])
                    yield
                v1 = v1r[c % KR]
                P.cp("act", v1[:, :, 0:64], zc[:, 1024:1536].rearrange("p (a b) -> p a b", a=8), [zc], [v1])
                qT = qTr[c % NR]
                kT = kTr[c % KR]
                for w_, dst in ((0, qT), (1, kT)):
                    for r in range(2):
                        for j in range(4):
                            h = r * 4 + j
                            P.tr(pTb[0:64, 512 + j * 128:512 + (j + 1) * 128], qk[:, w_, h * 64:(h + 1) * 64], identb[:], [qk, identb], [pTq], sig=(j == 3))
                        yield
                        P.cp("dve" if (w_ + r) % 2 else "act", dst[:, r * 4:(r + 1) * 4, :].rearrange("p a b -> p (a b)"), pTb[0:64, 512:1024], [pTq], [dst])
                        yield

            yaT = [[Buf(yas[i].t, "yaT") for _ in range(2)] for i in range(2)]
            rdens = [P.sb([128, 4], F32, "rden%d" % i, st) for i in range(2)]

            def attn_half(i, hh):
                js, kind, idx0 = na_slots(i, nt, seg_t)
                slots = [(s, j) for s, j in enumerate(js) if 0 <= j < nt]
                qT = qTr[i % NR]
                po = pss[5 + hh]
                ya = yas[i % 2]
                yat = yaT[i % 2][hh]
                ES, PT = ESs[hh], PTs[hh]
                pa_, pb_ = pss[hh * 2], pss[hh * 2 + 1]
                rd = rdens[hh]
                s_lo = slots[0][0]
                s_hi = slots[-1][0]
                n_ = s_hi - s_lo + 1
                for hl in range(4):
                    h = hh * 4 + hl
                    for s, j in slots:
                        bank = pa_ if s < 4 else pb_
                        P.mm(bank[:, (s % 4) * 128:(s % 4 + 1) * 128], kTr[j % KR][:, h, :], qT[:, h, :], True, True,
                             [kTr[j % KR], qT], [bank], sig=True)
                    yield
                    a1_ = min(s_hi, 3)
                    if s_lo <= 3:
                        P.act(ES[:, s_lo * 128:(a1_ + 1) * 128], pa_[:, s_lo * 128:(a1_ + 1) * 128], AF.Exp, [pa_], [ES], scale=0.125)
                    if s_hi >= 4:
                        b0_ = max(s_lo, 4)
                        P.act(ES[:, b0_ * 128:(s_hi + 1) * 128], pb_[:, (b0_ - 4) * 128:(s_hi - 3) * 128], AF.Exp, [pb_], [ES], scale=0.125)
                    yield
                    es4 = ES[:, s_lo * 128:(s_hi + 1) * 128].rearrange("p (a b) -> p a b", b=64)
                    pt4 = PT[:, s_lo * 128:(s_hi + 1) * 128].rearrange("p (a b) -> p a b", b=64)
                    if kind == "T":
                        P.tt("dve", pt4, es4, tabT[:, h, 2 * s_lo:2 * s_hi + 2, :], ALU.mult, [ES, tabT], [PT])
                    else:
                        P.tt("dve", es4, es4, tabG[:, h, 2 * s_lo + 1:2 * s_hi + 3, :], ALU.mult, [ES, tabG], [ES])
                        yield
                        bi = bts.index(i)
                        vv = valid[:, bi, s_lo:s_hi + 1, :].rearrange("p a b -> p (a b)").unsqueeze(2).to_broadcast([128, 2 * n_, 64])
                        P.tt("pool", pt4, es4, vv, ALU.mult, [ES, valid], [PT])
                    yield
                    for s, j in slots:
                        P.mm(po[:, hl * 65:(hl + 1) * 65], PT[:, s * 128:(s + 1) * 128], v1r[j % KR][:, h, :],
                             s == s_lo, s == s_hi, [PT, v1r[j % KR]], [po], sig=(s == s_hi))
                yield
                po3 = po[:, 0:260].rearrange("p (a b) -> p a b", a=4)
                P.rcp(rd[:], po3[:, :, 64], [po], [rd])
                yield
                P.tt("dve", ya[:, hh * 256:(hh + 1) * 256].rearrange("p (a b) -> p a b", a=4), po3[:, :, 0:64],
                     rd[:].unsqueeze(2).to_broadcast([128, 4, 64]), ALU.mult, [po, rd], [yat])

            def attn_store(i):
                P.dma("act", scr_ya[i], yas[i % 2][:], yaT[i % 2], ())

            LAG = 4
            rr([xz(0)])
            for it in range(nt + LAG):
                streams = [xz(it + 1) if it + 1 < nt else None, qk_(it) if it < nt else None]
                if it - LAG >= 0:
                    streams += [attn_half(it - LAG, 0), attn_half(it - LAG, 1)]
                rr(streams)
                if it - LAG >= 0:
                    attn_store(it - LAG)
            P.barrier()
            P.emit()

        with contextlib.ExitStack() as st:
            alloc_stages(st, 4)
            gv = alloc_gv(st)
            wgt = P.sb([128, 8, 2048], BF16, "wgt", st)
            wao = P.sb([128, 4, D], BF16, "wao", st)
            wbo = P.sb([128, 4, D], BF16, "wbo", st)
            wo = P.sb([128, 8, D], BF16, "wo", st)
            load_w(wgt, None, w_in, 8, 2048, col0=3392)
            load_w(wao, None, w_a_out, 4, D)
            load_w(wbo, None, w_b_out, 4, D)
            load_w(wo, None, w_o, 8, D)
            NP = 2
            xt = [P.sb([128, D], F32, "xt%d" % i, st) for i in range(NP)]
            yl = [P.sb([128, 2, 512], F32, "yl%d" % i, st) for i in range(NP)]
            ssC = [P.sb([128, 1], F32, "ss%d" % i, st) for i in range(NP)]
            nbC = [P.sb([128, D], BF16, "nb%d" % i, st) for i in range(NP)]
            nTC = [P.sb([128, 8, 128], BF16, "nT%d" % i, st) for i in range(NP)]
            ylb = [P.sb([128, 2, 512], BF16, "ylb%d" % i, st) for i in range(NP)]
            ylT = [P.sb([128, 8, 128], BF16, "ylT%d" % i, st) for i in range(NP)]
            gtsC = [P.sb([128, 2048], F32, "gts%d" % i, st) for i in range(NP)]
            mrgC = [P.sb([128, D], F32, "mrg%d" % i, st) for i in range(NP)]
            mrbC = [P.sb([128, D], BF16, "mrb%d" % i, st) for i in range(NP)]
            mTC = [P.sb([128, 8, 128], BF16, "mT%d" % i, st) for i in range(NP)]
            hhC = [P.sb([128, D], F32, "hh%d" % i, st) for i in range(NP)]
            pTC = [P.ps([128, 1024], BF16, "pTC%d" % i, st) for i in range(NP)]
            pzC = [[P.ps([128, 512], F32, "pzC%d_%d" % (i, j), st) for j in range(3)] for i in range(NP)]
            xqC = [newsem() for _ in range(NP)]
            yqC = [newsem() for _ in range(NP)]
            hqC = [newsem() for _ in range(NP)]

            def tileC(i):
                p = i % NP
                x_, yl_, ss, nb, nT, pT = xt[p], yl[p], ssC[p], nbC[p], nTC[p], pTC[p]
                gts, mrg, mrb, mT, h_ = gtsC[p], mrgC[p], mrbC[p], mTC[p], hhC[p]
                pzs = pzC[p]
                cnt = [0]

                def pz_():
                    cnt[0] += 1
                    return pzs[cnt[0] % 3]
                P.dma("sp", x_[:], xs[i * 128:(i + 1) * 128, :], (), [x_], sem=xqC[p])
                P.dma("sp", yl_[:, 0, :], scr_ya[i], (), [yl_], sem=yqC[p])
                P.dma("sp", yl_[:, 1, :], scr_yrw[i], (), [yl_], sem=yqC[p])
                yield
                P.act(nb[:], x_[:], AF.Square, [x_], [nb, ss], accum=ss[:])
                yield
                P.ts("dve", ss[:], ss[:], 1.0 / D, EPS, ALU.mult, ALU.add, [ss], [ss])
                yield
                P.act(ss[:], ss[:], AF.Ln, [ss], [ss])
                yield
                P.act(ss[:], ss[:], AF.Exp, [ss], [ss], scale=-0.5)
                yield
                P.stt("dve", nb[:], x_[:], ss[:], gv[:, 0, :], ALU.mult, ALU.mult, [x_, ss, gv], [nb])
                P.cp("pool", ylb[p][:], yl_[:], [yl_], [ylb[p]])
                yield
                for k in range(8):
                    P.tr(pT[:, k * 128:(k + 1) * 128], nb[:, k * 128:(k + 1) * 128], identb[:], [nb, identb], [pT], sig=(k == 7))
                yield
                P.cp("act", nT[:].rearrange("p a b -> p (a b)"), pT[:], [pT], [nT])
                yield
                for k in range(8):
                    P.tr(pT[:, k * 128:(k + 1) * 128], ylb[p][:, k // 4, (k % 4) * 128:(k % 4 + 1) * 128], identb[:], [ylb[p], identb], [pT], sig=(k == 7))
                yield
                P.cp("dve", ylT[p][:].rearrange("p a b -> p (a b)"), pT[:], [pT], [ylT[p]])
                for g0 in range(4):
                    pz = pz_()
                    for k in range(8):
                        P.mm(pz[:], nT[:, k, :], wgt[:, k, g0 * 512:(g0 + 1) * 512], k == 0, k == 7, [nT, wgt], [pz])
                    yield
                    P.act(gts[:, g0 * 512:(g0 + 1) * 512], pz[:], AF.Sigmoid, [pz], [gts])
                for g0 in range(2):
                    pz = pz_()
                    for k in range(4):
                        P.mm(pz[:], ylT[p][:, k, :], wao[:, k, g0 * 512:(g0 + 1) * 512], k == 0, k == 3, [ylT[p], wao], [pz])
                    yield
                    P.tt("dve", mrg[:, g0 * 512:(g0 + 1) * 512], pz[:], gts[:, g0 * 512:(g0 + 1) * 512], ALU.mult, [pz, gts], [mrg])
                    pz2 = pz_()
                    for k in range(4):
                        P.mm(pz2[:], ylT[p][:, 4 + k, :], wbo[:, k, g0 * 512:(g0 + 1) * 512], k == 0, k == 3, [ylT[p], wbo], [pz2])
                    yield
                    P.tt("dve", gts[:, 1024 + g0 * 512:1024 + (g0 + 1) * 512], pz2[:], gts[:, 1024 + g0 * 512:1024 + (g0 + 1) * 512], ALU.mult, [pz2, gts], [gts])
                    yield
                    P.tt("pool", mrb[:, g0 * 512:(g0 + 1) * 512], mrg[:, g0 * 512:(g0 + 1) * 512], gts[:, 1024 + g0 * 512:1024 + (g0 + 1) * 512], ALU.add, [mrg, gts], [mrb])
                yield
                for k in range(8):
                    P.tr(pT[:, k * 128:(k + 1) * 128], mrb[:, k * 128:(k + 1) * 128], identb[:], [mrb, identb], [pT], sig=(k == 7))
                yield
                P.cp("act", mT[:].rearrange("p a b -> p (a b)"), pT[:], [pT], [mT])
                for g0 in range(2):
                    pz = pz_()
                    for k in range(8):
                        P.mm(pz[:], mT[:, k, :], wo[:, k, g0 * 512:(g0 + 1) * 512], k == 0, k == 7, [mT, wo], [pz])
                    yield
                    P.tt("dve", h_[:, g0 * 512:(g0 + 1) * 512], pz[:], x_[:, g0 * 512:(g0 + 1) * 512], ALU.add, [pz, x_], [h_])
                P.dma("act", scr_h[i], h_[:], [h_], (), sem=hqC[p])

            rr([chain_gens(tileC, range(0, nt, 2)), delayed_start(chain_gens(tileC, range(1, nt, 2)), 16)])
            P.barrier()
            P.emit()

        with contextlib.ExitStack() as st:
            gv = alloc_gv(st, (1, 2))
            wf1 = P.sb([128, 8, 4096], BF16, "wf1", st)
            wf2 = P.sb([128, 32, D], BF16, "wf2", st)
            wpg = P.sb([128, 8, D], BF16, "wpg", st)
            wpl = P.sb([128, 2, D], BF16, "wpl", st)
            with contextlib.ExitStack() as st_w:
                alloc_stages(st_w, 6)
                load_w(wf1, None, w_ff1, 8, 4096)
                load_w(wf2, None, w_ff2, 32, D)
                load_w(wpg, None, w_pgate, 8, D)
                load_w(wpl, None, w_ple, 2, D)
                P.barrier()
                P.emit()
            ht = [P.sb([128, D], F32, "ht%d" % i, st) for i in range(2)]
            pt_ = [P.sb([128, 256], F32, "pt%d" % i, st) for i in range(2)]
            ss3 = [P.sb([128, 1], F32, "ss%d" % i, st) for i in range(2)]
            nb3 = [P.sb([128, D], BF16, "nb%d" % i, st) for i in range(2)]
            nT3 = [P.sb([128, 8, 128], BF16, "nT%d" % i, st) for i in range(2)]
            hT = [P.sb([128, 32, 128], BF16, "hT%d" % i, st) for i in range(2)]
            rl3 = [P.sb([128, 512], F32, "rl%d" % i, st) for i in range(2)]
            pb3 = [P.sb([128, 256], BF16, "pb%d" % i, st) for i in range(2)]
            pT2 = [P.sb([128, 2, 128], BF16, "pT2%d" % i, st) for i in range(2)]
            gt3 = [P.sb([128, D], F32, "gt%d" % i, st) for i in range(2)]
            pT3 = [P.ps([128, 1024], BF16, "pT3%d" % i, st) for i in range(2)]
            pz3 = [[P.ps([128, 512], F32, "pz3%d_%d" % (i, j), st) for j in range(3)] for i in range(2)]
            hq3 = [newsem() for _ in range(2)]
            pq3 = [newsem() for _ in range(2)]
            oq3 = [newsem() for _ in range(2)]

            def norm3(p, src, grow):
                ss, nb, nT, pT = ss3[p], nb3[p], nT3[p], pT3[p]
                P.act(nb[:], src[:], AF.Square, [src], [nb, ss], accum=ss[:])
                yield
                P.ts("dve", ss[:], ss[:], 1.0 / D, EPS, ALU.mult, ALU.add, [ss], [ss])
                yield
                P.act(ss[:], ss[:], AF.Ln, [ss], [ss])
                yield
                P.act(ss[:], ss[:], AF.Exp, [ss], [ss], scale=-0.5)
                yield
                P.stt("dve", nb[:], src[:], ss[:], gv[:, grow, :], ALU.mult, ALU.mult, [src, ss, gv], [nb])
                yield
                for k in range(8):
                    P.tr(pT[:, k * 128:(k + 1) * 128], nb[:, k * 128:(k + 1) * 128], identb[:], [nb, identb], [pT], sig=(k == 7))
                yield
                P.cp("act", nT[:].rearrange("p a b -> p (a b)"), pT[:], [pT], [nT])
                yield

            def tile3(c):
                p = c % 2
                h_, p_, nT, pT, rl, gt_ = ht[p], pt_[p], nT3[p], pT3[p], rl3[p], gt3[p]
                pzs = pz3[p]
                cnt = [0]

                def pz_():
                    cnt[0] += 1
                    return pzs[cnt[0] % 3]
                P.dma("sp", h_[:], scr_h[c], (), [h_], sem=hq3[p])
                P.dma("sp", p_[:], pp[c * 128:(c + 1) * 128, :], (), [p_], sem=pq3[p])
                yield
                yield from norm3(p, h_, 0)
                for f4 in range(8):
                    pz = pz_()
                    for f in range(4):
                        fc = f4 * 4 + f
                        for k in range(8):
                            P.mm(pz[:, f * 128:(f + 1) * 128], wf1[:, k, fc * 128:(fc + 1) * 128], nT[:, k, :], k == 0, k == 7,
                                 [wf1, nT], [pz], sig=(k == 7 and f == 3))
                    yield
                    P.act(rl[:], pz[:], AF.Relu, [pz], [rl])
                    yield
                    P.tt("dve" if f4 % 2 else "pool", hT[p][:, f4 * 4:(f4 + 1) * 4, :].rearrange("p a b -> p (a b)"), rl[:], rl[:], ALU.mult, [rl], [hT[p]])
                for g0 in range(2):
                    pz = pz_()
                    for k in range(32):
                        P.mm(pz[:], hT[p][:, k, :], wf2[:, k, g0 * 512:(g0 + 1) * 512], k == 0, k == 31, [hT[p], wf2], [pz])
                    yield
                    P.tt("dve", h_[:, g0 * 512:(g0 + 1) * 512], pz[:], h_[:, g0 * 512:(g0 + 1) * 512], ALU.add, [pz, h_], [h_])
                yield
                yield from norm3(p, h_, 1)
                P.cp("pool", pb3[p][:], p_[:], [p_], [pb3[p]])
                yield
                for k in range(2):
                    P.tr(pT[:, k * 128:(k + 1) * 128], pb3[p][:, k * 128:(k + 1) * 128], identb[:], [pb3[p], identb], [pT], sig=(k == 1))
                yield
                P.cp("act", pT2[p][:].rearrange("p a b -> p (a b)"), pT[:, 0:256], [pT], [pT2[p]])
                for g0 in range(2):
                    pz = pz_()
                    for k in range(8):
                        P.mm(pz[:], nT[:, k, :], wpg[:, k, g0 * 512:(g0 + 1) * 512], k == 0, k == 7, [nT, wpg], [pz])
                    yield
                    P.act(gt_[:, g0 * 512:(g0 + 1) * 512], pz[:], AF.Sigmoid, [pz], [gt_])
                    pz2 = pz_()
                    for k in range(2):
                        P.mm(pz2[:], pT2[p][:, k, :], wpl[:, k, g0 * 512:(g0 + 1) * 512], k == 0, k == 1, [pT2[p], wpl], [pz2])
                    yield
                    P.tt("dve", gt_[:, g0 * 512:(g0 + 1) * 512], pz2[:], gt_[:, g0 * 512:(g0 + 1) * 512], ALU.mult, [pz2, gt_], [gt_])
                    yield
                    P.tt("pool", gt_[:, g0 * 512:(g0 + 1) * 512], gt_[:, g0 * 512:(g0 + 1) * 512], h_[:, g0 * 512:(g0 + 1) * 512], ALU.add, [gt_, h_], [gt_])
                P.dma("act", yout[c * 128:(c + 1) * 128, :], gt_[:], [gt_], (), sem=oq3[p])

            rr([chain_gens(tile3, range(0, nt, 2)), delayed_start(chain_gens(tile3, range(1, nt, 2)), 24)])
            P.barrier()
            P.emit()
    return nc


NT_FULL = 64
SEG_T_FULL = 16
_CACHE = {}


def make_in_maps(super_x, super_p, prompt_flags, W, nt, seg_t):
    rbG, rbT = expand_rel_bias(np.asarray(W["rel_bias"][0], np.float32))
    t8 = lambda v: np.tile(np.asarray(v, np.float32).reshape(-1), 8)
    vec512 = np.stack([W["w0_f"][0], W["w0_b"][0], W["a0"][0], W["k_k"][0], W["k_a"][0],
                       np.asarray(W["r_k"][0]).reshape(-1), W["ln_x_w"][0], W["ln_x_b"][0], t8(W["q_gain"][0])]).astype(np.float32)
    vec512b = t8(W["k_gain"][0])[None, :].astype(np.float32)
    gvec = np.stack([W["g_mix"][0], W["g_ffn"][0], W["g_ple"][0]]).astype(np.float32)
    w_up = np.stack([W["w_up_f"][0], W["w_up_b"][0], W["a_up"][0]]).astype(np.float32)
    shared = dict(w_in=W["w_in"][0], conv_w=W["conv_w"][0], vec512=vec512, vec512b=vec512b, gvec=gvec, w_up=w_up,
                  g_up=W["g_up"][0], w_a_out=W["w_a_out"][0], w_b_out=W["w_b_out"][0], w_o=W["w_o"][0],
                  w_ff1=W["w_ff1"][0], w_ff2=W["w_ff2"][0], w_ple=W["w_ple"][0], w_pgate=W["w_pgate"][0],
                  rbG=rbG, rbT=rbT)
    shared = {k: np.ascontiguousarray(np.asarray(v, np.float32)) for k, v in shared.items()}
    consts = {True: host_consts(nt, seg_t, True), False: host_consts(nt, seg_t, False)}
    maps = []
    for x, p, pf in zip(super_x, super_p, prompt_flags):
        m = dict(shared)
        hc = consts[bool(pf)]
        m.update(xs=np.ascontiguousarray(x, np.float32), pp=np.ascontiguousarray(p, np.float32),
                 flag=np.full((128, 1), 1.0 if pf else 0.0, np.float32),
                 c_ident=hc["ident"], c_cum=hc["cum"], c_amask=hc["amask"], c_nmask=hc["nmask"],
                 c_gm=hc["gm"], c_tm=hc["tm"], c_val=hc["val"])
        maps.append(m)
    return maps


def kernel(**inputs):
    nt, seg_t = NT_FULL, SEG_T_FULL
    xp = np.asarray(inputs["x_prompt"], np.float32)
    xsm = np.asarray(inputs["x_sample"], np.float32)
    pq = np.asarray(inputs["p_prompt"], np.float32)[0]
    psm = np.asarray(inputs["p_sample"], np.float32)[0]
    sx = [xp[0], xp[1]] + [xsm[4 * i:4 * i + 4].reshape(8192, D) for i in range(4)]
    sp_ = [pq[0], pq[1]] + [psm[4 * i:4 * i + 4].reshape(8192, 256) for i in range(4)]
    fl = [True, True, False, False, False, False]
    sx += [sx[4], sx[5]]
    sp_ += [sp_[4], sp_[5]]
    fl += [False, False]
    W = {k: np.asarray(v) for k, v in inputs.items() if k not in ("x_prompt", "x_sample", "p_prompt", "p_sample")}
    maps = make_in_maps(sx, sp_, fl, W, nt, seg_t)
    if "nc" not in _CACHE:
        _CACHE["nc"] = build(nt, seg_t)
    res = run_bass_kernel_spmd(_CACHE["nc"], maps, core_ids=list(range(NCORE)))
    outs = [np.asarray(r["yout"], np.float32) for r in res.results]
    y_prompt = np.stack([outs[0], outs[1]])
    y_sample = np.concatenate([outs[2 + i].reshape(4, 2048, D) for i in range(4)], 0)
    return (y_prompt, y_sample)
```

```python
import contextlib
import numpy as np
import concourse.bass as bass
import concourse.mybir as mybir
from concourse.bass_utils import run_bass_kernel_spmd

F32 = mybir.dt.float32
BF16 = mybir.dt.bfloat16
ALU = mybir.AluOpType
AF = mybir.ActivationFunctionType
AX = mybir.AxisListType

D = 1024
DS = 0.606531
EPS = 1e-6
GN_EPS = 64e-5
RWC = 1856
NCORE = 8


class Buf:
    __slots__ = ("t", "lw", "rd", "name")

    def __init__(self, t, name=""):
        self.t = t
        self.lw = None
        self.rd = {}
        self.name = name

    def __getitem__(self, k):
        return self.t[k]


class Prog:
    ENG = ("pe", "act", "dve", "pool", "sp")

    def __init__(self, nc, stack):
        self.nc = nc
        self.stack = stack
        self.q = {e: [] for e in self.ENG}
        self.sems = {}
        self.cnt = {}
        self.seen = {e: {} for e in self.ENG}
        for e in self.ENG:
            self.sems[e] = stack.enter_context(nc.semaphore("s_" + e))
            self.cnt[e] = 0
        self.n_inst = 0
        self.rr = 0
        self.dma_pool = {}
        self.dma_rr = {}
        self.DMA_POOL_SIZE = {"sp": 32, "act": 16, "pool": 8}

    def sb(self, shape, dt, name, stack=None):
        self.uid = getattr(self, "uid", 0) + 1
        name = "%s_%d" % (name, self.uid)
        t = (stack or self.stack).enter_context(self.nc.sbuf_tensor(name, list(shape), dt))
        return Buf(t, name)

    def ps(self, shape, dt, name, stack=None):
        self.uid = getattr(self, "uid", 0) + 1
        name = "%s_%d" % (name, self.uid)
        t = (stack or self.stack).enter_context(self.nc.psum_tensor(name, list(shape), dt))
        return Buf(t, name)

    def dma_sem(self, name):
        s = self.stack.enter_context(self.nc.semaphore(name))
        self.sems[name] = s
        self.cnt[name] = 0
        return name

    def _waits(self, eng, reads, writes):
        need = {}
        for b in reads:
            if b.lw is not None:
                k, v = b.lw
                need[k] = max(need.get(k, 0), v)
        for b in writes:
            if b.lw is not None:
                k, v = b.lw
                need[k] = max(need.get(k, 0), v)
            for k, v in b.rd.items():
                need[k] = max(need.get(k, 0), v)
        out = []
        seen = self.seen[eng]
        for k, v in need.items():
            if k == eng and eng in ("pe", "sp"):
                continue
            if seen.get(k, 0) < v:
                seen[k] = v
                out.append((k, v))
        return out

    def op(self, eng, fn, reads=(), writes=(), sig=True):
        waits = self._waits(eng, reads, writes)
        if sig:
            self.cnt[eng] += 1
            val = self.cnt[eng]
        else:
            val = self.cnt[eng] + 1
        sem = self.sems[eng]
        sems = self.sems

        def run(e):
            for k, v in waits:
                e.wait_ge(sems[k], v)
            ins = fn(e)
            if sig:
                ins.then_inc(sem, 1)
        self.q[eng].append(run)
        self.n_inst += 1
        for b in reads:
            b.rd[eng] = max(b.rd.get(eng, 0), val)
        for b in writes:
            b.lw = (eng, val)
            b.rd = {}

    def dma(self, eng, out_ap, in_ap, reads=(), writes=(), sem=None):
        pool = self.dma_pool.setdefault(eng, [])
        if not pool:
            for i in range(self.DMA_POOL_SIZE.get(eng, 8)):
                pool.append(self.dma_sem("dp_%s_%d" % (eng, i)))
            self.dma_rr[eng] = 0
        sem = pool[self.dma_rr[eng] % len(pool)]
        self.dma_rr[eng] += 1
        waits = self._waits(eng, reads, writes)
        prev = self.cnt[sem]
        if prev > 0 and self.seen[eng].get(sem, 0) < prev:
            self.seen[eng][sem] = prev
            waits = [w for w in waits if w[0] != sem] + [(sem, prev)]
        self.cnt[sem] += 16
        val = self.cnt[sem]
        s = self.sems[sem]
        sems = self.sems

        def run(e):
            for k, v in waits:
                e.wait_ge(sems[k], v)
            e.dma_start(out=out_ap, in_=in_ap).then_inc(s, 16)
        self.q[eng].append(run)
        self.n_inst += 1
        for b in reads:
            b.rd[sem] = max(b.rd.get(sem, 0), val)
        for b in writes:
            b.lw = (sem, val)
            b.rd = {}

    def barrier(self):
        tot = dict(self.cnt)
        sems = self.sems
        for eng in self.ENG:
            waits = []
            for k, v in tot.items():
                if v > 0 and k != eng and self.seen[eng].get(k, 0) < v:
                    self.seen[eng][k] = v
                    waits.append((k, v))

            def run(e, waits=waits):
                for k, v in waits:
                    e.wait_ge(sems[k], v)
            self.q[eng].append(run)

    def emit(self):
        nc = self.nc
        q = self.q
        with nc.Block() as block:
            @block.tensor
            def _(e):
                for f in q["pe"]:
                    f(e)

            @block.scalar
            def _(e):
                for f in q["act"]:
                    f(e)

            @block.vector
            def _(e):
                for f in q["dve"]:
                    f(e)

            @block.gpsimd
            def _(e):
                for f in q["pool"]:
                    f(e)

            @block.sync
            def _(e):
                for f in q["sp"]:
                    f(e)
        self.q = {e: [] for e in self.ENG}

    def tt(self, eng, out, a, b, op, R, W):
        self.op(eng, lambda e: e.tensor_tensor(out, a, b, op), R, W)

    def ts(self, eng, out, a, s1, s2, op0, op1, R, W):
        if s2 is None:
            self.op(eng, lambda e: e.tensor_scalar(out, a, s1, None, op0), R, W)
        else:
            self.op(eng, lambda e: e.tensor_scalar(out, a, s1, s2, op0, op1), R, W)

    def stt(self, eng, out, a, s, b, op0, op1, R, W):
        self.op(eng, lambda e: e.scalar_tensor_tensor(out, a, s, b, op0, op1), R, W)

    def act(self, out, a, func, R, W, scale=1.0, accum=None):
        if accum is None:
            self.op("act", lambda e: e.activation(out, a, func, scale=scale), R, W)
        else:
            self.op("act", lambda e: e.activation(out, a, func, scale=scale, accum_out=accum), R, W)

    def cp(self, eng, out, a, R, W):
        if eng == "act":
            self.op("act", lambda e: e.activation(out, a, AF.Copy), R, W)
        else:
            self.op(eng, lambda e: e.tensor_copy(out, a), R, W)

    def red(self, eng, out, a, R, W):
        self.op(eng, lambda e: e.reduce_sum(out, a, AX.X), R, W)

    def rcp(self, out, a, R, W):
        self.op("dve", lambda e: e.reciprocal(out, a), R, W)

    def mm(self, out, lhsT, rhs, start, stop, R, W, sig=None):
        if sig is None:
            sig = stop
        self.op("pe", lambda e: e.matmul(out, lhsT, rhs, start=start, stop=stop), R, W, sig=sig)

    def tr(self, out, a, ident, R, W, sig=True):
        self.op("pe", lambda e: e.transpose(out, a, ident), R, W, sig=sig)


def na_slots(i, nt, seg_t):
    m = i % seg_t
    if 2 <= m <= seg_t - 3:
        js = [i + 2 - s for s in range(5)]
        return js, "T", -4
    js = [i + 3 - s for s in range(7)]
    return js, "G", -6


def boundary_tiles(nt, seg_t):
    return [i for i in range(nt) if not (2 <= i % seg_t <= seg_t - 3)]


def host_consts(nt, seg_t, prompt):
    j = np.arange(128)[:, None]
    t = np.arange(128)[None, :]
    lt = (j < t).astype(np.float32)
    le = (j <= t).astype(np.float32)
    gt = (j > t).astype(np.float32)
    ge = (j >= t).astype(np.float32)
    cum = np.stack([lt, le, gt, ge], 1)
    amask = np.stack([np.concatenate([-lt, -le, lt, le], 1),
                      np.concatenate([-gt, -ge, gt, ge], 1)], 1)
    nmask = np.stack([-gt, -lt], 1)
    kc = np.arange(64)[:, None]
    qc = np.arange(64)[None, :]
    cs = np.clip(qc - 8, 0, 48)
    colm = ((kc >= cs) & (kc < cs + 16)).astype(np.float32)
    par = (np.arange(128) // 64)
    gm = np.zeros((128, 16, 64), np.float32)
    tm = np.zeros((128, 10, 64), np.float32)
    for p in range(128):
        for a, idx in enumerate(range(-7, 9)):
            dr = par[p] - idx
            if -7 <= dr <= 7:
                gm[p, a] = colm[p % 64]
        for a, idx in enumerate(range(-4, 6)):
            dr = par[p] - idx
            if -4 <= dr <= 3:
                tm[p, a] = colm[p % 64]
    bts = boundary_tiles(nt, seg_t)
    val = np.zeros((128, len(bts), 7, 2), np.float32)
    seq_t = nt if prompt else seg_t
    rows = 2 * seq_t
    for bi, i in enumerate(bts):
        s0 = (i // seq_t) * seq_t
        for s in range(7):
            jt = i + 3 - s
            if jt < 0 or jt >= nt or jt // seq_t != i // seq_t:
                continue
            for pr in range(2):
                for qp in range(2):
                    r = 2 * (i - s0) + qp
                    kr_ = 2 * (jt - s0) + pr
                    rs = min(max(r - 4, 0), rows - 8)
                    if rs <= kr_ < rs + 8:
                        val[pr * 64:(pr + 1) * 64, bi, s, qp] = 1.0
    return dict(cum=cum, amask=amask, nmask=nmask, gm=gm, tm=tm, val=val,
                ident=np.eye(128, dtype=np.float32))


def expand_rel_bias(rb):
    par = (np.arange(128) // 64)[:, None, None]
    kc = (np.arange(128) % 64)[:, None, None]
    qc = np.arange(64)[None, None, :]
    dc = np.clip(kc - qc + 15, 0, 30)

    def tab(idxs):
        idx = np.array(list(idxs))[None, :, None]
        dr = np.clip(par - idx + 7, 0, 14)
        return np.ascontiguousarray(np.transpose(rb[:, dr, dc], (1, 0, 2, 3)))
    return tab(range(-7, 9)), tab(range(-4, 6))


def rr(gens):
    gens = [g for g in gens if g is not None]
    while gens:
        nxt = []
        for g in gens:
            try:
                next(g)
                nxt.append(g)
            except StopIteration:
                pass
        gens = nxt


def fast(gen, n):
    while True:
        for _ in range(n):
            try:
                next(gen)
            except StopIteration:
                return
        yield


def delayed_start(gen, n):
    for _ in range(n):
        yield
    yield from gen


def chain_gens(fn, items):
    for it_ in items:
        yield from fn(it_)


def build(nt, seg_t, dbg=False):
    S = nt * 128
    nc = bass.Bass("TRN2", target_bir_lowering=False)
    dr = lambda n, sh, kind="ExternalInput": nc.dram_tensor(n, list(sh), F32, kind=kind).ap()
    xs = dr("xs", [S, D])
    pp = dr("pp", [S, 256])
    flag = dr("flag", [128, 1])
    w_in = dr("w_in", [D, 5440])
    conv_w = dr("conv_w", [3, RWC])
    vec512 = dr("vec512", [9, 512])
    vec512b = dr("vec512b", [1, 512])
    gvec = dr("gvec", [3, D])
    w_up = dr("w_up", [3, 64, 512])
    g_up = dr("g_up", [128, 512])
    w_a_out = dr("w_a_out", [512, D])
    w_b_out = dr("w_b_out", [512, D])
    w_o = dr("w_o", [D, D])
    w_ff1 = dr("w_ff1", [D, 4096])
    w_ff2 = dr("w_ff2", [4096, D])
    w_ple = dr("w_ple", [256, D])
    w_pgate = dr("w_pgate", [D, D])
    nb_t = len(boundary_tiles(nt, seg_t))
    c_ident = dr("c_ident", [128, 128])
    c_cum = dr("c_cum", [128, 4, 128])
    c_amask = dr("c_amask", [128, 2, 512])
    c_nmask = dr("c_nmask", [128, 2, 128])
    c_gm = dr("c_gm", [128, 16, 64])
    c_tm = dr("c_tm", [128, 10, 64])
    c_val = dr("c_val", [128, nb_t, 7, 2])
    rbG = dr("rbG", [128, 8, 16, 64])
    rbT = dr("rbT", [128, 8, 10, 64])
    yout = dr("yout", [S, D], "ExternalOutput")
    scr_u = dr("scr_u", [nt, 128, RWC], "ExternalOutput" if dbg else "Internal")
    scr_y = dr("scr_y", [nt, 128, 512], "Internal")
    scr_h = dr("scr_h", [nt, 128, D], "Internal")
    scr_yrw = dr("scr_yrw", [nt, 128, 512], "Internal")
    scr_ya = dr("scr_ya", [nt, 128, 512], "Internal")

    with contextlib.ExitStack() as st0:
        P = Prog(nc, st0)
        ldq = [None] * 4
        cq = wq = stq = None
        s2q = [None] * 2

        def newsem():
            return None

        identf = P.sb([128, 128], F32, "identf")
        identb = P.sb([128, 128], BF16, "identb")
        ones1 = P.sb([128, 1], F32, "ones1")
        flg = P.sb([128, 1], F32, "flg")
        wup = gup = None
        P.dma("sp", identf[:], c_ident, writes=[identf], sem=cq)
        P.dma("sp", flg[:], flag, writes=[flg], sem=cq)
        P.cp("dve", identb[:], identf[:], [identf], [identb])
        P.op("pool", lambda e: e.memset(ones1[:], 1.0), (), [ones1])
        W0F, W0B, A0, KK, KA, RK, LNW, LNB, QG = range(9)
        stages = [None, None]
        gv = None
        vb = vb2 = cum = amask = nmask = None

        def alloc_stages(st_, n=2):
            del stages[:]
            for i_ in range(n):
                stages.append(P.sb([128, 2048], F32, "stage", st_))

        def alloc_gv(st_, rows=(0, 1, 2)):
            g_ = P.sb([128, len(rows), D], F32, "gv", st_)
            for i_, r in enumerate(rows):
                P.dma("sp", g_[:, i_, :], gvec[r, :].partition_broadcast(128), writes=[g_], sem=cq)
            return g_

        def small_consts(st_, need2=False, rw=True):
            vb_ = P.sb([128, 9, 512], F32, "vb", st_)
            vb2_ = P.sb([128, 512 if need2 else 1], F32, "vb2", st_)
            cum_ = amask_ = nmask_ = None
            if rw:
                cum_ = P.sb([128, 4, 128], F32, "cum", st_)
                amask_ = P.sb([128, 2, 512], F32, "amask", st_)
                nmask_ = P.sb([128, 2, 128], F32, "nmask", st_)
                P.dma("sp", cum_[:], c_cum, writes=[cum_], sem=cq)
                P.dma("sp", amask_[:], c_amask, writes=[amask_], sem=cq)
                P.dma("sp", nmask_[:], c_nmask, writes=[nmask_], sem=cq)
            for r in range(9):
                P.dma("sp", vb_[:, r, :], vec512[r, :].partition_broadcast(128), writes=[vb_], sem=cq)
            if need2:
                P.dma("sp", vb2_[:], vec512b[0, :].partition_broadcast(128), writes=[vb2_], sem=cq)
            return vb_, vb2_, cum_, amask_, nmask_

        wcnt = [0]

        def load_w(dst, _unused, src, kchunks, ncols, col0=0):
            for k in range(kchunks):
                for c0 in range(0, ncols, 2048):
                    cwid = min(2048, ncols - c0)
                    sg = stages[wcnt[0] % len(stages)]
                    wcnt[0] += 1
                    P.dma("sp" if wcnt[0] % 2 else "act", sg[:, 0:cwid], src[k * 128:(k + 1) * 128, col0 + c0:col0 + c0 + cwid], writes=[sg], sem=wq)
                    eng = "act" if (wcnt[0] % 2) else "dve"
                    P.cp(eng, dst[:, k, c0:c0 + cwid], sg[:, 0:cwid], [sg], [dst])

        def rmsnorm_T(st_bufs, xt, grow, nT):
            sq, ss, nb, pT = st_bufs
            P.act(nb[:], xt[:], AF.Square, [xt], [nb, ss], accum=ss[:])
            P.ts("dve", ss[:], ss[:], 1.0 / D, EPS, ALU.mult, ALU.add, [ss], [ss])
            P.act(ss[:], ss[:], AF.Ln, [ss], [ss])
            P.act(ss[:], ss[:], AF.Exp, [ss], [ss], scale=-0.5)
            P.stt("dve", nb[:], xt[:], ss[:], gv[:, grow, :], ALU.mult, ALU.mult, [xt, ss, gv], [nb])
            for c in range(8):
                P.tr(pT[:, c * 128:(c + 1) * 128], nb[:, c * 128:(c + 1) * 128], identb[:], [nb, identb], [pT], sig=(c == 7))
            P.cp("act", nT[:].rearrange("p a b -> p (a b)"), pT[:], [pT], [nT])

        with contextlib.ExitStack() as st:
            alloc_stages(st)
            gv = alloc_gv(st)
            wrw = P.sb([128, 8, RWC], BF16, "wrw", st)
            load_w(wrw, None, w_in, 8, RWC, col0=1536)
            cw = P.sb([128, 3, RWC], F32, "cw", st)
            for r in range(3):
                P.dma("sp", cw[:, r, :], conv_w[r, :].partition_broadcast(128), writes=[cw], sem=cq)
            xt = [P.sb([128, D], F32, "xt%d" % i, st) for i in range(2)]
            ssA = [P.sb([128, 1], F32, "ss%d" % i, st) for i in range(2)]
            nbA = [P.sb([128, D], BF16, "nb%d" % i, st) for i in range(2)]
            nTA = [P.sb([128, 8, 128], BF16, "nT%d" % i, st) for i in range(2)]
            z = [P.sb([128, RWC], F32, "z%d" % i, st) for i in range(6)]
            zp = [P.sb([128, RWC], F32, "zp%d" % i, st) for i in range(2)]
            zn = [P.sb([128, RWC], F32, "zn%d" % i, st) for i in range(2)]
            tB = [P.sb([1, RWC], F32, "tB%d" % i, st) for i in range(2)]
            pTA = [P.ps([128, 1024], BF16, "pTA%d" % i, st) for i in range(2)]
            pzA = [[P.ps([128, 512], F32, "pzA%d_%d" % (i, j), st) for j in range(3)] for i in range(2)]
            xq = [newsem() for _ in range(2)]
            shs = [newsem() for _ in range(2)]
            uq = [newsem() for _ in range(2)]

            def tileA(c):
                p = c % 2
                x_ = xt[p]
                P.dma("sp", x_[:], xs[c * 128:(c + 1) * 128, :], (), [x_], sem=xq[p])
                yield
                ss, nb, nT, pT = ssA[p], nbA[p], nTA[p], pTA[p]
                P.act(nb[:], x_[:], AF.Square, [x_], [nb, ss], accum=ss[:])
                P.ts("dve", ss[:], ss[:], 1.0 / D, EPS, ALU.mult, ALU.add, [ss], [ss])
                yield
                P.act(ss[:], ss[:], AF.Ln, [ss], [ss])
                P.act(ss[:], ss[:], AF.Exp, [ss], [ss], scale=-0.5)
                yield
                P.stt("dve", nb[:], x_[:], ss[:], gv[:, 0, :], ALU.mult, ALU.mult, [x_, ss, gv], [nb])
                for k in range(8):
                    P.tr(pT[:, k * 128:(k + 1) * 128], nb[:, k * 128:(k + 1) * 128], identb[:], [nb, identb], [pT], sig=(k == 7))
                yield
                P.cp("act", nT[:].rearrange("p a b -> p (a b)"), pT[:], [pT], [nT])
                zc = z[c % 6]
                for gi, g0 in enumerate(range(0, RWC, 512)):
                    gw = min(512, RWC - g0)
                    pz = pzA[p][gi % 3]
                    for k in range(8):
                        P.mm(pz[:, 0:gw], nT[:, k, :], wrw[:, k, g0:g0 + gw], k == 0, k == 7, [nT, wrw], [pz])
                    yield
                    P.cp("act", zc[:, g0:g0 + gw], pz[:, 0:gw], [pz], [zc])

            zpT = [[Buf(zp[i].t, "zpT") for _ in range(3)] for i in range(2)]
            znT = [[Buf(zn[i].t, "znT") for _ in range(3)] for i in range(2)]

            def convA(c):
                p = c % 2
                zc = z[c % 6]
                zp_, zn_, tB_ = zp[p], zn[p], tB[p]
                zpt, znt = zpT[p], znT[p]
                first = (c == 0)
                last = (c == nt - 1)
                segs = (c % seg_t == 0)
                sege = (c % seg_t == seg_t - 1)
                P.dma("sp", zp_[1:113, :], zc[0:112, :], [zc], [zpt[0]])
                P.dma("sp", zp_[113:128, :], zc[112:127, :], [zc], [zpt[1]])
                if first:
                    P.op("pool", lambda e: e.memset(zp_[0:1, :], 0.0), (), [zpt[2]])
                else:
                    P.dma("sp", zp_[0:1, :], z[(c - 1) % 6][127:128, :], [z[(c - 1) % 6]], [zpt[2]])
                P.dma("sp", zn_[0:112, :], zc[1:113, :], [zc], [znt[0]])
                P.dma("sp", zn_[112:127, :], zc[113:128, :], [zc], [znt[1]])
                if last:
                    P.op("pool", lambda e: e.memset(tB_[:], 0.0), (), [tB_])
                elif sege:
                    P.ts("dve", tB_[:], z[(c + 1) % 6][0:1, :], flg[0:1, 0:1], None, ALU.mult, None, [z[(c + 1) % 6], flg], [tB_])
                else:
                    P.cp("dve", tB_[:], z[(c + 1) % 6][0:1, :], [z[(c + 1) % 6]], [tB_])
                P.dma("sp", zn_[127:128, :], tB_[:], [tB_], [znt[2]])
                yield
                if segs and not first:
                    P.ts("dve", zp_[0:1, :], zp_[0:1, :], flg[0:1, 0:1], None, ALU.mult, None, [zpt[2], flg], [zpt[2]])
                P.tt("dve", zn_[:], zn_[:], cw[:, 2, :], ALU.mult, znt + [cw], znt)
                P.tt("dve", zp_[:], zp_[:], cw[:, 0, :], ALU.mult, zpt + [cw], zpt)
                yield
                P.tt("dve", zn_[:], zn_[:], zp_[:], ALU.add, znt + zpt, znt)
                P.tt("dve", zp_[:], zc[:], cw[:, 1, :], ALU.mult, [zc, cw] + zpt, zpt)
                yield
                P.tt("dve", zn_[:], zn_[:], zp_[:], ALU.add, znt + zpt, znt)
                P.dma("act", scr_u[c], zn_[:], znt, ())

            tA = lambda c: tileA(c) if 0 <= c < nt else None
            cA = lambda c: convA(c) if 0 <= c < nt else None
            rr([tA(0), tA(1)])
            rr([tA(2)])
            for t0 in range(0, nt, 2):
                rr([tA(t0 + 3), tA(t0 + 4), cA(t0), cA(t0 + 1)])
            P.barrier()
            P.emit()

        half = nt // 2
        with contextlib.ExitStack() as st:
            vb, vb2, cum, amask, nmask = small_consts(st)
            wup = P.sb([64, 3, 512], BF16, "wup", st)
            gup = P.sb([128, 512], BF16, "gup", st)
            with contextlib.ExitStack() as st_w:
                alloc_stages(st_w)
                P.dma("sp", stages[0][0:64, 0:1536].rearrange("p (a b) -> p a b", a=3), w_up.rearrange("a p b -> p a b"), writes=[stages[0]], sem=wq)
                P.cp("dve", wup[:].rearrange("p a b -> p (a b)"), stages[0][0:64, 0:1536], [stages[0]], [wup])
                P.dma("sp", stages[1][:, 0:512], g_up, writes=[stages[1]], sem=wq)
                P.cp("dve", gup[:], stages[1][:, 0:512], [stages[1]], [gup])
                P.barrier()
                P.emit()
            SD = []
            for d in range(2):
                X = {}
                X["u"] = P.sb([128, RWC], F32, "u", st)
                X["t320"] = P.sb([128, 256], BF16, "t320", st)
                X["lT"] = P.sb([128, 3, 128], BF16, "lT", st)
                X["sg"] = P.sb([128, 512], F32, "sg", st)
                X["a"] = P.sb([128, 512], F32, "a_", st)
                X["ex"] = [P.sb([128, 512], F32, "ex%d" % i, st) for i in range(2)]
                X["kk"] = P.sb([128, 512], F32, "kk", st)
                X["tmp"] = P.sb([128, 512], F32, "tmp", st)
                X["s8"] = P.sb([128, 8], F32, "s8", st)
                X["bs"] = P.sb([128, 8], F32, "bs", st)
                X["kt"] = P.sb([128, 512], F32, "kt", st)
                X["beta"] = P.sb([128, 512], F32, "beta", st)
                X["tmb"] = P.sb([128, 4, 512], BF16, "tmb", st)
                X["fm"] = P.sb([64, 8, 4, 128], BF16, "fm", st)
                X["Vb"] = [P.sb([128, 512], BF16, "Vb%d" % i, st) for i in range(2)]
                X["KH"] = [P.sb([128, 512], BF16, "KH%d" % i, st) for i in range(2)]
                X["BH"] = [P.sb([128, 512], BF16, "BH%d" % i, st) for i in range(2)]
                X["eLC"] = [P.sb([64, 8], F32, "eLC%d" % i, st) for i in range(2)]
                X["g"] = [P.sb([128, 512], F32, "g%d" % i, st) for i in range(2)]
                X["bv"] = [P.sb([128, 512], F32, "bv%d" % i, st) for i in range(2)]
                yt = [P.sb([128, 512], F32, "y%d" % i, st) for i in range(2)]
                X["y"] = yt
                X["yh"] = [[Buf(yt[i].t, "yh") for _ in range(2)] for i in range(2)]
                X["yo"] = P.sb([128, 512], F32, "yo", st)
                X["oc"] = P.sb([128, 512], F32, "oc", st)
                X["s8c"] = P.sb([128, 8], F32, "s8c", st)
                X["s8d"] = P.sb([128, 8], F32, "s8d", st)
                X["yrw"] = X["oc"]
                X["prp"] = P.ps([128, 512], F32, "prp", st)
                X["pT"] = P.ps([128, 1024], BF16, "pTd", st)
                X["uq"] = newsem()
                X["yq"] = newsem()
                X["sq"] = newsem()
                X["rq"] = newsem()
                X["hg"] = []
                for hg in range(2):
                    G = {}
                    G["AG"] = P.sb([128, 4, 4, 128], BF16, "AG", st)
                    G["PN"] = [P.sb([128, 4, 128], BF16, "PN%d" % i, st) for i in range(2)]
                    G["PX"] = [P.sb([128, 4, 128], BF16, "PX%d" % i, st) for i in range(2)]
                    G["ZN"] = P.sb([128, 4, 128], BF16, "ZN", st)
                    G["ZX"] = P.sb([128, 4, 128], BF16, "ZX", st)
                    G["RHS"] = P.sb([128, 4, 64], BF16, "RHS", st)
                    G["U"] = P.sb([128, 4, 64], BF16, "U", st)
                    G["Hf"] = P.sb([64, 4, 64], F32, "Hf", st)
                    G["Hb"] = P.sb([64, 4, 64], BF16, "Hb", st)
                    G["ps"] = P.ps([128, 512], F32, "pinv", st)
                    X["hg"].append(G)
                SD.append(X)
            scrY = [Buf(None, "scrY%d" % c) for c in range(nt)]

            def tile_of(d, k):
                return k if d == 0 else nt - 1 - k

            def delayed(gen, n):
                for _ in range(n):
                    yield
                yield from gen

            def is_second(d, c):
                return (c >= half) if d == 0 else (c < half)

            def prepA(d, k):
                X = SD[d]
                c = tile_of(d, k)
                par = k % 2
                comb = is_second(d, c)
                u, t320, lT, sg, a_, ex = X["u"], X["t320"], X["lT"], X["sg"], X["a"], X["ex"]
                kk, tmp, s8, kt, beta, tmb = X["kk"], X["tmp"], X["s8"], X["kt"], X["beta"], X["tmb"]
                prp, pT = X["prp"], X["pT"]
                P.dma("sp", u[:], scr_u[c], (), [u], sem=X["uq"])
                yield
                r_ = u[:, 0:512]
                k_ = u[:, 512:1024]
                v_ = u[:, 1024:1536]
                P.act(t320[:, 0:64], u[:, 1536 + 64 * d:1600 + 64 * d], AF.Tanh, [u], [t320])
                P.act(t320[:, 64:128], u[:, 1664:1728], AF.Copy, [u], [t320])
                if comb:
                    P.act(t320[:, 128:256], u[:, 1728:1856], AF.Sigmoid, [u], [t320])
                yield
                P.tr(pT[0:64, 0:128], t320[:, 0:64], identb[:], [t320, identb], [pT], sig=False)
                P.tr(pT[0:64, 128:256], t320[:, 64:128], identb[:], [t320, identb], [pT], sig=not comb)
                if comb:
                    P.tr(pT[:, 256:384], t320[:, 128:256], identb[:], [t320, identb], [pT])
                yield
                if comb:
                    P.cp("dve", lT[:].rearrange("p a b -> p (a b)"), pT[:, 0:384], [pT], [lT])
                else:
                    P.cp("dve", lT[0:64, 0:2, :].rearrange("p a b -> p (a b)"), pT[0:64, 0:256], [pT], [lT])
                yield
                P.mm(prp[:], lT[0:64, 0, :], wup[:, d, :], True, True, [lT, wup], [prp])
                yield
                P.tt("dve", tmp[:], prp[:], vb[:, W0F + d, :], ALU.add, [prp, vb], [tmp])
                yield
                P.act(sg[:], tmp[:], AF.Sigmoid, [tmp], [sg])
                P.mm(prp[:], lT[0:64, 1, :], wup[:, 2, :], True, True, [lT, wup], [prp])
                yield
                P.tt("dve", tmp[:], prp[:], vb[:, A0, :], ALU.add, [prp, vb], [tmp])
                yield
                P.act(a_[:], tmp[:], AF.Sigmoid, [tmp], [a_])
                P.tt("pool", kk[:], k_, vb[:, KK, :], ALU.mult, [u, vb], [kk])
                yield
                P.act(tmp[:], kk[:], AF.Square, [kk], [tmp])
                yield
                P.red("dve", s8[:], tmp[:].rearrange("p (a b) -> p a b", a=8), [tmp], [s8])
                yield
                P.act(s8[:], s8[:], AF.Ln, [s8], [s8])
                P.act(s8[:], s8[:], AF.Exp, [s8], [s8], scale=-0.5)
                yield
                P.ts("dve", s8[:], s8[:], 1e12, None, ALU.min, None, [s8], [s8])
                yield
                kk3 = kk[:].rearrange("p (a b) -> p a b", a=8)
                P.tt("dve", kk3, kk3, s8[:].unsqueeze(2).to_broadcast([128, 8, 64]), ALU.mult, [kk, s8], [kk])
                P.stt("dve", tmp[:], a_[:], -1.0, vb[:, KA, :], ALU.add, ALU.mult, [a_, vb], [tmp])
                yield
                P.stt("dve", kt[:], tmp[:], 1.0, k_, ALU.add, ALU.mult, [tmp, u], [kt])
                P.tt("pool", beta[:], kk[:], a_[:], ALU.mult, [kk, a_], [beta])
                yield
                m1, m2, m4 = (0, 1, 2) if d == 0 else (2, 3, 0)
                P.mm(prp[:], cum[:, m1, :], sg[:], True, True, [cum, sg], [prp])
                yield
                P.act(ex[0][:], prp[:], AF.Exp, [prp], [ex[0]], scale=-DS)
                yield
                P.mm(prp[:], cum[:, m2, :], sg[:], True, True, [cum, sg], [prp])
                P.tt("dve", tmb[:, 0, :], kk[:], ex[0][:], ALU.mult, [kk, ex[0]], [tmb])
                yield
                P.act(ex[1][:], prp[:], AF.Exp, [prp], [ex[1]], scale=-DS)
                P.act(ex[0][:], prp[:], AF.Exp, [prp], [ex[0]], scale=DS)
                yield
                P.mm(prp[:], cum[:, m4, :], sg[:], True, True, [cum, sg], [prp])
                P.tt("pool", tmb[:, 1, :], r_, ex[1][:], ALU.mult, [u, ex[1]], [tmb])
                P.tt("dve", tmb[:, 2, :], kt[:], ex[0][:], ALU.mult, [kt, ex[0]], [tmb])
                yield
                P.tt("pool", tmb[:, 3, :], beta[:], ex[0][:], ALU.mult, [beta, ex[0]], [tmb])
                P.act(ex[1][:], prp[:], AF.Exp, [prp], [ex[1]], scale=-DS)
                yield
                for h in range(8):
                    P.mm(prp[0:64, h:h + 1], sg[:, h * 64:(h + 1) * 64], ones1[:], True, True, [sg, ones1], [prp], sig=(h == 7))
                P.tt("dve", X["KH"][par][:], kt[:], ex[1][:], ALU.mult, [kt, ex[1]], [X["KH"][par]])
                yield
                P.act(X["eLC"][par][:], prp[0:64, 0:8], AF.Exp, [prp], [X["eLC"][par]], scale=-DS)
                P.stt("dve", X["BH"][par][:], beta[:], -1.0, ex[1][:], ALU.mult, ALU.mult, [beta, ex[1]], [X["BH"][par]])
                yield
                P.cp("act", X["Vb"][par][:], v_, [u], [X["Vb"][par]])
                if comb:
                    P.tt("pool", tmp[:], r_, kt[:], ALU.mult, [u, kt], [tmp])
                    yield
                    P.tt("pool", tmp[:], tmp[:], vb[:, RK, :], ALU.mult, [tmp, vb], [tmp])
                    yield
                    P.red("dve", X["bs"][:], tmp[:].rearrange("p (a b) -> p a b", a=8), [tmp], [X["bs"]])

            def commit(d, k):
                X = SD[d]
                c = tile_of(d, k)
                par = k % 2
                comb = is_second(d, c)
                tmb, fm, pT, prp = X["tmb"], X["fm"], X["pT"], X["prp"]
                for hp in range(4):
                    for hh in range(2):
                        h = hp * 2 + hh
                        for m in range(4):
                            P.tr(pT[0:64, (hh * 4 + m) * 128:(hh * 4 + m + 1) * 128], tmb[:, m, h * 64:(h + 1) * 64],
                                 identb[:], [tmb, identb], [pT], sig=(hh == 1 and m == 3))
                    yield
                    P.cp("act" if hp % 2 else "dve", fm[:, hp * 2:hp * 2 + 2, :, :].rearrange("p a b c -> p (a b c)"),
                         pT[0:64, :], [pT], [fm])
                if comb:
                    P.mm(prp[:], X["lT"][:, 2, :], gup[:], True, True, [X["lT"], gup], [prp])
                    P.cp("act", X["g"][par][:], prp[:], [prp], [X["g"][par]])
                    P.tt("pool", X["bv"][par][:].rearrange("p (a b) -> p a b", a=8),
                         X["u"][:, 1024:1536].rearrange("p (a b) -> p a b", a=8),
                         X["bs"][:].unsqueeze(2).to_broadcast([128, 8, 64]), ALU.mult, [X["u"], X["bs"]], [X["bv"][par]])

            def invchain(d, hg, k):
                X = SD[d]
                G = X["hg"][hg]
                c = tile_of(d, k)
                par = k % 2
                fm, Vb, KH, BH, eLC = X["fm"], X["Vb"][par], X["KH"][par], X["BH"][par], X["eLC"][par]
                AG, PNs, PXs, ZN, ZX, RHS, U = G["AG"], G["PN"], G["PX"], G["ZN"], G["ZX"], G["RHS"], G["U"]
                Hf, Hb, ps = G["Hf"], G["Hb"], G["ps"]
                yb = X["yh"][par][hg]
                ytile = X["y"][par]
                hs = [hg * 4 + i for i in range(4)]
                if k == 0:
                    P.op("pool", lambda e: e.memset(Hf[:], 0.0), (), [Hf])
                    P.op("pool", lambda e: e.memset(Hb[:], 0.0), (), [Hb])
                else:
                    joint = (c % seg_t == 0) if d == 0 else (c % seg_t == seg_t - 1)
                    if joint:
                        P.ts("dve", Hf[:], Hf[:], flg[0:64, 0:1], None, ALU.mult, None, [Hf, flg], [Hf])
                        P.cp("act", Hb[:], Hf[:], [Hf], [Hb])
                for i, h in enumerate(hs):
                    P.mm(ps[:, 0:256], fm[:, h, 3, :], fm[:, h, 0:2, :].rearrange("p a b -> p (a b)"), True, True, [fm], [ps], sig=False)
                    P.mm(ps[:, 256:512], fm[:, h, 2, :], fm[:, h, 0:2, :].rearrange("p a b -> p (a b)"), True, True, [fm], [ps])
                    yield
                    P.tt("dve", AG[:, i, :, :].rearrange("p a b -> p (a b)"), ps[:], amask[:, d, :], ALU.mult, [ps, amask], [AG])
                for i, h in enumerate(hs):
                    P.mm(ps[:, i * 128:(i + 1) * 128], fm[:, h, 0, :], fm[:, h, 3, :], True, True, [fm], [ps], sig=(i == 3))
                yield
                curN = PNs[0]
                P.tt("dve", curN[:], ps[:].rearrange("p (a b) -> p a b", a=4),
                     nmask[:, d, :].unsqueeze(1).to_broadcast([128, 4, 128]), ALU.mult, [ps, nmask], [curN])
                P.tt("pool", ZX[:], AG[:, :, 0, :], identb[:].unsqueeze(1).to_broadcast([128, 4, 128]), ALU.add, [AG, identb], [ZX])
                yield
                for i, h in enumerate(hs):
                    P.mm(ps[:, i * 64:(i + 1) * 64], fm[:, h, 0, :], Hb[:, i, :], True, False, [fm, Hb], [ps], sig=False)
                    P.mm(ps[:, i * 64:(i + 1) * 64], AG[:, i, 2, :], Vb[:, h * 64:(h + 1) * 64], False, True, [AG, Vb], [ps], sig=False)
                for i, h in enumerate(hs):
                    P.mm(ps[:, 256 + i * 64:256 + (i + 1) * 64], fm[:, h, 1, :], Hb[:, i, :], True, False, [fm, Hb], [ps], sig=False)
                    P.mm(ps[:, 256 + i * 64:256 + (i + 1) * 64], AG[:, i, 3, :], Vb[:, h * 64:(h + 1) * 64], False, True, [AG, Vb], [ps], sig=(i == 3))
                yield
                P.cp("act", RHS[:].rearrange("p a b -> p (a b)"), ps[:, 0:256], [ps], [RHS])
                P.cp("act", ytile[:, hg * 256:(hg + 1) * 256], ps[:, 256:512], [ps], [yb])
                yield
                for i, h in enumerate(hs):
                    P.mm(ps[0:64, i * 64:(i + 1) * 64], KH[:, h * 64:(h + 1) * 64], Vb[:, h * 64:(h + 1) * 64], True, True, [KH, Vb], [ps], sig=(i == 3))
                P.tt("pool", Hf[:], Hf[:], eLC[:, hg * 4:(hg + 1) * 4].unsqueeze(2).to_broadcast([64, 4, 64]), ALU.mult, [Hf, eLC], [Hf])
                yield
                P.tt("dve", Hf[:], Hf[:], ps[0:64, 0:256].rearrange("p (a b) -> p a b", a=4), ALU.add, [Hf, ps], [Hf])
                yield
                curX_ap = lambda i: AG[:, i, 0, :]
                curX_buf = AG
                for lvl in range(1, 7):
                    both = lvl < 6
                    need_side = "N" if (lvl % 2 == 1) else "X"
                    newN = PNs[lvl % 2]
                    newX = PXs[lvl % 2]
                    doX = both or need_side == "X"
                    doN = both or need_side == "N"
                    if doX:
                        for i in range(4):
                            P.mm(ps[:, i * 128:(i + 1) * 128], curN[:, i, :], curX_ap(i), True, True, [curN, curX_buf], [ps], sig=(i == 3))
                        yield
                        P.cp("act", newX[:].rearrange("p a b -> p (a b)"), ps[:], [ps], [newX])
                    if doN:
                        for i in range(4):
                            P.mm(ps[:, i * 128:(i + 1) * 128], curX_ap(i), curN[:, i, :], True, True, [curN, curX_buf], [ps], sig=(i == 3))
                        yield
                        P.cp("dve", newN[:].rearrange("p a b -> p (a b)"), ps[:], [ps], [newN])
                    curN = newN
                    curX_buf = newX
                    curX_ap = (lambda nx: (lambda i: nx[:, i, :]))(newX)
                    if need_side == "N":
                        for i in range(4):
                            P.mm(ps[:, i * 128:(i + 1) * 128], ZX[:, i, :], newN[:, i, :], True, False, [ZX, newN], [ps], sig=False)
                            P.mm(ps[:, i * 128:(i + 1) * 128], ZX[:, i, :], identb[:], False, True, [ZX, identb], [ps], sig=(i == 3))
                        yield
                        P.cp("act", ZN[:].rearrange("p a b -> p (a b)"), ps[:], [ps], [ZN])
                    else:
                        for i in range(4):
                            P.mm(ps[:, i * 128:(i + 1) * 128], ZN[:, i, :], newX[:, i, :], True, False, [ZN, newX], [ps], sig=False)
                            P.mm(ps[:, i * 128:(i + 1) * 128], ZN[:, i, :], identb[:], False, True, [ZN, identb], [ps], sig=(i == 3))
                        yield
                        P.cp("dve", ZX[:].rearrange("p a b -> p (a b)"), ps[:], [ps], [ZX])
                for i in range(4):
                    P.mm(ps[:, i * 64:(i + 1) * 64], ZX[:, i, :], RHS[:, i, :], True, True, [ZX, RHS], [ps], sig=(i == 3))
                yield
                P.cp("dve", U[:].rearrange("p a b -> p (a b)"), ps[:, 0:256], [ps], [U])
                yield
                for i, h in enumerate(hs):
                    P.mm(ps[:, i * 64:(i + 1) * 64], AG[:, i, 1, :], U[:, i, :], True, True, [AG, U], [ps], sig=False)
                for i, h in enumerate(hs):
                    P.mm(ps[0:64, 256 + i * 64:256 + (i + 1) * 64], BH[:, h * 64:(h + 1) * 64], U[:, i, :], True, True, [BH, U], [ps], sig=(i == 3))
                yield
                P.tt("dve", ytile[:, hg * 256:(hg + 1) * 256], ytile[:, hg * 256:(hg + 1) * 256], ps[:, 0:256], ALU.add, [ps, yb], [yb])
                P.tt("dve", Hf[:], Hf[:], ps[0:64, 256:512].rearrange("p (a b) -> p a b", a=4), ALU.add, [Hf, ps], [Hf])
                yield
                P.cp("act", Hb[:], Hf[:], [Hf], [Hb])

            def combine(d, k):
                X = SD[d]
                c = tile_of(d, k)
                par = k % 2
                y = X["y"][par]
                ybs = X["yh"][par]
                if not is_second(d, c):
                    P.dma("act", scr_y[c], y[:], ybs, [scrY[c]], sem=X["sq"])
                    return
                yo, oc, s8, s8b, yrw = X["yo"], X["oc"], X["s8c"], X["s8d"], X["yrw"]
                g_, bv = X["g"][par], X["bv"][par]
                P.dma("sp", yo[:], scr_y[c], [scrY[c]], [yo], sem=X["yq"])
                yield
                P.tt("dve", yo[:], yo[:], y[:], ALU.add, [yo] + ybs, [yo])
                yield
                y3 = yo[:].rearrange("p (a b) -> p a b", a=8)
                oc3 = oc[:].rearrange("p (a b) -> p a b", a=8)
                P.red("dve", s8[:], y3, [yo], [s8])
                yield
                P.ts("dve", s8[:], s8[:], 1.0 / 64, None, ALU.mult, None, [s8], [s8])
                yield
                P.tt("dve", oc3, y3, s8[:].unsqueeze(2).to_broadcast([128, 8, 64]), ALU.subtract, [yo, s8], [oc])
                yield
                P.act(yo[:], oc[:], AF.Square, [oc], [yo])
                yield
                P.red("dve", s8b[:], yo[:].rearrange("p (a b) -> p a b", a=8), [yo], [s8b])
                yield
                P.ts("dve", s8b[:], s8b[:], 1.0 / 64, GN_EPS, ALU.mult, ALU.add, [s8b], [s8b])
                yield
                P.act(s8b[:], s8b[:], AF.Ln, [s8b], [s8b])
                yield
                P.act(s8b[:], s8b[:], AF.Exp, [s8b], [s8b], scale=-0.5)
                yield
                P.tt("dve", oc3, oc3, s8b[:].unsqueeze(2).to_broadcast([128, 8, 64]), ALU.mult, [oc, s8b], [oc])
                yield
                P.tt("pool", oc[:], oc[:], vb[:, LNW, :], ALU.mult, [oc, vb], [oc])
                yield
                P.tt("pool", oc[:], oc[:], vb[:, LNB, :], ALU.add, [oc, vb], [oc])
                yield
                P.tt("pool", oc[:], oc[:], bv[:], ALU.add, [oc, bv], [oc])
                yield
                P.tt("pool", yrw[:], oc[:], g_[:], ALU.mult, [oc, g_], [yrw])
                P.dma("act", scr_yrw[c], yrw[:], [yrw], (), sem=X["rq"])

            def prep_commit(d, k):
                yield from prepA(d, k)
                yield from commit(d, k)

            rr([prep_commit(0, 0), prep_commit(1, 0)])
            for k in range(nt):
                streams = []
                if k + 1 < nt:
                    streams += [prep_commit(0, k + 1), prep_commit(1, k + 1)]
                streams += [invchain(d, hg, k) for d in range(2) for hg in range(2)]
                if k >= 1:
                    streams += [combine(0, k - 1), combine(1, k - 1)]
                rr(streams)
            rr([combine(0, nt - 1), combine(1, nt - 1)])
            P.barrier()
            P.emit()

        bts = boundary_tiles(nt, seg_t)
        with contextlib.ExitStack() as st:
            alloc_stages(st, 4)
            gv = alloc_gv(st, (0,))
            vb, vb2, cum, amask, nmask = small_consts(st, True, rw=False)
            wna = P.sb([128, 8, 1536], BF16, "wna", st)
            load_w(wna, None, w_in, 8, 1536, col0=0)
            tabG = P.sb([128, 8, 16, 64], BF16, "tabG", st)
            tabT = P.sb([128, 8, 10, 64], BF16, "tabT", st)
            valid = P.sb([128, nb_t, 7, 2], F32, "valid", st)
            P.dma("sp", valid[:], c_val, (), [valid], sem=cq)
            with contextlib.ExitStack() as st_t:
                gm = P.sb([128, 16, 64], F32, "gm", st_t)
                tmk = P.sb([128, 10, 64], F32, "tmk", st_t)
                P.dma("sp", gm[:], c_gm, (), [gm], sem=cq)
                P.dma("sp", tmk[:], c_tm, (), [tmk], sem=cq)
                for h in range(8):
                    sgb = stages[h % 2]
                    P.dma("sp", sgb[:, 0:1024].rearrange("p (a b) -> p a b", a=16), rbG[:, h, :, :], (), [sgb], sem=wq)
                    P.act(sgb[:, 0:1024], sgb[:, 0:1024], AF.Exp, [sgb], [sgb])
                    P.tt("dve", tabG[:, h, :, :], sgb[:, 0:1024].rearrange("p (a b) -> p a b", a=16), gm[:], ALU.mult, [sgb, gm], [tabG])
                    P.dma("sp", sgb[:, 1024:1664].rearrange("p (a b) -> p a b", a=10), rbT[:, h, :, :], (), [sgb], sem=wq)
                    P.act(sgb[:, 1024:1664], sgb[:, 1024:1664], AF.Exp, [sgb], [sgb])
                    P.tt("dve", tabT[:, h, :, :], sgb[:, 1024:1664].rearrange("p (a b) -> p a b", a=10), tmk[:], ALU.mult, [sgb, tmk], [tabT])
                P.barrier()
                P.emit()
            xt = [P.sb([128, D], F32, "xt%d" % i, st) for i in range(2)]
            ss = P.sb([128, 1], F32, "ss", st)
            nb = P.sb([128, D], BF16, "nb", st)
            nT = P.sb([128, 8, 128], BF16, "nT", st)
            NR = 5
            qTr = [P.sb([64, 8, 128], BF16, "qTr%d" % i, st) for i in range(NR)]
            KR = 8
            kTr = [P.sb([64, 8, 128], BF16, "kTr%d" % i, st) for i in range(KR)]
            v1r = [P.sb([128, 8, 65], BF16, "v1r%d" % i, st) for i in range(KR)]
            zna = P.sb([128, 1536], F32, "zna", st)
            tmp2 = P.sb([128, 512], F32, "tmp2", st)
            qk = P.sb([128, 2, 512], BF16, "qk", st)
            s16 = P.sb([128, 16], F32, "s16", st)
            ESs = [P.sb([128, 896], F32, "ES%d" % i, st) for i in range(2)]
            PTs = [P.sb([128, 896], BF16, "PT%d" % i, st) for i in range(2)]
            yas = [P.sb([128, 512], F32, "ya%d" % i, st) for i in range(2)]
            rden = P.sb([128, 8], F32, "rden", st)
            pTb = P.ps([128, 1024], BF16, "pTb", st)
            pss = [P.ps([128, 512], F32, "ps%d" % i, st) for i in range(7)]
            yaq = [newsem() for _ in range(2)]
            for r_ in v1r:
                P.op("pool", lambda e, r_=r_: e.memset(r_[:], 1.0), (), [r_])

            znab = [zna, P.sb([128, 1536], F32, "znab", st)]
            pTx = Buf(pTb.t, "pTx")
            pTq = Buf(pTb.t, "pTq")

            def xz(c):
                x_ = xt[c % 2]
                zc = znab[c % 2]
                P.dma("sp", x_[:], xs[c * 128:(c + 1) * 128, :], (), [x_])
                yield
                P.act(nb[:], x_[:], AF.Square, [x_], [nb, ss], accum=ss[:])
                yield
                P.ts("dve", ss[:], ss[:], 1.0 / D, EPS, ALU.mult, ALU.add, [ss], [ss])
                yield
                P.act(ss[:], ss[:], AF.Ln, [ss], [ss])
                yield
                P.act(ss[:], ss[:], AF.Exp, [ss], [ss], scale=-0.5)
                yield
                P.stt("dve", nb[:], x_[:], ss[:], gv[:, 0, :], ALU.mult, ALU.mult, [x_, ss, gv], [nb])
                yield
                for r in range(2):
                    for j in range(4):
                        k = r * 4 + j
                        P.tr(pTb[:, j * 128:(j + 1) * 128], nb[:, k * 128:(k + 1) * 128], identb[:], [nb, identb], [pTx], sig=(j == 3))
                    yield
                    P.cp("act", nT[:, r * 4:(r + 1) * 4, :].rearrange("p a b -> p (a b)"), pTb[:, 0:512], [pTx], [nT])
                    yield
                for g0 in range(3):
                    pz = pss[4]
                    for k in range(8):
                        P.mm(pz[:], nT[:, k, :], wna[:, k, g0 * 512:(g0 + 1) * 512], k == 0, k == 7, [nT, wna], [pz])
                    yield
                    P.cp("act", zc[:, g0 * 512:(g0 + 1) * 512], pz[:], [pz], [zc])
                    yield

            def qk_(c):
                zc = znab[c % 2]
                P.act(tmp2[:], zc[:, 0:512], AF.Square, [zc], [tmp2])
                yield
                P.red("dve", s16[:, 0:8], tmp2[:].rearrange("p (a b) -> p a b", a=8), [tmp2], [s16])
                yield
                P.act(tmp2[:], zc[:, 512:1024], AF.Square, [zc], [tmp2])
                yield
                P.red("dve", s16[:, 8:16], tmp2[:].rearrange("p (a b) -> p a b", a=8), [tmp2], [s16])
                yield
                P.ts("dve", s16[:], s16[:], 1.0 / 64, EPS, ALU.mult, ALU.add, [s16], [s16])
                yield
                P.act(s16[:], s16[:], AF.Ln, [s16], [s16])
                yield
                P.act(s16[:], s16[:], AF.Exp, [s16], [s16], scale=-0.5)
                yield
                for w_ in range(2):
                    z3 = zc[:, w_ * 512:(w_ + 1) * 512].rearrange("p (a b) -> p a b", a=8)
                    t3 = tmp2[:].rearrange("p (a b) -> p a b", a=8)
                    P.tt("dve", t3, z3, s16[:, w_ * 8:(w_ + 1) * 8].unsqueeze(2).to_broadcast([128, 8, 64]), ALU.mult, [zc, s16], [tmp2])
                    yield
                    gsrc = vb[:, QG, :] if w_ == 0 else vb2[:]
                    P.tt("pool", qk[:, w_, :], tmp2[:], gsrc, ALU.mult, [tmp2, vb, vb2], [qk])
                    yield
                v1 = v1r[c % KR]
                P.cp("act", v1[:, :, 0:64], zc[:, 1024:1536].rearrange("p (a b) -> p a b", a=8), [zc], [v1])
                qT = qTr[c % NR]
                kT = kTr[c % KR]
                for w_, dst in ((0, qT), (1, kT)):
                    for r in range(2):
                        for j in range(4):
                            h = r * 4 + j
                            P.tr(pTb[0:64, 512 + j * 128:512 + (j + 1) * 128], qk[:, w_, h * 64:(h + 1) * 64], identb[:], [qk, identb], [pTq], sig=(j == 3))
                        yield
                        P.cp("dve" if (w_ + r) % 2 else "act", dst[:, r * 4:(r + 1) * 4, :].rearrange("p a b -> p (a b)"), pTb[0:64, 512:1024], [pTq], [dst])
                        yield

            yaT = [[Buf(yas[i].t, "yaT") for _ in range(2)] for i in range(2)]
            rdens = [P.sb([128, 4], F32, "rden%d" % i, st) for i in range(2)]

            def attn_half(i, hh):
                js, kind, idx0 = na_slots(i, nt, seg_t)
                slots = [(s, j) for s, j in enumerate(js) if 0 <= j < nt]
                qT = qTr[i % NR]
                po = pss[5 + hh]
                ya = yas[i % 2]
                yat = yaT[i % 2][hh]
                ES, PT = ESs[hh], PTs[hh]
                pa_, pb_ = pss[hh * 2], pss[hh * 2 + 1]
                rd = rdens[hh]
                s_lo = slots[0][0]
                s_hi = slots[-1][0]
                n_ = s_hi - s_lo + 1
                for hl in range(4):
                    h = hh * 4 + hl
                    for s, j in slots:
                        bank = pa_ if s < 4 else pb_
                        P.mm(bank[:, (s % 4) * 128:(s % 4 + 1) * 128], kTr[j % KR][:, h, :], qT[:, h, :], True, True,
                             [kTr[j % KR], qT], [bank], sig=True)
                    yield
                    a1_ = min(s_hi, 3)
                    if s_lo <= 3:
                        P.act(ES[:, s_lo * 128:(a1_ + 1) * 128], pa_[:, s_lo * 128:(a1_ + 1) * 128], AF.Exp, [pa_], [ES], scale=0.125)
                    if s_hi >= 4:
                        b0_ = max(s_lo, 4)
                        P.act(ES[:, b0_ * 128:(s_hi + 1) * 128], pb_[:, (b0_ - 4) * 128:(s_hi - 3) * 128], AF.Exp, [pb_], [ES], scale=0.125)
                    yield
                    es4 = ES[:, s_lo * 128:(s_hi + 1) * 128].rearrange("p (a b) -> p a b", b=64)
                    pt4 = PT[:, s_lo * 128:(s_hi + 1) * 128].rearrange("p (a b) -> p a b", b=64)
                    if kind == "T":
                        P.tt("dve", pt4, es4, tabT[:, h, 2 * s_lo:2 * s_hi + 2, :], ALU.mult, [ES, tabT], [PT])
                    else:
                        P.tt("dve", es4, es4, tabG[:, h, 2 * s_lo + 1:2 * s_hi + 3, :], ALU.mult, [ES, tabG], [ES])
                        yield
                        bi = bts.index(i)
                        vv = valid[:, bi, s_lo:s_hi + 1, :].rearrange("p a b -> p (a b)").unsqueeze(2).to_broadcast([128, 2 * n_, 64])
                        P.tt("pool", pt4, es4, vv, ALU.mult, [ES, valid], [PT])
                    yield
                    for s, j in slots:
                        P.mm(po[:, hl * 65:(hl + 1) * 65], PT[:, s * 128:(s + 1) * 128], v1r[j % KR][:, h, :],
                             s == s_lo, s == s_hi, [PT, v1r[j % KR]], [po], sig=(s == s_hi))
                yield
                po3 = po[:, 0:260].rearrange("p (a b) -> p a b", a=4)
                P.rcp(rd[:], po3[:, :, 64], [po], [rd])
                yield
                P.tt("dve", ya[:, hh * 256:(hh + 1) * 256].rearrange("p (a b) -> p a b", a=4), po3[:, :, 0:64],
                     rd[:].unsqueeze(2).to_broadcast([128, 4, 64]), ALU.mult, [po, rd], [yat])

            def attn_store(i):
                P.dma("act", scr_ya[i], yas[i % 2][:], yaT[i % 2], ())

            LAG = 4
            rr([xz(0)])
            for it in range(nt + LAG):
                streams = [xz(it + 1) if it + 1 < nt else None, qk_(it) if it < nt else None]
                if it - LAG >= 0:
                    streams += [attn_half(it - LAG, 0), attn_half(it - LAG, 1)]
                rr(streams)
                if it - LAG >= 0:
                    attn_store(it - LAG)
            P.barrier()
            P.emit()

        with contextlib.ExitStack() as st:
            alloc_stages(st, 2)
            gv = alloc_gv(st, (0,))
            wgt = P.sb([128, 8, 2048], BF16, "wgt", st)
            wao = P.sb([128, 4, D], BF16, "wao", st)
            wbo = P.sb([128, 4, D], BF16, "wbo", st)
            wo = P.sb([128, 8, D], BF16, "wo", st)
            load_w(wgt, None, w_in, 8, 2048, col0=3392)
            load_w(wao, None, w_a_out, 4, D)
            load_w(wbo, None, w_b_out, 4, D)
            load_w(wo, None, w_o, 8, D)
            NP = 3
            xt = [P.sb([128, D], F32, "xt%d" % i, st) for i in range(NP)]
            yl = [P.sb([128, 2, 512], F32, "yl%d" % i, st) for i in range(NP)]
            ssC = [P.sb([128, 1], F32, "ss%d" % i, st) for i in range(NP)]
            nbC = [P.sb([128, D], BF16, "nb%d" % i, st) for i in range(NP)]
            nTC = [P.sb([128, 8, 128], BF16, "nT%d" % i, st) for i in range(NP)]
            ylb = [P.sb([128, 2, 512], BF16, "ylb%d" % i, st) for i in range(NP)]
            ylT = [P.sb([128, 8, 128], BF16, "ylT%d" % i, st) for i in range(NP)]
            gtsC = [P.sb([128, 2048], F32, "gts%d" % i, st) for i in range(NP)]
            mrgC = [P.sb([128, D], F32, "mrg%d" % i, st) for i in range(NP)]
            mrbC = [P.sb([128, D], BF16, "mrb%d" % i, st) for i in range(NP)]
            mTC = [P.sb([128, 8, 128], BF16, "mT%d" % i, st) for i in range(NP)]
            hhC = [P.sb([128, D], F32, "hh%d" % i, st) for i in range(NP)]
            pTs = P.ps([128, 1024], BF16, "pTC", st)
            pTC = [pTs] * NP
            pzC = [[P.ps([128, 512], F32, "pzC%d_%d" % (i, j), st) for j in range(3 if i == 0 else 2)] for i in range(NP)]
            xqC = [newsem() for _ in range(NP)]
            yqC = [newsem() for _ in range(NP)]
            hqC = [newsem() for _ in range(NP)]

            def tileC(i):
                p = i % NP
                x_, yl_, ss, nb, nT, pT = xt[p], yl[p], ssC[p], nbC[p], nTC[p], pTC[p]
                gts, mrg, mrb, mT, h_ = gtsC[p], mrgC[p], mrbC[p], mTC[p], hhC[p]
                pzs = pzC[p]
                cnt = [0]

                def pz_():
                    cnt[0] += 1
                    return pzs[cnt[0] % len(pzs)]
                P.dma("sp", x_[:], xs[i * 128:(i + 1) * 128, :], (), [x_], sem=xqC[p])
                P.dma("sp", yl_[:, 0, :], scr_ya[i], (), [yl_], sem=yqC[p])
                P.dma("sp", yl_[:, 1, :], scr_yrw[i], (), [yl_], sem=yqC[p])
                yield
                P.act(nb[:], x_[:], AF.Square, [x_], [nb, ss], accum=ss[:])
                yield
                P.ts("dve", ss[:], ss[:], 1.0 / D, EPS, ALU.mult, ALU.add, [ss], [ss])
                yield
                P.act(ss[:], ss[:], AF.Ln, [ss], [ss])
                yield
                P.act(ss[:], ss[:], AF.Exp, [ss], [ss], scale=-0.5)
                yield
                P.stt("dve", nb[:], x_[:], ss[:], gv[:, 0, :], ALU.mult, ALU.mult, [x_, ss, gv], [nb])
                P.cp("pool", ylb[p][:], yl_[:], [yl_], [ylb[p]])
                yield
                for k in range(8):
                    P.tr(pT[:, k * 128:(k + 1) * 128], nb[:, k * 128:(k + 1) * 128], identb[:], [nb, identb], [pT], sig=(k == 7))
                P.cp("act", nT[:].rearrange("p a b -> p (a b)"), pT[:], [pT], [nT])
                yield
                for k in range(8):
                    P.tr(pT[:, k * 128:(k + 1) * 128], ylb[p][:, k // 4, (k % 4) * 128:(k % 4 + 1) * 128], identb[:], [ylb[p], identb], [pT], sig=(k == 7))
                P.cp("dve", ylT[p][:].rearrange("p a b -> p (a b)"), pT[:], [pT], [ylT[p]])
                for g0 in range(4):
                    pz = pz_()
                    for k in range(8):
                        P.mm(pz[:], nT[:, k, :], wgt[:, k, g0 * 512:(g0 + 1) * 512], k == 0, k == 7, [nT, wgt], [pz])
                    yield
                    P.act(gts[:, g0 * 512:(g0 + 1) * 512], pz[:], AF.Sigmoid, [pz], [gts])
                for g0 in range(2):
                    pz = pz_()
                    for k in range(4):
                        P.mm(pz[:], ylT[p][:, k, :], wao[:, k, g0 * 512:(g0 + 1) * 512], k == 0, k == 3, [ylT[p], wao], [pz])
                    yield
                    P.tt("dve", mrg[:, g0 * 512:(g0 + 1) * 512], pz[:], gts[:, g0 * 512:(g0 + 1) * 512], ALU.mult, [pz, gts], [mrg])
                    pz2 = pz_()
                    for k in range(4):
                        P.mm(pz2[:], ylT[p][:, 4 + k, :], wbo[:, k, g0 * 512:(g0 + 1) * 512], k == 0, k == 3, [ylT[p], wbo], [pz2])
                    yield
                    P.tt("dve", gts[:, 1024 + g0 * 512:1024 + (g0 + 1) * 512], pz2[:], gts[:, 1024 + g0 * 512:1024 + (g0 + 1) * 512], ALU.mult, [pz2, gts], [gts])
                    yield
                    P.tt("pool", mrb[:, g0 * 512:(g0 + 1) * 512], mrg[:, g0 * 512:(g0 + 1) * 512], gts[:, 1024 + g0 * 512:1024 + (g0 + 1) * 512], ALU.add, [mrg, gts], [mrb])
                yield
                for k in range(8):
                    P.tr(pT[:, k * 128:(k + 1) * 128], mrb[:, k * 128:(k + 1) * 128], identb[:], [mrb, identb], [pT], sig=(k == 7))
                P.cp("act", mT[:].rearrange("p a b -> p (a b)"), pT[:], [pT], [mT])
                for g0 in range(2):
                    pz = pz_()
                    for k in range(8):
                        P.mm(pz[:], mT[:, k, :], wo[:, k, g0 * 512:(g0 + 1) * 512], k == 0, k == 7, [mT, wo], [pz])
                    yield
                    P.tt("dve", h_[:, g0 * 512:(g0 + 1) * 512], pz[:], x_[:, g0 * 512:(g0 + 1) * 512], ALU.add, [pz, x_], [h_])
                P.dma("act", scr_h[i], h_[:], [h_], (), sem=hqC[p])

            rr([chain_gens(tileC, range(0, nt, 3)), delayed_start(chain_gens(tileC, range(1, nt, 3)), 10),
                delayed_start(chain_gens(tileC, range(2, nt, 3)), 20)])
            P.barrier()
            P.emit()

        with contextlib.ExitStack() as st:
            gv = alloc_gv(st, (1, 2))
            wf1 = P.sb([128, 8, 4096], BF16, "wf1", st)
            wf2 = P.sb([128, 32, D], BF16, "wf2", st)
            wpg = P.sb([128, 8, D], BF16, "wpg", st)
            wpl = P.sb([128, 2, D], BF16, "wpl", st)
            with contextlib.ExitStack() as st_w:
                alloc_stages(st_w, 6)
                load_w(wf1, None, w_ff1, 8, 4096)
                load_w(wf2, None, w_ff2, 32, D)
                load_w(wpg, None, w_pgate, 8, D)
                load_w(wpl, None, w_ple, 2, D)
                P.barrier()
                P.emit()
            ht = [P.sb([128, D], F32, "ht%d" % i, st) for i in range(2)]
            pt_ = [P.sb([128, 256], F32, "pt%d" % i, st) for i in range(2)]
            ss3 = [P.sb([128, 1], F32, "ss%d" % i, st) for i in range(2)]
            nb3 = [P.sb([128, D], BF16, "nb%d" % i, st) for i in range(2)]
            nT3 = [P.sb([128, 8, 128], BF16, "nT%d" % i, st) for i in range(2)]
            hT = [P.sb([128, 32, 128], BF16, "hT%d" % i, st) for i in range(2)]
            rl3 = [P.sb([128, 512], F32, "rl%d" % i, st) for i in range(2)]
            pb3 = [P.sb([128, 256], BF16, "pb%d" % i, st) for i in range(2)]
            pT2 = [P.sb([128, 2, 128], BF16, "pT2%d" % i, st) for i in range(2)]
            gt3 = [P.sb([128, D], F32, "gt%d" % i, st) for i in range(2)]
            pT3 = [P.ps([128, 1024], BF16, "pT3%d" % i, st) for i in range(2)]
            pz3 = [[P.ps([128, 512], F32, "pz3%d_%d" % (i, j), st) for j in range(3)] for i in range(2)]
            hq3 = [newsem() for _ in range(2)]
            pq3 = [newsem() for _ in range(2)]
            oq3 = [newsem() for _ in range(2)]

            def norm3(p, src, grow):
                ss, nb, nT, pT = ss3[p], nb3[p], nT3[p], pT3[p]
                P.act(nb[:], src[:], AF.Square, [src], [nb, ss], accum=ss[:])
                yield
                P.ts("dve", ss[:], ss[:], 1.0 / D, EPS, ALU.mult, ALU.add, [ss], [ss])
                yield
                P.act(ss[:], ss[:], AF.Ln, [ss], [ss])
                yield
                P.act(ss[:], ss[:], AF.Exp, [ss], [ss], scale=-0.5)
                yield
                P.stt("dve", nb[:], src[:], ss[:], gv[:, grow, :], ALU.mult, ALU.mult, [src, ss, gv], [nb])
                yield
                for k in range(8):
                    P.tr(pT[:, k * 128:(k + 1) * 128], nb[:, k * 128:(k + 1) * 128], identb[:], [nb, identb], [pT], sig=(k == 7))
                yield
                P.cp("act", nT[:].rearrange("p a b -> p (a b)"), pT[:], [pT], [nT])
                yield

            def tile3(c):
                p = c % 2
                h_, p_, nT, pT, rl, gt_ = ht[p], pt_[p], nT3[p], pT3[p], rl3[p], gt3[p]
                pzs = pz3[p]
                cnt = [0]

                def pz_():
                    cnt[0] += 1
                    return pzs[cnt[0] % 3]
                P.dma("sp", h_[:], scr_h[c], (), [h_], sem=hq3[p])
                P.dma("sp", p_[:], pp[c * 128:(c + 1) * 128, :], (), [p_], sem=pq3[p])
                yield
                yield from norm3(p, h_, 0)
                for f4 in range(8):
                    pz = pz_()
                    for f in range(4):
                        fc = f4 * 4 + f
                        for k in range(8):
                            P.mm(pz[:, f * 128:(f + 1) * 128], wf1[:, k, fc * 128:(fc + 1) * 128], nT[:, k, :], k == 0, k == 7,
                                 [wf1, nT], [pz], sig=(k == 7 and f == 3))
                    yield
                    P.act(rl[:], pz[:], AF.Relu, [pz], [rl])
                    yield
                    P.tt("dve" if f4 % 2 else "pool", hT[p][:, f4 * 4:(f4 + 1) * 4, :].rearrange("p a b -> p (a b)"), rl[:], rl[:], ALU.mult, [rl], [hT[p]])
                for g0 in range(2):
                    pz = pz_()
                    for k in range(32):
                        P.mm(pz[:], hT[p][:, k, :], wf2[:, k, g0 * 512:(g0 + 1) * 512], k == 0, k == 31, [hT[p], wf2], [pz])
                    yield
                    P.tt("dve", h_[:, g0 * 512:(g0 + 1) * 512], pz[:], h_[:, g0 * 512:(g0 + 1) * 512], ALU.add, [pz, h_], [h_])
                yield
                yield from norm3(p, h_, 1)
                P.cp("pool", pb3[p][:], p_[:], [p_], [pb3[p]])
                yield
                for k in range(2):
                    P.tr(pT[:, k * 128:(k + 1) * 128], pb3[p][:, k * 128:(k + 1) * 128], identb[:], [pb3[p], identb], [pT], sig=(k == 1))
                yield
                P.cp("act", pT2[p][:].rearrange("p a b -> p (a b)"), pT[:, 0:256], [pT], [pT2[p]])
                for g0 in range(2):
                    pz = pz_()
                    for k in range(8):
                        P.mm(pz[:], nT[:, k, :], wpg[:, k, g0 * 512:(g0 + 1) * 512], k == 0, k == 7, [nT, wpg], [pz])
                    yield
                    P.act(gt_[:, g0 * 512:(g0 + 1) * 512], pz[:], AF.Sigmoid, [pz], [gt_])
                    pz2 = pz_()
                    for k in range(2):
                        P.mm(pz2[:], pT2[p][:, k, :], wpl[:, k, g0 * 512:(g0 + 1) * 512], k == 0, k == 1, [pT2[p], wpl], [pz2])
                    yield
                    P.tt("dve", gt_[:, g0 * 512:(g0 + 1) * 512], pz2[:], gt_[:, g0 * 512:(g0 + 1) * 512], ALU.mult, [pz2, gt_], [gt_])
                    yield
                    P.tt("pool", gt_[:, g0 * 512:(g0 + 1) * 512], gt_[:, g0 * 512:(g0 + 1) * 512], h_[:, g0 * 512:(g0 + 1) * 512], ALU.add, [gt_, h_], [gt_])
                P.dma("act", yout[c * 128:(c + 1) * 128, :], gt_[:], [gt_], (), sem=oq3[p])

            rr([chain_gens(tile3, range(0, nt, 2)), delayed_start(chain_gens(tile3, range(1, nt, 2)), 24)])
            P.barrier()
            P.emit()
    return nc


NT_FULL = 64
SEG_T_FULL = 16
_CACHE = {}


def make_in_maps(super_x, super_p, prompt_flags, W, nt, seg_t):
    rbG, rbT = expand_rel_bias(np.asarray(W["rel_bias"][0], np.float32))
    t8 = lambda v: np.tile(np.asarray(v, np.float32).reshape(-1), 8)
    vec512 = np.stack([W["w0_f"][0], W["w0_b"][0], W["a0"][0], W["k_k"][0], W["k_a"][0],
                       np.asarray(W["r_k"][0]).reshape(-1), W["ln_x_w"][0], W["ln_x_b"][0], t8(W["q_gain"][0])]).astype(np.float32)
    vec512b = t8(W["k_gain"][0])[None, :].astype(np.float32)
    gvec = np.stack([W["g_mix"][0], W["g_ffn"][0], W["g_ple"][0]]).astype(np.float32)
    w_up = np.stack([W["w_up_f"][0], W["w_up_b"][0], W["a_up"][0]]).astype(np.float32)
    shared = dict(w_in=W["w_in"][0], conv_w=W["conv_w"][0], vec512=vec512, vec512b=vec512b, gvec=gvec, w_up=w_up,
                  g_up=W["g_up"][0], w_a_out=W["w_a_out"][0], w_b_out=W["w_b_out"][0], w_o=W["w_o"][0],
                  w_ff1=W["w_ff1"][0], w_ff2=W["w_ff2"][0], w_ple=W["w_ple"][0], w_pgate=W["w_pgate"][0],
                  rbG=rbG, rbT=rbT)
    shared = {k: np.ascontiguousarray(np.asarray(v, np.float32)) for k, v in shared.items()}
    consts = {True: host_consts(nt, seg_t, True), False: host_consts(nt, seg_t, False)}
    maps = []
    for x, p, pf in zip(super_x, super_p, prompt_flags):
        m = dict(shared)
        hc = consts[bool(pf)]
        m.update(xs=np.ascontiguousarray(x, np.float32), pp=np.ascontiguousarray(p, np.float32),
                 flag=np.full((128, 1), 1.0 if pf else 0.0, np.float32),
                 c_ident=hc["ident"], c_cum=hc["cum"], c_amask=hc["amask"], c_nmask=hc["nmask"],
                 c_gm=hc["gm"], c_tm=hc["tm"], c_val=hc["val"])
        maps.append(m)
    return maps


def kernel(**inputs):
    nt, seg_t = NT_FULL, SEG_T_FULL
    xp = np.asarray(inputs["x_prompt"], np.float32)
    xsm = np.asarray(inputs["x_sample"], np.float32)
    pq = np.asarray(inputs["p_prompt"], np.float32)[0]
    psm = np.asarray(inputs["p_sample"], np.float32)[0]
    sx = [xp[0], xp[1]] + [xsm[4 * i:4 * i + 4].reshape(8192, D) for i in range(4)]
    sp_ = [pq[0], pq[1]] + [psm[4 * i:4 * i + 4].reshape(8192, 256) for i in range(4)]
    fl = [True, True, False, False, False, False]
    sx += [sx[4], sx[5]]
    sp_ += [sp_[4], sp_[5]]
    fl += [False, False]
    W = {k: np.asarray(v) for k, v in inputs.items() if k not in ("x_prompt", "x_sample", "p_prompt", "p_sample")}
    maps = make_in_maps(sx, sp_, fl, W, nt, seg_t)
    if "nc" not in _CACHE:
        _CACHE["nc"] = build(nt, seg_t)
    res = run_bass_kernel_spmd(_CACHE["nc"], maps, core_ids=list(range(NCORE)))
    outs = [np.asarray(r["yout"], np.float32) for r in res.results]
    y_prompt = np.stack([outs[0], outs[1]])
    y_sample = np.concatenate([outs[2 + i].reshape(4, 2048, D) for i in range(4)], 0)
    return (y_prompt, y_sample)
```

```python
import contextlib
import numpy as np
import concourse.bass as bass
import concourse.mybir as mybir
from concourse.bass_utils import run_bass_kernel_spmd

F32 = mybir.dt.float32
BF16 = mybir.dt.bfloat16
ALU = mybir.AluOpType
AF = mybir.ActivationFunctionType
AX = mybir.AxisListType

D = 1024
DS = 0.606531
EPS = 1e-6
GN_EPS = 64e-5
RWC = 1856
NCORE = 8


class Buf:
    __slots__ = ("t", "lw", "rd", "name")

    def __init__(self, t, name=""):
        self.t = t
        self.lw = None
        self.rd = {}
        self.name = name

    def __getitem__(self, k):
        return self.t[k]


class Prog:
    ENG = ("pe", "act", "dve", "pool", "sp")

    def __init__(self, nc, stack):
        self.nc = nc
        self.stack = stack
        self.q = {e: [] for e in self.ENG}
        self.sems = {}
        self.cnt = {}
        self.seen = {e: {} for e in self.ENG}
        for e in self.ENG:
            self.sems[e] = stack.enter_context(nc.semaphore("s_" + e))
            self.cnt[e] = 0
        self.n_inst = 0
        self.rr = 0
        self.dma_pool = {}
        self.dma_rr = {}
        self.DMA_POOL_SIZE = {"sp": 32, "act": 16, "pool": 8}

    def sb(self, shape, dt, name, stack=None):
        self.uid = getattr(self, "uid", 0) + 1
        name = "%s_%d" % (name, self.uid)
        t = (stack or self.stack).enter_context(self.nc.sbuf_tensor(name, list(shape), dt))
        return Buf(t, name)

    def ps(self, shape, dt, name, stack=None):
        self.uid = getattr(self, "uid", 0) + 1
        name = "%s_%d" % (name, self.uid)
        t = (stack or self.stack).enter_context(self.nc.psum_tensor(name, list(shape), dt))
        return Buf(t, name)

    def dma_sem(self, name):
        s = self.stack.enter_context(self.nc.semaphore(name))
        self.sems[name] = s
        self.cnt[name] = 0
        return name

    def _waits(self, eng, reads, writes):
        need = {}
        for b in reads:
            if b.lw is not None:
                k, v = b.lw
                need[k] = max(need.get(k, 0), v)
        for b in writes:
            if b.lw is not None:
                k, v = b.lw
                need[k] = max(need.get(k, 0), v)
            for k, v in b.rd.items():
                need[k] = max(need.get(k, 0), v)
        out = []
        seen = self.seen[eng]
        for k, v in need.items():
            if k == eng and eng in ("pe", "sp"):
                continue
            if seen.get(k, 0) < v:
                seen[k] = v
                out.append((k, v))
        return out

    def op(self, eng, fn, reads=(), writes=(), sig=True):
        waits = self._waits(eng, reads, writes)
        if sig:
            self.cnt[eng] += 1
            val = self.cnt[eng]
        else:
            val = self.cnt[eng] + 1
        sem = self.sems[eng]
        sems = self.sems

        def run(e):
            for k, v in waits:
                e.wait_ge(sems[k], v)
            ins = fn(e)
            if sig:
                ins.then_inc(sem, 1)
        self.q[eng].append(run)
        self.n_inst += 1
        for b in reads:
            b.rd[eng] = max(b.rd.get(eng, 0), val)
        for b in writes:
            b.lw = (eng, val)
            b.rd = {}

    def dma(self, eng, out_ap, in_ap, reads=(), writes=(), sem=None):
        pool = self.dma_pool.setdefault(eng, [])
        if not pool:
            for i in range(self.DMA_POOL_SIZE.get(eng, 8)):
                pool.append(self.dma_sem("dp_%s_%d" % (eng, i)))
            self.dma_rr[eng] = 0
        sem = pool[self.dma_rr[eng] % len(pool)]
        self.dma_rr[eng] += 1
        waits = self._waits(eng, reads, writes)
        prev = self.cnt[sem]
        if prev > 0 and self.seen[eng].get(sem, 0) < prev:
            self.seen[eng][sem] = prev
            waits = [w for w in waits if w[0] != sem] + [(sem, prev)]
        self.cnt[sem] += 16
        val = self.cnt[sem]
        s = self.sems[sem]
        sems = self.sems

        def run(e):
            for k, v in waits:
                e.wait_ge(sems[k], v)
            e.dma_start(out=out_ap, in_=in_ap).then_inc(s, 16)
        self.q[eng].append(run)
        self.n_inst += 1
        for b in reads:
            b.rd[sem] = max(b.rd.get(sem, 0), val)
        for b in writes:
            b.lw = (sem, val)
            b.rd = {}

    def barrier(self):
        tot = dict(self.cnt)
        sems = self.sems
        for eng in self.ENG:
            waits = []
            for k, v in tot.items():
                if v > 0 and k != eng and self.seen[eng].get(k, 0) < v:
                    self.seen[eng][k] = v
                    waits.append((k, v))

            def run(e, waits=waits):
                for k, v in waits:
                    e.wait_ge(sems[k], v)
            self.q[eng].append(run)

    def emit(self):
        nc = self.nc
        q = self.q
        with nc.Block() as block:
            @block.tensor
            def _(e):
                for f in q["pe"]:
                    f(e)

            @block.scalar
            def _(e):
                for f in q["act"]:
                    f(e)

            @block.vector
            def _(e):
                for f in q["dve"]:
                    f(e)

            @block.gpsimd
            def _(e):
                for f in q["pool"]:
                    f(e)

            @block.sync
            def _(e):
                for f in q["sp"]:
                    f(e)
        self.q = {e: [] for e in self.ENG}

    def tt(self, eng, out, a, b, op, R, W):
        self.op(eng, lambda e: e.tensor_tensor(out, a, b, op), R, W)

    def ts(self, eng, out, a, s1, s2, op0, op1, R, W):
        if s2 is None:
            self.op(eng, lambda e: e.tensor_scalar(out, a, s1, None, op0), R, W)
        else:
            self.op(eng, lambda e: e.tensor_scalar(out, a, s1, s2, op0, op1), R, W)

    def stt(self, eng, out, a, s, b, op0, op1, R, W):
        self.op(eng, lambda e: e.scalar_tensor_tensor(out, a, s, b, op0, op1), R, W)

    def act(self, out, a, func, R, W, scale=1.0, accum=None):
        if accum is None:
            self.op("act", lambda e: e.activation(out, a, func, scale=scale), R, W)
        else:
            self.op("act", lambda e: e.activation(out, a, func, scale=scale, accum_out=accum), R, W)

    def cp(self, eng, out, a, R, W):
        if eng == "act":
            self.op("act", lambda e: e.activation(out, a, AF.Copy), R, W)
        else:
            self.op(eng, lambda e: e.tensor_copy(out, a), R, W)

    def red(self, eng, out, a, R, W):
        self.op(eng, lambda e: e.reduce_sum(out, a, AX.X), R, W)

    def rcp(self, out, a, R, W):
        self.op("dve", lambda e: e.reciprocal(out, a), R, W)

    def mm(self, out, lhsT, rhs, start, stop, R, W, sig=None):
        if sig is None:
            sig = stop
        self.op("pe", lambda e: e.matmul(out, lhsT, rhs, start=start, stop=stop), R, W, sig=sig)

    def tr(self, out, a, ident, R, W, sig=True):
        self.op("pe", lambda e: e.transpose(out, a, ident), R, W, sig=sig)


def na_slots(i, nt, seg_t):
    m = i % seg_t
    if 2 <= m <= seg_t - 3:
        js = [i + 2 - s for s in range(5)]
        return js, "T", -4
    js = [i + 3 - s for s in range(7)]
    return js, "G", -6


def boundary_tiles(nt, seg_t):
    return [i for i in range(nt) if not (2 <= i % seg_t <= seg_t - 3)]


def host_consts(nt, seg_t, prompt):
    j = np.arange(128)[:, None]
    t = np.arange(128)[None, :]
    lt = (j < t).astype(np.float32)
    le = (j <= t).astype(np.float32)
    gt = (j > t).astype(np.float32)
    ge = (j >= t).astype(np.float32)
    cum = np.stack([lt, le, gt, ge], 1)
    amask = np.stack([np.concatenate([-lt, -le, lt, le], 1),
                      np.concatenate([-gt, -ge, gt, ge], 1)], 1)
    nmask = np.stack([-gt, -lt], 1)
    kc = np.arange(64)[:, None]
    qc = np.arange(64)[None, :]
    cs = np.clip(qc - 8, 0, 48)
    colm = ((kc >= cs) & (kc < cs + 16)).astype(np.float32)
    par = (np.arange(128) // 64)
    gm = np.zeros((128, 16, 64), np.float32)
    tm = np.zeros((128, 10, 64), np.float32)
    for p in range(128):
        for a, idx in enumerate(range(-7, 9)):
            dr = par[p] - idx
            if -7 <= dr <= 7:
                gm[p, a] = colm[p % 64]
        for a, idx in enumerate(range(-4, 6)):
            dr = par[p] - idx
            if -4 <= dr <= 3:
                tm[p, a] = colm[p % 64]
    bts = boundary_tiles(nt, seg_t)
    val = np.zeros((128, len(bts), 7, 2), np.float32)
    seq_t = nt if prompt else seg_t
    rows = 2 * seq_t
    for bi, i in enumerate(bts):
        s0 = (i // seq_t) * seq_t
        for s in range(7):
            jt = i + 3 - s
            if jt < 0 or jt >= nt or jt // seq_t != i // seq_t:
                continue
            for pr in range(2):
                for qp in range(2):
                    r = 2 * (i - s0) + qp
                    kr_ = 2 * (jt - s0) + pr
                    rs = min(max(r - 4, 0), rows - 8)
                    if rs <= kr_ < rs + 8:
                        val[pr * 64:(pr + 1) * 64, bi, s, qp] = 1.0
    return dict(cum=cum, amask=amask, nmask=nmask, gm=gm, tm=tm, val=val,
                ident=np.eye(128, dtype=np.float32))


def expand_rel_bias(rb):
    par = (np.arange(128) // 64)[:, None, None]
    kc = (np.arange(128) % 64)[:, None, None]
    qc = np.arange(64)[None, None, :]
    dc = np.clip(kc - qc + 15, 0, 30)

    def tab(idxs):
        idx = np.array(list(idxs))[None, :, None]
        dr = np.clip(par - idx + 7, 0, 14)
        return np.ascontiguousarray(np.transpose(rb[:, dr, dc], (1, 0, 2, 3)))
    return tab(range(-7, 9)), tab(range(-4, 6))


def rr(gens):
    gens = [g for g in gens if g is not None]
    while gens:
        nxt = []
        for g in gens:
            try:
                next(g)
                nxt.append(g)
            except StopIteration:
                pass
        gens = nxt


def fast(gen, n):
    while True:
        for _ in range(n):
            try:
                next(gen)
            except StopIteration:
                return
        yield


def delayed_start(gen, n):
    for _ in range(n):
        yield
    yield from gen


def chain_gens(fn, items):
    for it_ in items:
        yield from fn(it_)


def build(nt, seg_t, dbg=False):
    S = nt * 128
    nc = bass.Bass("TRN2", target_bir_lowering=False)
    dr = lambda n, sh, kind="ExternalInput": nc.dram_tensor(n, list(sh), F32, kind=kind).ap()
    xs = dr("xs", [S, D])
    pp = dr("pp", [S, 256])
    flag = dr("flag", [128, 1])
    w_in = dr("w_in", [D, 5440])
    conv_w = dr("conv_w", [3, RWC])
    vec512 = dr("vec512", [9, 512])
    vec512b = dr("vec512b", [1, 512])
    gvec = dr("gvec", [3, D])
    w_up = dr("w_up", [3, 64, 512])
    g_up = dr("g_up", [128, 512])
    w_a_out = dr("w_a_out", [512, D])
    w_b_out = dr("w_b_out", [512, D])
    w_o = dr("w_o", [D, D])
    w_ff1 = dr("w_ff1", [D, 4096])
    w_ff2 = dr("w_ff2", [4096, D])
    w_ple = dr("w_ple", [256, D])
    w_pgate = dr("w_pgate", [D, D])
    nb_t = len(boundary_tiles(nt, seg_t))
    c_ident = dr("c_ident", [128, 128])
    c_cum = dr("c_cum", [128, 4, 128])
    c_amask = dr("c_amask", [128, 2, 512])
    c_nmask = dr("c_nmask", [128, 2, 128])
    c_gm = dr("c_gm", [128, 16, 64])
    c_tm = dr("c_tm", [128, 10, 64])
    c_val = dr("c_val", [128, nb_t, 7, 2])
    rbG = dr("rbG", [128, 8, 16, 64])
    rbT = dr("rbT", [128, 8, 10, 64])
    yout = dr("yout", [S, D], "ExternalOutput")
    scr_u = dr("scr_u", [nt, 128, RWC], "ExternalOutput" if dbg else "Internal")
    scr_y = dr("scr_y", [nt, 128, 512], "Internal")
    scr_h = dr("scr_h", [nt, 128, D], "Internal")
    scr_yrw = dr("scr_yrw", [nt, 128, 512], "Internal")
    scr_ya = dr("scr_ya", [nt, 128, 512], "Internal")

    with contextlib.ExitStack() as st0:
        P = Prog(nc, st0)
        ldq = [None] * 4
        cq = wq = stq = None
        s2q = [None] * 2

        def newsem():
            return None

        identf = P.sb([128, 128], F32, "identf")
        identb = P.sb([128, 128], BF16, "identb")
        ones1 = P.sb([128, 1], F32, "ones1")
        flg = P.sb([128, 1], F32, "flg")
        wup = gup = None
        P.dma("sp", identf[:], c_ident, writes=[identf], sem=cq)
        P.dma("sp", flg[:], flag, writes=[flg], sem=cq)
        P.cp("dve", identb[:], identf[:], [identf], [identb])
        P.op("pool", lambda e: e.memset(ones1[:], 1.0), (), [ones1])
        W0F, W0B, A0, KK, KA, RK, LNW, LNB, QG = range(9)
        stages = [None, None]
        gv = None
        vb = vb2 = cum = amask = nmask = None

        def alloc_stages(st_, n=2):
            del stages[:]
            for i_ in range(n):
                stages.append(P.sb([128, 2048], F32, "stage", st_))

        def alloc_gv(st_, rows=(0, 1, 2)):
            g_ = P.sb([128, len(rows), D], F32, "gv", st_)
            for i_, r in enumerate(rows):
                P.dma("sp", g_[:, i_, :], gvec[r, :].partition_broadcast(128), writes=[g_], sem=cq)
            return g_

        def small_consts(st_, need2=False, rw=True):
            vb_ = P.sb([128, 9, 512], F32, "vb", st_)
            vb2_ = P.sb([128, 512 if need2 else 1], F32, "vb2", st_)
            cum_ = amask_ = nmask_ = None
            if rw:
                cum_ = P.sb([128, 4, 128], F32, "cum", st_)
                amask_ = P.sb([128, 2, 512], F32, "amask", st_)
                nmask_ = P.sb([128, 2, 128], F32, "nmask", st_)
                P.dma("sp", cum_[:], c_cum, writes=[cum_], sem=cq)
                P.dma("sp", amask_[:], c_amask, writes=[amask_], sem=cq)
                P.dma("sp", nmask_[:], c_nmask, writes=[nmask_], sem=cq)
            for r in range(9):
                P.dma("sp", vb_[:, r, :], vec512[r, :].partition_broadcast(128), writes=[vb_], sem=cq)
            if need2:
                P.dma("sp", vb2_[:], vec512b[0, :].partition_broadcast(128), writes=[vb2_], sem=cq)
            return vb_, vb2_, cum_, amask_, nmask_

        wcnt = [0]

        def load_w(dst, _unused, src, kchunks, ncols, col0=0):
            for k in range(kchunks):
                for c0 in range(0, ncols, 2048):
                    cwid = min(2048, ncols - c0)
                    sg = stages[wcnt[0] % len(stages)]
                    wcnt[0] += 1
                    P.dma("sp" if wcnt[0] % 2 else "act", sg[:, 0:cwid], src[k * 128:(k + 1) * 128, col0 + c0:col0 + c0 + cwid], writes=[sg], sem=wq)
                    eng = "act" if (wcnt[0] % 2) else "dve"
                    P.cp(eng, dst[:, k, c0:c0 + cwid], sg[:, 0:cwid], [sg], [dst])

        def rmsnorm_T(st_bufs, xt, grow, nT):
            sq, ss, nb, pT = st_bufs
            P.act(nb[:], xt[:], AF.Square, [xt], [nb, ss], accum=ss[:])
            P.ts("dve", ss[:], ss[:], 1.0 / D, EPS, ALU.mult, ALU.add, [ss], [ss])
            P.act(ss[:], ss[:], AF.Ln, [ss], [ss])
            P.act(ss[:], ss[:], AF.Exp, [ss], [ss], scale=-0.5)
            P.stt("dve", nb[:], xt[:], ss[:], gv[:, grow, :], ALU.mult, ALU.mult, [xt, ss, gv], [nb])
            for c in range(8):
                P.tr(pT[:, c * 128:(c + 1) * 128], nb[:, c * 128:(c + 1) * 128], identb[:], [nb, identb], [pT], sig=(c == 7))
            P.cp("act", nT[:].rearrange("p a b -> p (a b)"), pT[:], [pT], [nT])

        with contextlib.ExitStack() as st:
            alloc_stages(st)
            gv = alloc_gv(st)
            wrw = P.sb([128, 8, RWC], BF16, "wrw", st)
            load_w(wrw, None, w_in, 8, RWC, col0=1536)
            cw = P.sb([128, 3, RWC], F32, "cw", st)
            for r in range(3):
                P.dma("sp", cw[:, r, :], conv_w[r, :].partition_broadcast(128), writes=[cw], sem=cq)
            xt = [P.sb([128, D], F32, "xt%d" % i, st) for i in range(2)]
            ssA = [P.sb([128, 1], F32, "ss%d" % i, st) for i in range(2)]
            nbA = [P.sb([128, D], BF16, "nb%d" % i, st) for i in range(2)]
            nTA = [P.sb([128, 8, 128], BF16, "nT%d" % i, st) for i in range(2)]
            z = [P.sb([128, RWC], F32, "z%d" % i, st) for i in range(6)]
            zp = [P.sb([128, RWC], F32, "zp%d" % i, st) for i in range(2)]
            zn = [P.sb([128, RWC], F32, "zn%d" % i, st) for i in range(2)]
            tB = [P.sb([1, RWC], F32, "tB%d" % i, st) for i in range(2)]
            pTA = [P.ps([128, 1024], BF16, "pTA%d" % i, st) for i in range(2)]
            pzA = [[P.ps([128, 512], F32, "pzA%d_%d" % (i, j), st) for j in range(3)] for i in range(2)]
            xq = [newsem() for _ in range(2)]
            shs = [newsem() for _ in range(2)]
            uq = [newsem() for _ in range(2)]

            def tileA(c):
                p = c % 2
                x_ = xt[p]
                P.dma("sp", x_[:], xs[c * 128:(c + 1) * 128, :], (), [x_], sem=xq[p])
                yield
                ss, nb, nT, pT = ssA[p], nbA[p], nTA[p], pTA[p]
                P.act(nb[:], x_[:], AF.Square, [x_], [nb, ss], accum=ss[:])
                P.ts("dve", ss[:], ss[:], 1.0 / D, EPS, ALU.mult, ALU.add, [ss], [ss])
                yield
                P.act(ss[:], ss[:], AF.Ln, [ss], [ss])
                P.act(ss[:], ss[:], AF.Exp, [ss], [ss], scale=-0.5)
                yield
                P.stt("dve", nb[:], x_[:], ss[:], gv[:, 0, :], ALU.mult, ALU.mult, [x_, ss, gv], [nb])
                for k in range(8):
                    P.tr(pT[:, k * 128:(k + 1) * 128], nb[:, k * 128:(k + 1) * 128], identb[:], [nb, identb], [pT], sig=(k == 7))
                yield
                P.cp("act", nT[:].rearrange("p a b -> p (a b)"), pT[:], [pT], [nT])
                zc = z[c % 6]
                for gi, g0 in enumerate(range(0, RWC, 512)):
                    gw = min(512, RWC - g0)
                    pz = pzA[p][gi % 3]
                    for k in range(8):
                        P.mm(pz[:, 0:gw], nT[:, k, :], wrw[:, k, g0:g0 + gw], k == 0, k == 7, [nT, wrw], [pz])
                    yield
                    P.cp("act", zc[:, g0:g0 + gw], pz[:, 0:gw], [pz], [zc])

            zpT = [[Buf(zp[i].t, "zpT") for _ in range(3)] for i in range(2)]
            znT = [[Buf(zn[i].t, "znT") for _ in range(3)] for i in range(2)]

            def convA(c):
                p = c % 2
                zc = z[c % 6]
                zp_, zn_, tB_ = zp[p], zn[p], tB[p]
                zpt, znt = zpT[p], znT[p]
                first = (c == 0)
                last = (c == nt - 1)
                segs = (c % seg_t == 0)
                sege = (c % seg_t == seg_t - 1)
                P.dma("sp", zp_[1:113, :], zc[0:112, :], [zc], [zpt[0]])
                P.dma("sp", zp_[113:128, :], zc[112:127, :], [zc], [zpt[1]])
                if first:
                    P.op("pool", lambda e: e.memset(zp_[0:1, :], 0.0), (), [zpt[2]])
                else:
                    P.dma("sp", zp_[0:1, :], z[(c - 1) % 6][127:128, :], [z[(c - 1) % 6]], [zpt[2]])
                P.dma("sp", zn_[0:112, :], zc[1:113, :], [zc], [znt[0]])
                P.dma("sp", zn_[112:127, :], zc[113:128, :], [zc], [znt[1]])
                if last:
                    P.op("pool", lambda e: e.memset(tB_[:], 0.0), (), [tB_])
                elif sege:
                    P.ts("dve", tB_[:], z[(c + 1) % 6][0:1, :], flg[0:1, 0:1], None, ALU.mult, None, [z[(c + 1) % 6], flg], [tB_])
                else:
                    P.cp("dve", tB_[:], z[(c + 1) % 6][0:1, :], [z[(c + 1) % 6]], [tB_])
                P.dma("sp", zn_[127:128, :], tB_[:], [tB_], [znt[2]])
                yield
                if segs and not first:
                    P.ts("dve", zp_[0:1, :], zp_[0:1, :], flg[0:1, 0:1], None, ALU.mult, None, [zpt[2], flg], [zpt[2]])
                P.tt("dve", zn_[:], zn_[:], cw[:, 2, :], ALU.mult, znt + [cw], znt)
                P.tt("dve", zp_[:], zp_[:], cw[:, 0, :], ALU.mult, zpt + [cw], zpt)
                yield
                P.tt("dve", zn_[:], zn_[:], zp_[:], ALU.add, znt + zpt, znt)
                P.tt("dve", zp_[:], zc[:], cw[:, 1, :], ALU.mult, [zc, cw] + zpt, zpt)
                yield
                P.tt("dve", zn_[:], zn_[:], zp_[:], ALU.add, znt + zpt, znt)
                P.dma("act", scr_u[c], zn_[:], znt, ())

            tA = lambda c: tileA(c) if 0 <= c < nt else None
            cA = lambda c: convA(c) if 0 <= c < nt else None
            rr([tA(0), tA(1)])
            rr([tA(2)])
            for t0 in range(0, nt, 2):
                rr([tA(t0 + 3), tA(t0 + 4), cA(t0), cA(t0 + 1)])
            P.barrier()
            P.emit()

        half = nt // 2
        with contextlib.ExitStack() as st:
            vb, vb2, cum, amask, nmask = small_consts(st)
            wup = P.sb([64, 3, 512], BF16, "wup", st)
            gup = P.sb([128, 512], BF16, "gup", st)
            with contextlib.ExitStack() as st_w:
                alloc_stages(st_w)
                P.dma("sp", stages[0][0:64, 0:1536].rearrange("p (a b) -> p a b", a=3), w_up.rearrange("a p b -> p a b"), writes=[stages[0]], sem=wq)
                P.cp("dve", wup[:].rearrange("p a b -> p (a b)"), stages[0][0:64, 0:1536], [stages[0]], [wup])
                P.dma("sp", stages[1][:, 0:512], g_up, writes=[stages[1]], sem=wq)
                P.cp("dve", gup[:], stages[1][:, 0:512], [stages[1]], [gup])
                P.barrier()
                P.emit()
            SD = []
            for d in range(2):
                X = {}
                X["u"] = P.sb([128, RWC], F32, "u", st)
                X["t320"] = P.sb([128, 256], BF16, "t320", st)
                X["lT"] = P.sb([128, 3, 128], BF16, "lT", st)
                X["sg"] = P.sb([128, 512], F32, "sg", st)
                X["a"] = P.sb([128, 512], F32, "a_", st)
                X["ex"] = [P.sb([128, 512], F32, "ex%d" % i, st) for i in range(2)]
                X["kk"] = P.sb([128, 512], F32, "kk", st)
                X["tmp"] = P.sb([128, 512], F32, "tmp", st)
                X["s8"] = P.sb([128, 8], F32, "s8", st)
                X["bs"] = P.sb([128, 8], F32, "bs", st)
                X["kt"] = P.sb([128, 512], F32, "kt", st)
                X["beta"] = P.sb([128, 512], F32, "beta", st)
                X["tmb"] = P.sb([128, 4, 512], BF16, "tmb", st)
                X["fm"] = P.sb([64, 8, 4, 128], BF16, "fm", st)
                X["Vb"] = [P.sb([128, 512], BF16, "Vb%d" % i, st) for i in range(2)]
                X["KH"] = [P.sb([128, 512], BF16, "KH%d" % i, st) for i in range(2)]
                X["BH"] = [P.sb([128, 512], BF16, "BH%d" % i, st) for i in range(2)]
                X["eLC"] = [P.sb([64, 8], F32, "eLC%d" % i, st) for i in range(2)]
                X["g"] = [P.sb([128, 512], F32, "g%d" % i, st) for i in range(2)]
                X["bv"] = [P.sb([128, 512], F32, "bv%d" % i, st) for i in range(2)]
                yt = [P.sb([128, 512], F32, "y%d" % i, st) for i in range(2)]
                X["y"] = yt
                X["yh"] = [[Buf(yt[i].t, "yh") for _ in range(2)] for i in range(2)]
                X["yo"] = P.sb([128, 512], F32, "yo", st)
                X["oc"] = P.sb([128, 512], F32, "oc", st)
                X["s8c"] = P.sb([128, 8], F32, "s8c", st)
                X["s8d"] = P.sb([128, 8], F32, "s8d", st)
                X["yrw"] = X["oc"]
                X["prp"] = P.ps([128, 512], F32, "prp", st)
                X["pT"] = P.ps([128, 1024], BF16, "pTd", st)
                X["uq"] = newsem()
                X["yq"] = newsem()
                X["sq"] = newsem()
                X["rq"] = newsem()
                X["hg"] = []
                for hg in range(2):
                    G = {}
                    G["AG"] = P.sb([128, 4, 4, 128], BF16, "AG", st)
                    G["PN"] = [P.sb([128, 4, 128], BF16, "PN%d" % i, st) for i in range(2)]
                    G["PX"] = [P.sb([128, 4, 128], BF16, "PX%d" % i, st) for i in range(2)]
                    G["ZN"] = P.sb([128, 4, 128], BF16, "ZN", st)
                    G["ZX"] = P.sb([128, 4, 128], BF16, "ZX", st)
                    G["RHS"] = P.sb([128, 4, 64], BF16, "RHS", st)
                    G["U"] = P.sb([128, 4, 64], BF16, "U", st)
                    G["Hf"] = P.sb([64, 4, 64], F32, "Hf", st)
                    G["Hb"] = P.sb([64, 4, 64], BF16, "Hb", st)
                    G["ps"] = P.ps([128, 512], F32, "pinv", st)
                    X["hg"].append(G)
                SD.append(X)
            scrY = [Buf(None, "scrY%d" % c) for c in range(nt)]

            def tile_of(d, k):
                return k if d == 0 else nt - 1 - k

            def delayed(gen, n):
                for _ in range(n):
                    yield
                yield from gen

            def is_second(d, c):
                return (c >= half) if d == 0 else (c < half)

            def prepA(d, k):
                X = SD[d]
                c = tile_of(d, k)
                par = k % 2
                comb = is_second(d, c)
                u, t320, lT, sg, a_, ex = X["u"], X["t320"], X["lT"], X["sg"], X["a"], X["ex"]
                kk, tmp, s8, kt, beta, tmb = X["kk"], X["tmp"], X["s8"], X["kt"], X["beta"], X["tmb"]
                prp, pT = X["prp"], X["pT"]
                P.dma("sp", u[:], scr_u[c], (), [u], sem=X["uq"])
                yield
                r_ = u[:, 0:512]
                k_ = u[:, 512:1024]
                v_ = u[:, 1024:1536]
                P.act(t320[:, 0:64], u[:, 1536 + 64 * d:1600 + 64 * d], AF.Tanh, [u], [t320])
                P.act(t320[:, 64:128], u[:, 1664:1728], AF.Copy, [u], [t320])
                if comb:
                    P.act(t320[:, 128:256], u[:, 1728:1856], AF.Sigmoid, [u], [t320])
                yield
                P.tr(pT[0:64, 0:128], t320[:, 0:64], identb[:], [t320, identb], [pT], sig=False)
                P.tr(pT[0:64, 128:256], t320[:, 64:128], identb[:], [t320, identb], [pT], sig=not comb)
                if comb:
                    P.tr(pT[:, 256:384], t320[:, 128:256], identb[:], [t320, identb], [pT])
                yield
                if comb:
                    P.cp("dve", lT[:].rearrange("p a b -> p (a b)"), pT[:, 0:384], [pT], [lT])
                else:
                    P.cp("dve", lT[0:64, 0:2, :].rearrange("p a b -> p (a b)"), pT[0:64, 0:256], [pT], [lT])
                yield
                P.mm(prp[:], lT[0:64, 0, :], wup[:, d, :], True, True, [lT, wup], [prp])
                yield
                P.tt("dve", tmp[:], prp[:], vb[:, W0F + d, :], ALU.add, [prp, vb], [tmp])
                yield
                P.act(sg[:], tmp[:], AF.Sigmoid, [tmp], [sg])
                P.mm(prp[:], lT[0:64, 1, :], wup[:, 2, :], True, True, [lT, wup], [prp])
                yield
                P.tt("dve", tmp[:], prp[:], vb[:, A0, :], ALU.add, [prp, vb], [tmp])
                yield
                P.act(a_[:], tmp[:], AF.Sigmoid, [tmp], [a_])
                P.tt("pool", kk[:], k_, vb[:, KK, :], ALU.mult, [u, vb], [kk])
                yield
                P.act(tmp[:], kk[:], AF.Square, [kk], [tmp])
                yield
                P.red("dve", s8[:], tmp[:].rearrange("p (a b) -> p a b", a=8), [tmp], [s8])
                yield
                P.act(s8[:], s8[:], AF.Ln, [s8], [s8])
                P.act(s8[:], s8[:], AF.Exp, [s8], [s8], scale=-0.5)
                yield
                P.ts("dve", s8[:], s8[:], 1e12, None, ALU.min, None, [s8], [s8])
                yield
                kk3 = kk[:].rearrange("p (a b) -> p a b", a=8)
                P.tt("dve", kk3, kk3, s8[:].unsqueeze(2).to_broadcast([128, 8, 64]), ALU.mult, [kk, s8], [kk])
                P.stt("dve", tmp[:], a_[:], -1.0, vb[:, KA, :], ALU.add, ALU.mult, [a_, vb], [tmp])
                yield
                P.stt("dve", kt[:], tmp[:], 1.0, k_, ALU.add, ALU.mult, [tmp, u], [kt])
                P.tt("pool", beta[:], kk[:], a_[:], ALU.mult, [kk, a_], [beta])
                yield
                m1, m2, m4 = (0, 1, 2) if d == 0 else (2, 3, 0)
                P.mm(prp[:], cum[:, m1, :], sg[:], True, True, [cum, sg], [prp])
                yield
                P.act(ex[0][:], prp[:], AF.Exp, [prp], [ex[0]], scale=-DS)
                yield
                P.mm(prp[:], cum[:, m2, :], sg[:], True, True, [cum, sg], [prp])
                P.tt("dve", tmb[:, 0, :], kk[:], ex[0][:], ALU.mult, [kk, ex[0]], [tmb])
                yield
                P.act(ex[1][:], prp[:], AF.Exp, [prp], [ex[1]], scale=-DS)
                P.act(ex[0][:], prp[:], AF.Exp, [prp], [ex[0]], scale=DS)
                yield
                P.mm(prp[:], cum[:, m4, :], sg[:], True, True, [cum, sg], [prp])
                P.tt("pool", tmb[:, 1, :], r_, ex[1][:], ALU.mult, [u, ex[1]], [tmb])
                P.tt("dve", tmb[:, 2, :], kt[:], ex[0][:], ALU.mult, [kt, ex[0]], [tmb])
                yield
                P.tt("pool", tmb[:, 3, :], beta[:], ex[0][:], ALU.mult, [beta, ex[0]], [tmb])
                P.act(ex[1][:], prp[:], AF.Exp, [prp], [ex[1]], scale=-DS)
                yield
                for h in range(8):
                    P.mm(prp[0:64, h:h + 1], sg[:, h * 64:(h + 1) * 64], ones1[:], True, True, [sg, ones1], [prp], sig=(h == 7))
                P.tt("dve", X["KH"][par][:], kt[:], ex[1][:], ALU.mult, [kt, ex[1]], [X["KH"][par]])
                yield
                P.act(X["eLC"][par][:], prp[0:64, 0:8], AF.Exp, [prp], [X["eLC"][par]], scale=-DS)
                P.stt("dve", X["BH"][par][:], beta[:], -1.0, ex[1][:], ALU.mult, ALU.mult, [beta, ex[1]], [X["BH"][par]])
                yield
                P.cp("act", X["Vb"][par][:], v_, [u], [X["Vb"][par]])
                if comb:
                    P.tt("pool", tmp[:], r_, kt[:], ALU.mult, [u, kt], [tmp])
                    yield
                    P.tt("pool", tmp[:], tmp[:], vb[:, RK, :], ALU.mult, [tmp, vb], [tmp])
                    yield
                    P.red("dve", X["bs"][:], tmp[:].rearrange("p (a b) -> p a b", a=8), [tmp], [X["bs"]])

            def commit(d, k):
                X = SD[d]
                c = tile_of(d, k)
                par = k % 2
                comb = is_second(d, c)
                tmb, fm, pT, prp = X["tmb"], X["fm"], X["pT"], X["prp"]
                for hp in range(4):
                    for hh in range(2):
                        h = hp * 2 + hh
                        for m in range(4):
                            P.tr(pT[0:64, (hh * 4 + m) * 128:(hh * 4 + m + 1) * 128], tmb[:, m, h * 64:(h + 1) * 64],
                                 identb[:], [tmb, identb], [pT], sig=(hh == 1 and m == 3))
                    yield
                    P.cp("act" if hp % 2 else "dve", fm[:, hp * 2:hp * 2 + 2, :, :].rearrange("p a b c -> p (a b c)"),
                         pT[0:64, :], [pT], [fm])
                if comb:
                    P.mm(prp[:], X["lT"][:, 2, :], gup[:], True, True, [X["lT"], gup], [prp])
                    P.cp("act", X["g"][par][:], prp[:], [prp], [X["g"][par]])
                    P.tt("pool", X["bv"][par][:].rearrange("p (a b) -> p a b", a=8),
                         X["u"][:, 1024:1536].rearrange("p (a b) -> p a b", a=8),
                         X["bs"][:].unsqueeze(2).to_broadcast([128, 8, 64]), ALU.mult, [X["u"], X["bs"]], [X["bv"][par]])

            def invchain(d, hg, k):
                X = SD[d]
                G = X["hg"][hg]
                c = tile_of(d, k)
                par = k % 2
                fm, Vb, KH, BH, eLC = X["fm"], X["Vb"][par], X["KH"][par], X["BH"][par], X["eLC"][par]
                AG, PNs, PXs, ZN, ZX, RHS, U = G["AG"], G["PN"], G["PX"], G["ZN"], G["ZX"], G["RHS"], G["U"]
                Hf, Hb, ps = G["Hf"], G["Hb"], G["ps"]
                yb = X["yh"][par][hg]
                ytile = X["y"][par]
                hs = [hg * 4 + i for i in range(4)]
                if k == 0:
                    P.op("pool", lambda e: e.memset(Hf[:], 0.0), (), [Hf])
                    P.op("pool", lambda e: e.memset(Hb[:], 0.0), (), [Hb])
                else:
                    joint = (c % seg_t == 0) if d == 0 else (c % seg_t == seg_t - 1)
                    if joint:
                        P.ts("dve", Hf[:], Hf[:], flg[0:64, 0:1], None, ALU.mult, None, [Hf, flg], [Hf])
                        P.cp("act", Hb[:], Hf[:], [Hf], [Hb])
                for i, h in enumerate(hs):
                    P.mm(ps[:, 0:256], fm[:, h, 3, :], fm[:, h, 0:2, :].rearrange("p a b -> p (a b)"), True, True, [fm], [ps], sig=False)
                    P.mm(ps[:, 256:512], fm[:, h, 2, :], fm[:, h, 0:2, :].rearrange("p a b -> p (a b)"), True, True, [fm], [ps])
                    yield
                    P.tt("dve", AG[:, i, :, :].rearrange("p a b -> p (a b)"), ps[:], amask[:, d, :], ALU.mult, [ps, amask], [AG])
                for i, h in enumerate(hs):
                    P.mm(ps[:, i * 128:(i + 1) * 128], fm[:, h, 0, :], fm[:, h, 3, :], True, True, [fm], [ps], sig=(i == 3))
                yield
                curN = PNs[0]
                P.tt("dve", curN[:], ps[:].rearrange("p (a b) -> p a b", a=4),
                     nmask[:, d, :].unsqueeze(1).to_broadcast([128, 4, 128]), ALU.mult, [ps, nmask], [curN])
                P.tt("pool", ZX[:], AG[:, :, 0, :], identb[:].unsqueeze(1).to_broadcast([128, 4, 128]), ALU.add, [AG, identb], [ZX])
                yield
                for i, h in enumerate(hs):
                    P.mm(ps[:, i * 64:(i + 1) * 64], fm[:, h, 0, :], Hb[:, i, :], True, False, [fm, Hb], [ps], sig=False)
                    P.mm(ps[:, i * 64:(i + 1) * 64], AG[:, i, 2, :], Vb[:, h * 64:(h + 1) * 64], False, True, [AG, Vb], [ps], sig=False)
                for i, h in enumerate(hs):
                    P.mm(ps[:, 256 + i * 64:256 + (i + 1) * 64], fm[:, h, 1, :], Hb[:, i, :], True, False, [fm, Hb], [ps], sig=False)
                    P.mm(ps[:, 256 + i * 64:256 + (i + 1) * 64], AG[:, i, 3, :], Vb[:, h * 64:(h + 1) * 64], False, True, [AG, Vb], [ps], sig=(i == 3))
                yield
                P.cp("act", RHS[:].rearrange("p a b -> p (a b)"), ps[:, 0:256], [ps], [RHS])
                P.cp("act", ytile[:, hg * 256:(hg + 1) * 256], ps[:, 256:512], [ps], [yb])
                yield
                for i, h in enumerate(hs):
                    P.mm(ps[0:64, i * 64:(i + 1) * 64], KH[:, h * 64:(h + 1) * 64], Vb[:, h * 64:(h + 1) * 64], True, True, [KH, Vb], [ps], sig=(i == 3))
                P.tt("pool", Hf[:], Hf[:], eLC[:, hg * 4:(hg + 1) * 4].unsqueeze(2).to_broadcast([64, 4, 64]), ALU.mult, [Hf, eLC], [Hf])
                yield
                P.tt("dve", Hf[:], Hf[:], ps[0:64, 0:256].rearrange("p (a b) -> p a b", a=4), ALU.add, [Hf, ps], [Hf])
                yield
                curX_ap = lambda i: AG[:, i, 0, :]
                curX_buf = AG
                for lvl in range(1, 7):
                    both = lvl < 6
                    newN = PNs[lvl % 2]
                    newX = PXs[lvl % 2]
                    if both:
                        for i in range(4):
                            P.mm(ps[:, i * 128:(i + 1) * 128], curN[:, i, :], curX_ap(i), True, True, [curN, curX_buf], [ps], sig=(i == 3))
                        yield
                        P.cp("act", newX[:].rearrange("p a b -> p (a b)"), ps[:], [ps], [newX])
                    for i in range(4):
                        P.mm(ps[:, i * 128:(i + 1) * 128], curX_ap(i), curN[:, i, :], True, True, [curN, curX_buf], [ps], sig=(i == 3))
                    yield
                    P.cp("act", newN[:].rearrange("p a b -> p (a b)"), ps[:], [ps], [newN])
                    curN = newN
                    if both:
                        curX_buf = newX
                        curX_ap = (lambda nx: (lambda i: nx[:, i, :]))(newX)
                    for i in range(4):
                        P.mm(ps[:, i * 128:(i + 1) * 128], newN[:, i, :], ZX[:, i, :], True, True, [newN, ZX], [ps], sig=(i == 3))
                    yield
                    P.tt("dve", ZX[:].rearrange("p a b -> p (a b)"), ps[:], ZX[:].rearrange("p a b -> p (a b)"), ALU.add, [ps, ZX], [ZX])
                for i in range(4):
                    P.mm(ps[:, i * 64:(i + 1) * 64], ZX[:, i, :], RHS[:, i, :], True, True, [ZX, RHS], [ps], sig=(i == 3))
                yield
                P.cp("dve", U[:].rearrange("p a b -> p (a b)"), ps[:, 0:256], [ps], [U])
                yield
                for i, h in enumerate(hs):
                    P.mm(ps[:, i * 64:(i + 1) * 64], AG[:, i, 1, :], U[:, i, :], True, True, [AG, U], [ps], sig=False)
                for i, h in enumerate(hs):
                    P.mm(ps[0:64, 256 + i * 64:256 + (i + 1) * 64], BH[:, h * 64:(h + 1) * 64], U[:, i, :], True, True, [BH, U], [ps], sig=(i == 3))
                yield
                P.tt("dve", ytile[:, hg * 256:(hg + 1) * 256], ytile[:, hg * 256:(hg + 1) * 256], ps[:, 0:256], ALU.add, [ps, yb], [yb])
                P.tt("dve", Hf[:], Hf[:], ps[0:64, 256:512].rearrange("p (a b) -> p a b", a=4), ALU.add, [Hf, ps], [Hf])
                yield
                P.cp("act", Hb[:], Hf[:], [Hf], [Hb])

            def combine(d, k):
                X = SD[d]
                c = tile_of(d, k)
                par = k % 2
                y = X["y"][par]
                ybs = X["yh"][par]
                if not is_second(d, c):
                    P.dma("act", scr_y[c], y[:], ybs, [scrY[c]], sem=X["sq"])
                    return
                yo, oc, s8, s8b, yrw = X["yo"], X["oc"], X["s8c"], X["s8d"], X["yrw"]
                g_, bv = X["g"][par], X["bv"][par]
                P.dma("sp", yo[:], scr_y[c], [scrY[c]], [yo], sem=X["yq"])
                yield
                P.tt("dve", yo[:], yo[:], y[:], ALU.add, [yo] + ybs, [yo])
                yield
                y3 = yo[:].rearrange("p (a b) -> p a b", a=8)
                oc3 = oc[:].rearrange("p (a b) -> p a b", a=8)
                P.red("dve", s8[:], y3, [yo], [s8])
                yield
                P.ts("dve", s8[:], s8[:], 1.0 / 64, None, ALU.mult, None, [s8], [s8])
                yield
                P.tt("dve", oc3, y3, s8[:].unsqueeze(2).to_broadcast([128, 8, 64]), ALU.subtract, [yo, s8], [oc])
                yield
                P.act(yo[:], oc[:], AF.Square, [oc], [yo])
                yield
                P.red("dve", s8b[:], yo[:].rearrange("p (a b) -> p a b", a=8), [yo], [s8b])
                yield
                P.ts("dve", s8b[:], s8b[:], 1.0 / 64, GN_EPS, ALU.mult, ALU.add, [s8b], [s8b])
                yield
                P.act(s8b[:], s8b[:], AF.Ln, [s8b], [s8b])
                yield
                P.act(s8b[:], s8b[:], AF.Exp, [s8b], [s8b], scale=-0.5)
                yield
                P.tt("dve", oc3, oc3, s8b[:].unsqueeze(2).to_broadcast([128, 8, 64]), ALU.mult, [oc, s8b], [oc])
                yield
                P.tt("pool", oc[:], oc[:], vb[:, LNW, :], ALU.mult, [oc, vb], [oc])
                yield
                P.tt("pool", oc[:], oc[:], vb[:, LNB, :], ALU.add, [oc, vb], [oc])
                yield
                P.tt("pool", oc[:], oc[:], bv[:], ALU.add, [oc, bv], [oc])
                yield
                P.tt("pool", yrw[:], oc[:], g_[:], ALU.mult, [oc, g_], [yrw])
                P.dma("act", scr_yrw[c], yrw[:], [yrw], (), sem=X["rq"])

            def prep_commit(d, k):
                yield from prepA(d, k)
                yield from commit(d, k)

            rr([prep_commit(0, 0), prep_commit(1, 0)])
            for k in range(nt):
                streams = []
                if k + 1 < nt:
                    streams += [prep_commit(0, k + 1), prep_commit(1, k + 1)]
                streams += [invchain(d, hg, k) for d in range(2) for hg in range(2)]
                if k >= 1:
                    streams += [combine(0, k - 1), combine(1, k - 1)]
                rr(streams)
            rr([combine(0, nt - 1), combine(1, nt - 1)])
            P.barrier()
            P.emit()

        bts = boundary_tiles(nt, seg_t)
        with contextlib.ExitStack() as st:
            alloc_stages(st, 4)
            gv = alloc_gv(st, (0,))
            vb, vb2, cum, amask, nmask = small_consts(st, True, rw=False)
            wna = P.sb([128, 8, 1536], BF16, "wna", st)
            load_w(wna, None, w_in, 8, 1536, col0=0)
            tabG = P.sb([128, 8, 16, 64], BF16, "tabG", st)
            tabT = P.sb([128, 8, 10, 64], BF16, "tabT", st)
            valid = P.sb([128, nb_t, 7, 2], F32, "valid", st)
            P.dma("sp", valid[:], c_val, (), [valid], sem=cq)
            with contextlib.ExitStack() as st_t:
                gm = P.sb([128, 16, 64], F32, "gm", st_t)
                tmk = P.sb([128, 10, 64], F32, "tmk", st_t)
                P.dma("sp", gm[:], c_gm, (), [gm], sem=cq)
                P.dma("sp", tmk[:], c_tm, (), [tmk], sem=cq)
                for h in range(8):
                    sgb = stages[h % 2]
                    P.dma("sp", sgb[:, 0:1024].rearrange("p (a b) -> p a b", a=16), rbG[:, h, :, :], (), [sgb], sem=wq)
                    P.act(sgb[:, 0:1024], sgb[:, 0:1024], AF.Exp, [sgb], [sgb])
                    P.tt("dve", tabG[:, h, :, :], sgb[:, 0:1024].rearrange("p (a b) -> p a b", a=16), gm[:], ALU.mult, [sgb, gm], [tabG])
                    P.dma("sp", sgb[:, 1024:1664].rearrange("p (a b) -> p a b", a=10), rbT[:, h, :, :], (), [sgb], sem=wq)
                    P.act(sgb[:, 1024:1664], sgb[:, 1024:1664], AF.Exp, [sgb], [sgb])
                    P.tt("dve", tabT[:, h, :, :], sgb[:, 1024:1664].rearrange("p (a b) -> p a b", a=10), tmk[:], ALU.mult, [sgb, tmk], [tabT])
                P.barrier()
                P.emit()
            xt = [P.sb([128, D], F32, "xt%d" % i, st) for i in range(2)]
            ss = P.sb([128, 1], F32, "ss", st)
            nb = P.sb([128, D], BF16, "nb", st)
            nT = P.sb([128, 8, 128], BF16, "nT", st)
            NR = 5
            qTr = [P.sb([64, 8, 128], BF16, "qTr%d" % i, st) for i in range(NR)]
            KR = 8
            kTr = [P.sb([64, 8, 128], BF16, "kTr%d" % i, st) for i in range(KR)]
            v1r = [P.sb([128, 8, 65], BF16, "v1r%d" % i, st) for i in range(KR)]
            zna = P.sb([128, 1536], F32, "zna", st)
            tmp2 = P.sb([128, 512], F32, "tmp2", st)
            qk = P.sb([128, 2, 512], BF16, "qk", st)
            s16 = P.sb([128, 16], F32, "s16", st)
            ESs = [P.sb([128, 896], F32, "ES%d" % i, st) for i in range(2)]
            PTs = [P.sb([128, 896], BF16, "PT%d" % i, st) for i in range(2)]
            yas = [P.sb([128, 512], F32, "ya%d" % i, st) for i in range(2)]
            rden = P.sb([128, 8], F32, "rden", st)
            pTb = P.ps([128, 1024], BF16, "pTb", st)
            pss = [P.ps([128, 512], F32, "ps%d" % i, st) for i in range(7)]
            yaq = [newsem() for _ in range(2)]
            for r_ in v1r:
                P.op("pool", lambda e, r_=r_: e.memset(r_[:], 1.0), (), [r_])

            znab = [zna, P.sb([128, 1536], F32, "znab", st)]
            pTx = Buf(pTb.t, "pTx")
            pTq = Buf(pTb.t, "pTq")

            def xz(c):
                x_ = xt[c % 2]
                zc = znab[c % 2]
                P.dma("sp", x_[:], xs[c * 128:(c + 1) * 128, :], (), [x_])
                yield
                P.act(nb[:], x_[:], AF.Square, [x_], [nb, ss], accum=ss[:])
                yield
                P.ts("dve", ss[:], ss[:], 1.0 / D, EPS, ALU.mult, ALU.add, [ss], [ss])
                yield
                P.act(ss[:], ss[:], AF.Ln, [ss], [ss])
                yield
                P.act(ss[:], ss[:], AF.Exp, [ss], [ss], scale=-0.5)
                yield
                P.stt("dve", nb[:], x_[:], ss[:], gv[:, 0, :], ALU.mult, ALU.mult, [x_, ss, gv], [nb])
                yield
                for r in range(2):
                    for j in range(4):
                        k = r * 4 + j
                        P.tr(pTb[:, j * 128:(j + 1) * 128], nb[:, k * 128:(k + 1) * 128], identb[:], [nb, identb], [pTx], sig=(j == 3))
                    yield
                    P.cp("act", nT[:, r * 4:(r + 1) * 4, :].rearrange("p a b -> p (a b)"), pTb[:, 0:512], [pTx], [nT])
                    yield
                for g0 in range(3):
                    pz = pss[4]
                    for k in range(8):
                        P.mm(pz[:], nT[:, k, :], wna[:, k, g0 * 512:(g0 + 1) * 512], k == 0, k == 7, [nT, wna], [pz])
                    yield
                    P.cp("act", zc[:, g0 * 512:(g0 + 1) * 512], pz[:], [pz], [zc])
                    yield

            def qk_(c):
                zc = znab[c % 2]
                P.act(tmp2[:], zc[:, 0:512], AF.Square, [zc], [tmp2])
                yield
                P.red("dve", s16[:, 0:8], tmp2[:].rearrange("p (a b) -> p a b", a=8), [tmp2], [s16])
                yield
                P.act(tmp2[:], zc[:, 512:1024], AF.Square, [zc], [tmp2])
                yield
                P.red("dve", s16[:, 8:16], tmp2[:].rearrange("p (a b) -> p a b", a=8), [tmp2], [s16])
                yield
                P.ts("dve", s16[:], s16[:], 1.0 / 64, EPS, ALU.mult, ALU.add, [s16], [s16])
                yield
                P.act(s16[:], s16[:], AF.Ln, [s16], [s16])
                yield
                P.act(s16[:], s16[:], AF.Exp, [s16], [s16], scale=-0.5)
                yield
                for w_ in range(2):
                    z3 = zc[:, w_ * 512:(w_ + 1) * 512].rearrange("p (a b) -> p a b", a=8)
                    t3 = tmp2[:].rearrange("p (a b) -> p a b", a=8)
                    P.tt("dve", t3, z3, s16[:, w_ * 8:(w_ + 1) * 8].unsqueeze(2).to_broadcast([128, 8, 64]), ALU.mult, [zc, s16], [tmp2])
                    yield
                    gsrc = vb[:, QG, :] if w_ == 0 else vb2[:]
                    P.tt("pool", qk[:, w_, :], tmp2[:], gsrc, ALU.mult, [tmp2, vb, vb2], [qk])
                    yield
                v1 = v1r[c % KR]
                P.cp("act", v1[:, :, 0:64], zc[:, 1024:1536].rearrange("p (a b) -> p a b", a=8), [zc], [v1])
                qT = qTr[c % NR]
                kT = kTr[c % KR]
                for w_, dst in ((0, qT), (1, kT)):
                    for r in range(2):
                        for j in range(4):
                            h = r * 4 + j
                            P.tr(pTb[0:64, 512 + j * 128:512 + (j + 1) * 128], qk[:, w_, h * 64:(h + 1) * 64], identb[:], [qk, identb], [pTq], sig=(j == 3))
                        yield
                        P.cp("dve" if (w_ + r) % 2 else "act", dst[:, r * 4:(r + 1) * 4, :].rearrange("p a b -> p (a b)"), pTb[0:64, 512:1024], [pTq], [dst])
                        yield

            yaT = [[Buf(yas[i].t, "yaT") for _ in range(2)] for i in range(2)]
            rdens = [P.sb([128, 4], F32, "rden%d" % i, st) for i in range(2)]

            def attn_half(i, hh):
                js, kind, idx0 = na_slots(i, nt, seg_t)
                slots = [(s, j) for s, j in enumerate(js) if 0 <= j < nt]
                qT = qTr[i % NR]
                po = pss[5 + hh]
                ya = yas[i % 2]
                yat = yaT[i % 2][hh]
                ES, PT = ESs[hh], PTs[hh]
                pa_, pb_ = pss[hh * 2], pss[hh * 2 + 1]
                rd = rdens[hh]
                s_lo = slots[0][0]
                s_hi = slots[-1][0]
                n_ = s_hi - s_lo + 1
                for hl in range(4):
                    h = hh * 4 + hl
                    for s, j in slots:
                        bank = pa_ if s < 4 else pb_
                        P.mm(bank[:, (s % 4) * 128:(s % 4 + 1) * 128], kTr[j % KR][:, h, :], qT[:, h, :], True, True,
                             [kTr[j % KR], qT], [bank], sig=True)
                    yield
                    a1_ = min(s_hi, 3)
                    if s_lo <= 3:
                        P.act(ES[:, s_lo * 128:(a1_ + 1) * 128], pa_[:, s_lo * 128:(a1_ + 1) * 128], AF.Exp, [pa_], [ES], scale=0.125)
                    if s_hi >= 4:
                        b0_ = max(s_lo, 4)
                        P.act(ES[:, b0_ * 128:(s_hi + 1) * 128], pb_[:, (b0_ - 4) * 128:(s_hi - 3) * 128], AF.Exp, [pb_], [ES], scale=0.125)
                    yield
                    es4 = ES[:, s_lo * 128:(s_hi + 1) * 128].rearrange("p (a b) -> p a b", b=64)
                    pt4 = PT[:, s_lo * 128:(s_hi + 1) * 128].rearrange("p (a b) -> p a b", b=64)
                    if kind == "T":
                        P.tt("dve", pt4, es4, tabT[:, h, 2 * s_lo:2 * s_hi + 2, :], ALU.mult, [ES, tabT], [PT])
                    else:
                        P.tt("dve", es4, es4, tabG[:, h, 2 * s_lo + 1:2 * s_hi + 3, :], ALU.mult, [ES, tabG], [ES])
                        yield
                        bi = bts.index(i)
                        vv = valid[:, bi, s_lo:s_hi + 1, :].rearrange("p a b -> p (a b)").unsqueeze(2).to_broadcast([128, 2 * n_, 64])
                        P.tt("pool", pt4, es4, vv, ALU.mult, [ES, valid], [PT])
                    yield
                    for s, j in slots:
                        P.mm(po[:, hl * 65:(hl + 1) * 65], PT[:, s * 128:(s + 1) * 128], v1r[j % KR][:, h, :],
                             s == s_lo, s == s_hi, [PT, v1r[j % KR]], [po], sig=(s == s_hi))
                yield
                po3 = po[:, 0:260].rearrange("p (a b) -> p a b", a=4)
                P.rcp(rd[:], po3[:, :, 64], [po], [rd])
                yield
                P.tt("dve", ya[:, hh * 256:(hh + 1) * 256].rearrange("p (a b) -> p a b", a=4), po3[:, :, 0:64],
                     rd[:].unsqueeze(2).to_broadcast([128, 4, 64]), ALU.mult, [po, rd], [yat])

            def attn_store(i):
                P.dma("act", scr_ya[i], yas[i % 2][:], yaT[i % 2], ())

            LAG = 4
            rr([xz(0)])
            for it in range(nt + LAG):
                streams = [xz(it + 1) if it + 1 < nt else None, qk_(it) if it < nt else None]
                if it - LAG >= 0:
                    streams += [attn_half(it - LAG, 0), attn_half(it - LAG, 1)]
                rr(streams)
                if it - LAG >= 0:
                    attn_store(it - LAG)
            P.barrier()
            P.emit()

        with contextlib.ExitStack() as st:
            alloc_stages(st, 2)
            gv = alloc_gv(st, (0,))
            wgt = P.sb([128, 8, 2048], BF16, "wgt", st)
            wao = P.sb([128, 4, D], BF16, "wao", st)
            wbo = P.sb([128, 4, D], BF16, "wbo", st)
            wo = P.sb([128, 8, D], BF16, "wo", st)
            load_w(wgt, None, w_in, 8, 2048, col0=3392)
            load_w(wao, None, w_a_out, 4, D)
            load_w(wbo, None, w_b_out, 4, D)
            load_w(wo, None, w_o, 8, D)
            NP = 3
            xt = [P.sb([128, D], F32, "xt%d" % i, st) for i in range(NP)]
            yl = [P.sb([128, 2, 512], F32, "yl%d" % i, st) for i in range(NP)]
            ssC = [P.sb([128, 1], F32, "ss%d" % i, st) for i in range(NP)]
            nbC = [P.sb([128, D], BF16, "nb%d" % i, st) for i in range(NP)]
            nTC = [P.sb([128, 8, 128], BF16, "nT%d" % i, st) for i in range(NP)]
            ylb = [P.sb([128, 2, 512], BF16, "ylb%d" % i, st) for i in range(NP)]
            ylT = [P.sb([128, 8, 128], BF16, "ylT%d" % i, st) for i in range(NP)]
            gtsC = [P.sb([128, 2048], F32, "gts%d" % i, st) for i in range(NP)]
            mrgC = [P.sb([128, D], F32, "mrg%d" % i, st) for i in range(NP)]
            mrbC = [P.sb([128, D], BF16, "mrb%d" % i, st) for i in range(NP)]
            mTC = [P.sb([128, 8, 128], BF16, "mT%d" % i, st) for i in range(NP)]
            hhC = [P.sb([128, D], F32, "hh%d" % i, st) for i in range(NP)]
            pTs = P.ps([128, 1024], BF16, "pTC", st)
            pTC = [pTs] * NP
            pzC = [[P.ps([128, 512], F32, "pzC%d_%d" % (i, j), st) for j in range(3 if i == 0 else 2)] for i in range(NP)]
            xqC = [newsem() for _ in range(NP)]
            yqC = [newsem() for _ in range(NP)]
            hqC = [newsem() for _ in range(NP)]

            def tileC(i):
                p = i % NP
                x_, yl_, ss, nb, nT, pT = xt[p], yl[p], ssC[p], nbC[p], nTC[p], pTC[p]
                gts, mrg, mrb, mT, h_ = gtsC[p], mrgC[p], mrbC[p], mTC[p], hhC[p]
                pzs = pzC[p]
                cnt = [0]

                def pz_():
                    cnt[0] += 1
                    return pzs[cnt[0] % len(pzs)]
                P.dma("sp", x_[:], xs[i * 128:(i + 1) * 128, :], (), [x_], sem=xqC[p])
                P.dma("sp", yl_[:, 0, :], scr_ya[i], (), [yl_], sem=yqC[p])
                P.dma("sp", yl_[:, 1, :], scr_yrw[i], (), [yl_], sem=yqC[p])
                yield
                P.act(nb[:], x_[:], AF.Square, [x_], [nb, ss], accum=ss[:])
                yield
                P.ts("dve", ss[:], ss[:], 1.0 / D, EPS, ALU.mult, ALU.add, [ss], [ss])
                yield
                P.act(ss[:], ss[:], AF.Ln, [ss], [ss])
                yield
                P.act(ss[:], ss[:], AF.Exp, [ss], [ss], scale=-0.5)
                yield
                P.stt("dve", nb[:], x_[:], ss[:], gv[:, 0, :], ALU.mult, ALU.mult, [x_, ss, gv], [nb])
                P.cp("pool", ylb[p][:], yl_[:], [yl_], [ylb[p]])
                yield
                for k in range(8):
                    P.tr(pT[:, k * 128:(k + 1) * 128], nb[:, k * 128:(k + 1) * 128], identb[:], [nb, identb], [pT], sig=(k == 7))
                P.cp("act", nT[:].rearrange("p a b -> p (a b)"), pT[:], [pT], [nT])
                yield
                for k in range(8):
                    P.tr(pT[:, k * 128:(k + 1) * 128], ylb[p][:, k // 4, (k % 4) * 128:(k % 4 + 1) * 128], identb[:], [ylb[p], identb], [pT], sig=(k == 7))
                P.cp("dve", ylT[p][:].rearrange("p a b -> p (a b)"), pT[:], [pT], [ylT[p]])
                for g0 in range(4):
                    pz = pz_()
                    for k in range(8):
                        P.mm(pz[:], nT[:, k, :], wgt[:, k, g0 * 512:(g0 + 1) * 512], k == 0, k == 7, [nT, wgt], [pz])
                    yield
                    P.act(gts[:, g0 * 512:(g0 + 1) * 512], pz[:], AF.Sigmoid, [pz], [gts])
                for g0 in range(2):
                    pz = pz_()
                    for k in range(4):
                        P.mm(pz[:], ylT[p][:, k, :], wao[:, k, g0 * 512:(g0 + 1) * 512], k == 0, k == 3, [ylT[p], wao], [pz])
                    yield
                    P.tt("dve", mrg[:, g0 * 512:(g0 + 1) * 512], pz[:], gts[:, g0 * 512:(g0 + 1) * 512], ALU.mult, [pz, gts], [mrg])
                    pz2 = pz_()
                    for k in range(4):
                        P.mm(pz2[:], ylT[p][:, 4 + k, :], wbo[:, k, g0 * 512:(g0 + 1) * 512], k == 0, k == 3, [ylT[p], wbo], [pz2])
                    yield
                    P.tt("dve", gts[:, 1024 + g0 * 512:1024 + (g0 + 1) * 512], pz2[:], gts[:, 1024 + g0 * 512:1024 + (g0 + 1) * 512], ALU.mult, [pz2, gts], [gts])
                    yield
                    P.tt("pool", mrb[:, g0 * 512:(g0 + 1) * 512], mrg[:, g0 * 512:(g0 + 1) * 512], gts[:, 1024 + g0 * 512:1024 + (g0 + 1) * 512], ALU.add, [mrg, gts], [mrb])
                yield
                for k in range(8):
                    P.tr(pT[:, k * 128:(k + 1) * 128], mrb[:, k * 128:(k + 1) * 128], identb[:], [mrb, identb], [pT], sig=(k == 7))
                P.cp("act", mT[:].rearrange("p a b -> p (a b)"), pT[:], [pT], [mT])
                for g0 in range(2):
                    pz = pz_()
                    for k in range(8):
                        P.mm(pz[:], mT[:, k, :], wo[:, k, g0 * 512:(g0 + 1) * 512], k == 0, k == 7, [mT, wo], [pz])
                    yield
                    P.tt("dve", h_[:, g0 * 512:(g0 + 1) * 512], pz[:], x_[:, g0 * 512:(g0 + 1) * 512], ALU.add, [pz, x_], [h_])
                P.dma("act", scr_h[i], h_[:], [h_], (), sem=hqC[p])

            rr([chain_gens(tileC, range(0, nt, 3)), delayed_start(chain_gens(tileC, range(1, nt, 3)), 10),
                delayed_start(chain_gens(tileC, range(2, nt, 3)), 20)])
            P.barrier()
            P.emit()

        with contextlib.ExitStack() as st:
            gv = alloc_gv(st, (1, 2))
            wf1 = P.sb([128, 8, 4096], BF16, "wf1", st)
            wf2 = P.sb([128, 32, D], BF16, "wf2", st)
            wpg = P.sb([128, 8, D], BF16, "wpg", st)
            wpl = P.sb([128, 2, D], BF16, "wpl", st)
            with contextlib.ExitStack() as st_w:
                alloc_stages(st_w, 6)
                load_w(wf1, None, w_ff1, 8, 4096)
                load_w(wf2, None, w_ff2, 32, D)
                load_w(wpg, None, w_pgate, 8, D)
                load_w(wpl, None, w_ple, 2, D)
                P.barrier()
                P.emit()
            ht = [P.sb([128, D], F32, "ht%d" % i, st) for i in range(2)]
            pt_ = [P.sb([128, 256], F32, "pt%d" % i, st) for i in range(2)]
            ss3 = [P.sb([128, 1], F32, "ss%d" % i, st) for i in range(2)]
            nb3 = [P.sb([128, D], BF16, "nb%d" % i, st) for i in range(2)]
            nT3 = [P.sb([128, 8, 128], BF16, "nT%d" % i, st) for i in range(2)]
            hT = [P.sb([128, 32, 128], BF16, "hT%d" % i, st) for i in range(2)]
            rl3 = [P.sb([128, 512], F32, "rl%d" % i, st) for i in range(2)]
            pb3 = [P.sb([128, 256], BF16, "pb%d" % i, st) for i in range(2)]
            pT2 = [P.sb([128, 2, 128], BF16, "pT2%d" % i, st) for i in range(2)]
            gt3 = [P.sb([128, D], F32, "gt%d" % i, st) for i in range(2)]
            pT3 = [P.ps([128, 1024], BF16, "pT3%d" % i, st) for i in range(2)]
            pz3 = [[P.ps([128, 512], F32, "pz3%d_%d" % (i, j), st) for j in range(3)] for i in range(2)]
            hq3 = [newsem() for _ in range(2)]
            pq3 = [newsem() for _ in range(2)]
            oq3 = [newsem() for _ in range(2)]

            def norm3(p, src, grow):
                ss, nb, nT, pT = ss3[p], nb3[p], nT3[p], pT3[p]
                P.act(nb[:], src[:], AF.Square, [src], [nb, ss], accum=ss[:])
                yield
                P.ts("dve", ss[:], ss[:], 1.0 / D, EPS, ALU.mult, ALU.add, [ss], [ss])
                yield
                P.act(ss[:], ss[:], AF.Ln, [ss], [ss])
                yield
                P.act(ss[:], ss[:], AF.Exp, [ss], [ss], scale=-0.5)
                yield
                P.stt("dve", nb[:], src[:], ss[:], gv[:, grow, :], ALU.mult, ALU.mult, [src, ss, gv], [nb])
                yield
                for k in range(8):
                    P.tr(pT[:, k * 128:(k + 1) * 128], nb[:, k * 128:(k + 1) * 128], identb[:], [nb, identb], [pT], sig=(k == 7))
                yield
                P.cp("act", nT[:].rearrange("p a b -> p (a b)"), pT[:], [pT], [nT])
                yield

            def tile3(c):
                p = c % 2
                h_, p_, nT, pT, rl, gt_ = ht[p], pt_[p], nT3[p], pT3[p], rl3[p], gt3[p]
                pzs = pz3[p]
                cnt = [0]

                def pz_():
                    cnt[0] += 1
                    return pzs[cnt[0] % 3]
                P.dma("sp", h_[:], scr_h[c], (), [h_], sem=hq3[p])
                P.dma("sp", p_[:], pp[c * 128:(c + 1) * 128, :], (), [p_], sem=pq3[p])
                yield
                yield from norm3(p, h_, 0)
                for f4 in range(8):
                    pz = pz_()
                    for f in range(4):
                        fc = f4 * 4 + f
                        for k in range(8):
                            P.mm(pz[:, f * 128:(f + 1) * 128], wf1[:, k, fc * 128:(fc + 1) * 128], nT[:, k, :], k == 0, k == 7,
                                 [wf1, nT], [pz], sig=(k == 7 and f == 3))
                    yield
                    P.act(rl[:], pz[:], AF.Relu, [pz], [rl])
                    yield
                    P.tt("dve" if f4 % 2 else "pool", hT[p][:, f4 * 4:(f4 + 1) * 4, :].rearrange("p a b -> p (a b)"), rl[:], rl[:], ALU.mult, [rl], [hT[p]])
                for g0 in range(2):
                    pz = pz_()
                    for k in range(32):
                        P.mm(pz[:], hT[p][:, k, :], wf2[:, k, g0 * 512:(g0 + 1) * 512], k == 0, k == 31, [hT[p], wf2], [pz])
                    yield
                    P.tt("dve", h_[:, g0 * 512:(g0 + 1) * 512], pz[:], h_[:, g0 * 512:(g0 + 1) * 512], ALU.add, [pz, h_], [h_])
                yield
                yield from norm3(p, h_, 1)
                P.cp("pool", pb3[p][:], p_[:], [p_], [pb3[p]])
                yield
                for k in range(2):
                    P.tr(pT[:, k * 128:(k + 1) * 128], pb3[p][:, k * 128:(k + 1) * 128], identb[:], [pb3[p], identb], [pT], sig=(k == 1))
                yield
                P.cp("act", pT2[p][:].rearrange("p a b -> p (a b)"), pT[:, 0:256], [pT], [pT2[p]])
                for g0 in range(2):
                    pz = pz_()
                    for k in range(8):
                        P.mm(pz[:], nT[:, k, :], wpg[:, k, g0 * 512:(g0 + 1) * 512], k == 0, k == 7, [nT, wpg], [pz])
                    yield
                    P.act(gt_[:, g0 * 512:(g0 + 1) * 512], pz[:], AF.Sigmoid, [pz], [gt_])
                    pz2 = pz_()
                    for k in range(2):
                        P.mm(pz2[:], pT2[p][:, k, :], wpl[:, k, g0 * 512:(g0 + 1) * 512], k == 0, k == 1, [pT2[p], wpl], [pz2])
                    yield
                    P.tt("dve", gt_[:, g0 * 512:(g0 + 1) * 512], pz2[:], gt_[:, g0 * 512:(g0 + 1) * 512], ALU.mult, [pz2, gt_], [gt_])
                    yield
                    P.tt("pool", gt_[:, g0 * 512:(g0 + 1) * 512], gt_[:, g0 * 512:(g0 + 1) * 512], h_[:, g0 * 512:(g0 + 1) * 512], ALU.add, [gt_, h_], [gt_])
                P.dma("act", yout[c * 128:(c + 1) * 128, :], gt_[:], [gt_], (), sem=oq3[p])

            rr([chain_gens(tile3, range(0, nt, 2)), delayed_start(chain_gens(tile3, range(1, nt, 2)), 24)])
            P.barrier()
            P.emit()
    return nc


NT_FULL = 64
SEG_T_FULL = 16
_CACHE = {}


def make_in_maps(super_x, super_p, prompt_flags, W, nt, seg_t):
    rbG, rbT = expand_rel_bias(np.asarray(W["rel_bias"][0], np.float32))
    t8 = lambda v: np.tile(np.asarray(v, np.float32).reshape(-1), 8)
    vec512 = np.stack([W["w0_f"][0], W["w0_b"][0], W["a0"][0], W["k_k"][0], W["k_a"][0],
                       np.asarray(W["r_k"][0]).reshape(-1), W["ln_x_w"][0], W["ln_x_b"][0], t8(W["q_gain"][0])]).astype(np.float32)
    vec512b = t8(W["k_gain"][0])[None, :].astype(np.float32)
    gvec = np.stack([W["g_mix"][0], W["g_ffn"][0], W["g_ple"][0]]).astype(np.float32)
    w_up = np.stack([W["w_up_f"][0], W["w_up_b"][0], W["a_up"][0]]).astype(np.float32)
    shared = dict(w_in=W["w_in"][0], conv_w=W["conv_w"][0], vec512=vec512, vec512b=vec512b, gvec=gvec, w_up=w_up,
                  g_up=W["g_up"][0], w_a_out=W["w_a_out"][0], w_b_out=W["w_b_out"][0], w_o=W["w_o"][0],
                  w_ff1=W["w_ff1"][0], w_ff2=W["w_ff2"][0], w_ple=W["w_ple"][0], w_pgate=W["w_pgate"][0],
                  rbG=rbG, rbT=rbT)
    shared = {k: np.ascontiguousarray(np.asarray(v, np.float32)) for k, v in shared.items()}
    consts = {True: host_consts(nt, seg_t, True), False: host_consts(nt, seg_t, False)}
    maps = []
    for x, p, pf in zip(super_x, super_p, prompt_flags):
        m = dict(shared)
        hc = consts[bool(pf)]
        m.update(xs=np.ascontiguousarray(x, np.float32), pp=np.ascontiguousarray(p, np.float32),
                 flag=np.full((128, 1), 1.0 if pf else 0.0, np.float32),
                 c_ident=hc["ident"], c_cum=hc["cum"], c_amask=hc["amask"], c_nmask=hc["nmask"],
                 c_gm=hc["gm"], c_tm=hc["tm"], c_val=hc["val"])
        maps.append(m)
    return maps


def kernel(**inputs):
    nt, seg_t = NT_FULL, SEG_T_FULL
    xp = np.asarray(inputs["x_prompt"], np.float32)
    xsm = np.asarray(inputs["x_sample"], np.float32)
    pq = np.asarray(inputs["p_prompt"], np.float32)[0]
    psm = np.asarray(inputs["p_sample"], np.float32)[0]
    sx = [xp[0], xp[1]] + [xsm[4 * i:4 * i + 4].reshape(8192, D) for i in range(4)]
    sp_ = [pq[0], pq[1]] + [psm[4 * i:4 * i + 4].reshape(8192, 256) for i in range(4)]
    fl = [True, True, False, False, False, False]
    sx += [sx[4], sx[5]]
    sp_ += [sp_[4], sp_[5]]
    fl += [False, False]
    W = {k: np.asarray(v) for k, v in inputs.items() if k not in ("x_prompt", "x_sample", "p_prompt", "p_sample")}
    maps = make_in_maps(sx, sp_, fl, W, nt, seg_t)
    if "nc" not in _CACHE:
        _CACHE["nc"] = build(nt, seg_t)
    res = run_bass_kernel_spmd(_CACHE["nc"], maps, core_ids=list(range(NCORE)))
    outs = [np.asarray(r["yout"], np.float32) for r in res.results]
    y_prompt = np.stack([outs[0], outs[1]])
    y_sample = np.concatenate([outs[2 + i].reshape(4, 2048, D) for i in range(4)], 0)
    return (y_prompt, y_sample)
```

```python
import contextlib
import numpy as np
import concourse.bass as bass
import concourse.mybir as mybir
from concourse.bass_utils import run_bass_kernel_spmd

F32 = mybir.dt.float32
BF16 = mybir.dt.bfloat16
ALU = mybir.AluOpType
AF = mybir.ActivationFunctionType
AX = mybir.AxisListType

D = 1024
DS = 0.606531
EPS = 1e-6
GN_EPS = 64e-5
RWC = 1856
NCORE = 8


class Buf:
    __slots__ = ("t", "lw", "rd", "name")

    def __init__(self, t, name=""):
        self.t = t
        self.lw = None
        self.rd = {}
        self.name = name

    def __getitem__(self, k):
        return self.t[k]


class Prog:
    ENG = ("pe", "act", "dve", "pool", "sp")

    def __init__(self, nc, stack):
        self.nc = nc
        self.stack = stack
        self.q = {e: [] for e in self.ENG}
        self.sems = {}
        self.cnt = {}
        self.seen = {e: {} for e in self.ENG}
        for e in self.ENG:
            self.sems[e] = stack.enter_context(nc.semaphore("s_" + e))
            self.cnt[e] = 0
        self.n_inst = 0
        self.rr = 0
        self.dma_pool = {}
        self.dma_rr = {}
        self.DMA_POOL_SIZE = {"sp": 32, "act": 16, "pool": 8}

    def sb(self, shape, dt, name, stack=None):
        self.uid = getattr(self, "uid", 0) + 1
        name = "%s_%d" % (name, self.uid)
        t = (stack or self.stack).enter_context(self.nc.sbuf_tensor(name, list(shape), dt))
        return Buf(t, name)

    def ps(self, shape, dt, name, stack=None):
        self.uid = getattr(self, "uid", 0) + 1
        name = "%s_%d" % (name, self.uid)
        t = (stack or self.stack).enter_context(self.nc.psum_tensor(name, list(shape), dt))
        return Buf(t, name)

    def dma_sem(self, name):
        s = self.stack.enter_context(self.nc.semaphore(name))
        self.sems[name] = s
        self.cnt[name] = 0
        return name

    def _waits(self, eng, reads, writes):
        need = {}
        for b in reads:
            if b.lw is not None:
                k, v = b.lw
                need[k] = max(need.get(k, 0), v)
        for b in writes:
            if b.lw is not None:
                k, v = b.lw
                need[k] = max(need.get(k, 0), v)
            for k, v in b.rd.items():
                need[k] = max(need.get(k, 0), v)
        out = []
        seen = self.seen[eng]
        for k, v in need.items():
            if k == eng and eng in ("pe", "sp"):
                continue
            if seen.get(k, 0) < v:
                seen[k] = v
                out.append((k, v))
        return out

    def op(self, eng, fn, reads=(), writes=(), sig=True):
        waits = self._waits(eng, reads, writes)
        if sig:
            self.cnt[eng] += 1
            val = self.cnt[eng]
        else:
            val = self.cnt[eng] + 1
        sem = self.sems[eng]
        sems = self.sems

        def run(e):
            for k, v in waits:
                e.wait_ge(sems[k], v)
            ins = fn(e)
            if sig:
                ins.then_inc(sem, 1)
        self.q[eng].append(run)
        self.n_inst += 1
        for b in reads:
            b.rd[eng] = max(b.rd.get(eng, 0), val)
        for b in writes:
            b.lw = (eng, val)
            b.rd = {}

    def dma(self, eng, out_ap, in_ap, reads=(), writes=(), sem=None):
        pool = self.dma_pool.setdefault(eng, [])
        if not pool:
            for i in range(self.DMA_POOL_SIZE.get(eng, 8)):
                pool.append(self.dma_sem("dp_%s_%d" % (eng, i)))
            self.dma_rr[eng] = 0
        sem = pool[self.dma_rr[eng] % len(pool)]
        self.dma_rr[eng] += 1
        waits = self._waits(eng, reads, writes)
        prev = self.cnt[sem]
        if prev > 0 and self.seen[eng].get(sem, 0) < prev:
            self.seen[eng][sem] = prev
            waits = [w for w in waits if w[0] != sem] + [(sem, prev)]
        self.cnt[sem] += 16
        val = self.cnt[sem]
        s = self.sems[sem]
        sems = self.sems

        def run(e):
            for k, v in waits:
                e.wait_ge(sems[k], v)
            e.dma_start(out=out_ap, in_=in_ap).then_inc(s, 16)
        self.q[eng].append(run)
        self.n_inst += 1
        for b in reads:
            b.rd[sem] = max(b.rd.get(sem, 0), val)
        for b in writes:
            b.lw = (sem, val)
            b.rd = {}

    def barrier(self):
        tot = dict(self.cnt)
        sems = self.sems
        for eng in self.ENG:
            waits = []
            for k, v in tot.items():
                if v > 0 and k != eng and self.seen[eng].get(k, 0) < v:
                    self.seen[eng][k] = v
                    waits.append((k, v))

            def run(e, waits=waits):
                for k, v in waits:
                    e.wait_ge(sems[k], v)
            self.q[eng].append(run)

    def emit(self):
        nc = self.nc
        q = self.q
        with nc.Block() as block:
            @block.tensor
            def _(e):
                for f in q["pe"]:
                    f(e)

            @block.scalar
            def _(e):
                for f in q["act"]:
                    f(e)

            @block.vector
            def _(e):
                for f in q["dve"]:
                    f(e)

            @block.gpsimd
            def _(e):
                for f in q["pool"]:
                    f(e)

            @block.sync
            def _(e):
                for f in q["sp"]:
                    f(e)
        self.q = {e: [] for e in self.ENG}

    def tt(self, eng, out, a, b, op, R, W):
        self.op(eng, lambda e: e.tensor_tensor(out, a, b, op), R, W)

    def ts(self, eng, out, a, s1, s2, op0, op1, R, W):
        if s2 is None:
            self.op(eng, lambda e: e.tensor_scalar(out, a, s1, None, op0), R, W)
        else:
            self.op(eng, lambda e: e.tensor_scalar(out, a, s1, s2, op0, op1), R, W)

    def stt(self, eng, out, a, s, b, op0, op1, R, W):
        self.op(eng, lambda e: e.scalar_tensor_tensor(out, a, s, b, op0, op1), R, W)

    def act(self, out, a, func, R, W, scale=1.0, accum=None):
        if accum is None:
            self.op("act", lambda e: e.activation(out, a, func, scale=scale), R, W)
        else:
            self.op("act", lambda e: e.activation(out, a, func, scale=scale, accum_out=accum), R, W)

    def cp(self, eng, out, a, R, W):
        if eng == "act":
            self.op("act", lambda e: e.activation(out, a, AF.Copy), R, W)
        else:
            self.op(eng, lambda e: e.tensor_copy(out, a), R, W)

    def red(self, eng, out, a, R, W):
        self.op(eng, lambda e: e.reduce_sum(out, a, AX.X), R, W)

    def rcp(self, out, a, R, W):
        self.op("dve", lambda e: e.reciprocal(out, a), R, W)

    def mm(self, out, lhsT, rhs, start, stop, R, W, sig=None):
        if sig is None:
            sig = stop
        self.op("pe", lambda e: e.matmul(out, lhsT, rhs, start=start, stop=stop), R, W, sig=sig)

    def tr(self, out, a, ident, R, W, sig=True):
        self.op("pe", lambda e: e.transpose(out, a, ident), R, W, sig=sig)


def na_slots(i, nt, seg_t):
    m = i % seg_t
    if 2 <= m <= seg_t - 3:
        js = [i + 2 - s for s in range(5)]
        return js, "T", -4
    js = [i + 3 - s for s in range(7)]
    return js, "G", -6


def boundary_tiles(nt, seg_t):
    return [i for i in range(nt) if not (2 <= i % seg_t <= seg_t - 3)]


def host_consts(nt, seg_t, prompt):
    j = np.arange(128)[:, None]
    t = np.arange(128)[None, :]
    lt = (j < t).astype(np.float32)
    le = (j <= t).astype(np.float32)
    gt = (j > t).astype(np.float32)
    ge = (j >= t).astype(np.float32)
    cum = np.stack([lt, le, gt, ge], 1)
    amask = np.stack([np.concatenate([-lt, -le, lt, le], 1),
                      np.concatenate([-gt, -ge, gt, ge], 1)], 1)
    nmask = np.stack([-gt, -lt], 1)
    kc = np.arange(64)[:, None]
    qc = np.arange(64)[None, :]
    cs = np.clip(qc - 8, 0, 48)
    colm = ((kc >= cs) & (kc < cs + 16)).astype(np.float32)
    par = (np.arange(128) // 64)
    gm = np.zeros((128, 16, 64), np.float32)
    tm = np.zeros((128, 10, 64), np.float32)
    for p in range(128):
        for a, idx in enumerate(range(-7, 9)):
            dr = par[p] - idx
            if -7 <= dr <= 7:
                gm[p, a] = colm[p % 64]
        for a, idx in enumerate(range(-4, 6)):
            dr = par[p] - idx
            if -4 <= dr <= 3:
                tm[p, a] = colm[p % 64]
    bts = boundary_tiles(nt, seg_t)
    val = np.zeros((128, len(bts), 7, 2), np.float32)
    seq_t = nt if prompt else seg_t
    rows = 2 * seq_t
    for bi, i in enumerate(bts):
        s0 = (i // seq_t) * seq_t
        for s in range(7):
            jt = i + 3 - s
            if jt < 0 or jt >= nt or jt // seq_t != i // seq_t:
                continue
            for pr in range(2):
                for qp in range(2):
                    r = 2 * (i - s0) + qp
                    kr_ = 2 * (jt - s0) + pr
                    rs = min(max(r - 4, 0), rows - 8)
                    if rs <= kr_ < rs + 8:
                        val[pr * 64:(pr + 1) * 64, bi, s, qp] = 1.0
    return dict(cum=cum, amask=amask, nmask=nmask, gm=gm, tm=tm, val=val,
                ident=np.eye(128, dtype=np.float32))


def expand_rel_bias(rb):
    par = (np.arange(128) // 64)[:, None, None]
    kc = (np.arange(128) % 64)[:, None, None]
    qc = np.arange(64)[None, None, :]
    dc = np.clip(kc - qc + 15, 0, 30)

    def tab(idxs):
        idx = np.array(list(idxs))[None, :, None]
        dr = np.clip(par - idx + 7, 0, 14)
        return np.ascontiguousarray(np.transpose(rb[:, dr, dc], (1, 0, 2, 3)))
    return tab(range(-7, 9)), tab(range(-4, 6))


def rr(gens):
    gens = [g for g in gens if g is not None]
    while gens:
        nxt = []
        for g in gens:
            try:
                next(g)
                nxt.append(g)
            except StopIteration:
                pass
        gens = nxt


def fast(gen, n):
    while True:
        for _ in range(n):
            try:
                next(gen)
            except StopIteration:
                return
        yield


def delayed_start(gen, n):
    for _ in range(n):
        yield
    yield from gen


def chain_gens(fn, items):
    for it_ in items:
        yield from fn(it_)


def build(nt, seg_t, dbg=False):
    S = nt * 128
    nc = bass.Bass("TRN2", target_bir_lowering=False)
    dr = lambda n, sh, kind="ExternalInput": nc.dram_tensor(n, list(sh), F32, kind=kind).ap()
    xs = dr("xs", [S, D])
    pp = dr("pp", [S, 256])
    flag = dr("flag", [128, 1])
    w_in = dr("w_in", [D, 5440])
    conv_w = dr("conv_w", [3, RWC])
    vec512 = dr("vec512", [9, 512])
    vec512b = dr("vec512b", [1, 512])
    gvec = dr("gvec", [3, D])
    w_up = dr("w_up", [3, 64, 512])
    g_up = dr("g_up", [128, 512])
    w_a_out = dr("w_a_out", [512, D])
    w_b_out = dr("w_b_out", [512, D])
    w_o = dr("w_o", [D, D])
    w_ff1 = dr("w_ff1", [D, 4096])
    w_ff2 = dr("w_ff2", [4096, D])
    w_ple = dr("w_ple", [256, D])
    w_pgate = dr("w_pgate", [D, D])
    nb_t = len(boundary_tiles(nt, seg_t))
    c_ident = dr("c_ident", [128, 128])
    c_cum = dr("c_cum", [128, 4, 128])
    c_amask = dr("c_amask", [128, 2, 512])
    c_nmask = dr("c_nmask", [128, 2, 128])
    c_gm = dr("c_gm", [128, 16, 64])
    c_tm = dr("c_tm", [128, 10, 64])
    c_val = dr("c_val", [128, nb_t, 7, 2])
    rbG = dr("rbG", [128, 8, 16, 64])
    rbT = dr("rbT", [128, 8, 10, 64])
    yout = dr("yout", [S, D], "ExternalOutput")
    scr_u = dr("scr_u", [nt, 128, RWC], "ExternalOutput" if dbg else "Internal")
    scr_y = dr("scr_y", [nt, 128, 512], "Internal")
    scr_h = dr("scr_h", [nt, 128, D], "Internal")
    scr_yrw = dr("scr_yrw", [nt, 128, 512], "Internal")
    scr_ya = dr("scr_ya", [nt, 128, 512], "Internal")

    with contextlib.ExitStack() as st0:
        P = Prog(nc, st0)
        ldq = [None] * 4
        cq = wq = stq = None
        s2q = [None] * 2

        def newsem():
            return None

        identf = P.sb([128, 128], F32, "identf")
        identb = P.sb([128, 128], BF16, "identb")
        ones1 = P.sb([128, 1], F32, "ones1")
        flg = P.sb([128, 1], F32, "flg")
        wup = gup = None
        P.dma("sp", identf[:], c_ident, writes=[identf], sem=cq)
        P.dma("sp", flg[:], flag, writes=[flg], sem=cq)
        P.cp("dve", identb[:], identf[:], [identf], [identb])
        P.op("pool", lambda e: e.memset(ones1[:], 1.0), (), [ones1])
        W0F, W0B, A0, KK, KA, RK, LNW, LNB, QG = range(9)
        stages = [None, None]
        gv = None
        vb = vb2 = cum = amask = nmask = None

        def alloc_stages(st_, n=2):
            del stages[:]
            for i_ in range(n):
                stages.append(P.sb([128, 2048], F32, "stage", st_))

        def alloc_gv(st_, rows=(0, 1, 2)):
            g_ = P.sb([128, len(rows), D], F32, "gv", st_)
            for i_, r in enumerate(rows):
                P.dma("sp", g_[:, i_, :], gvec[r, :].partition_broadcast(128), writes=[g_], sem=cq)
            return g_

        def small_consts(st_, need2=False, rw=True):
            vb_ = P.sb([128, 9, 512], F32, "vb", st_)
            vb2_ = P.sb([128, 512 if need2 else 1], F32, "vb2", st_)
            cum_ = amask_ = nmask_ = None
            if rw:
                cum_ = P.sb([128, 4, 128], F32, "cum", st_)
                amask_ = P.sb([128, 2, 512], F32, "amask", st_)
                nmask_ = P.sb([128, 2, 128], F32, "nmask", st_)
                P.dma("sp", cum_[:], c_cum, writes=[cum_], sem=cq)
                P.dma("sp", amask_[:], c_amask, writes=[amask_], sem=cq)
                P.dma("sp", nmask_[:], c_nmask, writes=[nmask_], sem=cq)
            for r in range(9):
                P.dma("sp", vb_[:, r, :], vec512[r, :].partition_broadcast(128), writes=[vb_], sem=cq)
            if need2:
                P.dma("sp", vb2_[:], vec512b[0, :].partition_broadcast(128), writes=[vb2_], sem=cq)
            return vb_, vb2_, cum_, amask_, nmask_

        wcnt = [0]

        def load_w(dst, _unused, src, kchunks, ncols, col0=0):
            for k in range(kchunks):
                for c0 in range(0, ncols, 2048):
                    cwid = min(2048, ncols - c0)
                    sg = stages[wcnt[0] % len(stages)]
                    wcnt[0] += 1
                    P.dma("sp" if wcnt[0] % 2 else "act", sg[:, 0:cwid], src[k * 128:(k + 1) * 128, col0 + c0:col0 + c0 + cwid], writes=[sg], sem=wq)
                    eng = "act" if (wcnt[0] % 2) else "dve"
                    P.cp(eng, dst[:, k, c0:c0 + cwid], sg[:, 0:cwid], [sg], [dst])

        def rmsnorm_T(st_bufs, xt, grow, nT):
            sq, ss, nb, pT = st_bufs
            P.act(nb[:], xt[:], AF.Square, [xt], [nb, ss], accum=ss[:])
            P.ts("dve", ss[:], ss[:], 1.0 / D, EPS, ALU.mult, ALU.add, [ss], [ss])
            P.act(ss[:], ss[:], AF.Ln, [ss], [ss])
            P.act(ss[:], ss[:], AF.Exp, [ss], [ss], scale=-0.5)
            P.stt("dve", nb[:], xt[:], ss[:], gv[:, grow, :], ALU.mult, ALU.mult, [xt, ss, gv], [nb])
            for c in range(8):
                P.tr(pT[:, c * 128:(c + 1) * 128], nb[:, c * 128:(c + 1) * 128], identb[:], [nb, identb], [pT], sig=(c == 7))
            P.cp("act", nT[:].rearrange("p a b -> p (a b)"), pT[:], [pT], [nT])

        with contextlib.ExitStack() as st:
            alloc_stages(st)
            gv = alloc_gv(st)
            wrw = P.sb([128, 8, RWC], BF16, "wrw", st)
            load_w(wrw, None, w_in, 8, RWC, col0=1536)
            cw = P.sb([128, 3, RWC], F32, "cw", st)
            for r in range(3):
                P.dma("sp", cw[:, r, :], conv_w[r, :].partition_broadcast(128), writes=[cw], sem=cq)
            xt = [P.sb([128, D], F32, "xt%d" % i, st) for i in range(2)]
            ssA = [P.sb([128, 1], F32, "ss%d" % i, st) for i in range(2)]
            nbA = [P.sb([128, D], BF16, "nb%d" % i, st) for i in range(2)]
            nTA = [P.sb([128, 8, 128], BF16, "nT%d" % i, st) for i in range(2)]
            z = [P.sb([128, RWC], F32, "z%d" % i, st) for i in range(6)]
            zp = [P.sb([128, RWC], F32, "zp%d" % i, st) for i in range(2)]
            zn = [P.sb([128, RWC], F32, "zn%d" % i, st) for i in range(2)]
            tB = [P.sb([1, RWC], F32, "tB%d" % i, st) for i in range(2)]
            pTA = [P.ps([128, 1024], BF16, "pTA%d" % i, st) for i in range(2)]
            pzA = [[P.ps([128, 512], F32, "pzA%d_%d" % (i, j), st) for j in range(3)] for i in range(2)]
            xq = [newsem() for _ in range(2)]
            shs = [newsem() for _ in range(2)]
            uq = [newsem() for _ in range(2)]

            def tileA(c):
                p = c % 2
                x_ = xt[p]
                P.dma("sp", x_[:], xs[c * 128:(c + 1) * 128, :], (), [x_], sem=xq[p])
                yield
                ss, nb, nT, pT = ssA[p], nbA[p], nTA[p], pTA[p]
                P.act(nb[:], x_[:], AF.Square, [x_], [nb, ss], accum=ss[:])
                P.ts("dve", ss[:], ss[:], 1.0 / D, EPS, ALU.mult, ALU.add, [ss], [ss])
                yield
                P.act(ss[:], ss[:], AF.Ln, [ss], [ss])
                P.act(ss[:], ss[:], AF.Exp, [ss], [ss], scale=-0.5)
                yield
                P.stt("dve", nb[:], x_[:], ss[:], gv[:, 0, :], ALU.mult, ALU.mult, [x_, ss, gv], [nb])
                for k in range(8):
                    P.tr(pT[:, k * 128:(k + 1) * 128], nb[:, k * 128:(k + 1) * 128], identb[:], [nb, identb], [pT], sig=(k == 7))
                yield
                P.cp("act", nT[:].rearrange("p a b -> p (a b)"), pT[:], [pT], [nT])
                zc = z[c % 6]
                for gi, g0 in enumerate(range(0, RWC, 512)):
                    gw = min(512, RWC - g0)
                    pz = pzA[p][gi % 3]
                    for k in range(8):
                        P.mm(pz[:, 0:gw], nT[:, k, :], wrw[:, k, g0:g0 + gw], k == 0, k == 7, [nT, wrw], [pz])
                    yield
                    P.cp("act", zc[:, g0:g0 + gw], pz[:, 0:gw], [pz], [zc])

            zpT = [[Buf(zp[i].t, "zpT") for _ in range(3)] for i in range(2)]
            znT = [[Buf(zn[i].t, "znT") for _ in range(3)] for i in range(2)]

            def convA(c):
                p = c % 2
                zc = z[c % 6]
                zp_, zn_, tB_ = zp[p], zn[p], tB[p]
                zpt, znt = zpT[p], znT[p]
                first = (c == 0)
                last = (c == nt - 1)
                segs = (c % seg_t == 0)
                sege = (c % seg_t == seg_t - 1)
                P.dma("sp", zp_[1:113, :], zc[0:112, :], [zc], [zpt[0]])
                P.dma("sp", zp_[113:128, :], zc[112:127, :], [zc], [zpt[1]])
                if first:
                    P.op("pool", lambda e: e.memset(zp_[0:1, :], 0.0), (), [zpt[2]])
                else:
                    P.dma("sp", zp_[0:1, :], z[(c - 1) % 6][127:128, :], [z[(c - 1) % 6]], [zpt[2]])
                P.dma("sp", zn_[0:112, :], zc[1:113, :], [zc], [znt[0]])
                P.dma("sp", zn_[112:127, :], zc[113:128, :], [zc], [znt[1]])
                if last:
                    P.op("pool", lambda e: e.memset(tB_[:], 0.0), (), [tB_])
                elif sege:
                    P.ts("dve", tB_[:], z[(c + 1) % 6][0:1, :], flg[0:1, 0:1], None, ALU.mult, None, [z[(c + 1) % 6], flg], [tB_])
                else:
                    P.cp("dve", tB_[:], z[(c + 1) % 6][0:1, :], [z[(c + 1) % 6]], [tB_])
                P.dma("sp", zn_[127:128, :], tB_[:], [tB_], [znt[2]])
                yield
                if segs and not first:
                    P.ts("dve", zp_[0:1, :], zp_[0:1, :], flg[0:1, 0:1], None, ALU.mult, None, [zpt[2], flg], [zpt[2]])
                P.tt("dve", zn_[:], zn_[:], cw[:, 2, :], ALU.mult, znt + [cw], znt)
                P.tt("dve", zp_[:], zp_[:], cw[:, 0, :], ALU.mult, zpt + [cw], zpt)
                yield
                P.tt("dve", zn_[:], zn_[:], zp_[:], ALU.add, znt + zpt, znt)
                P.tt("dve", zp_[:], zc[:], cw[:, 1, :], ALU.mult, [zc, cw] + zpt, zpt)
                yield
                P.tt("dve", zn_[:], zn_[:], zp_[:], ALU.add, znt + zpt, znt)
                P.dma("act", scr_u[c], zn_[:], znt, ())

            tA = lambda c: tileA(c) if 0 <= c < nt else None
            cA = lambda c: convA(c) if 0 <= c < nt else None
            rr([tA(0), tA(1)])
            rr([tA(2)])
            for t0 in range(0, nt, 2):
                rr([tA(t0 + 3), tA(t0 + 4), cA(t0), cA(t0 + 1)])
            P.barrier()
            P.emit()

        half = nt // 2
        with contextlib.ExitStack() as st:
            vb, vb2, cum, amask, nmask = small_consts(st)
            wup = P.sb([64, 3, 512], BF16, "wup", st)
            gup = P.sb([128, 512], BF16, "gup", st)
            with contextlib.ExitStack() as st_w:
                alloc_stages(st_w)
                P.dma("sp", stages[0][0:64, 0:1536].rearrange("p (a b) -> p a b", a=3), w_up.rearrange("a p b -> p a b"), writes=[stages[0]], sem=wq)
                P.cp("dve", wup[:].rearrange("p a b -> p (a b)"), stages[0][0:64, 0:1536], [stages[0]], [wup])
                P.dma("sp", stages[1][:, 0:512], g_up, writes=[stages[1]], sem=wq)
                P.cp("dve", gup[:], stages[1][:, 0:512], [stages[1]], [gup])
                P.barrier()
                P.emit()
            SD = []
            for d in range(2):
                X = {}
                X["u"] = P.sb([128, RWC], F32, "u", st)
                X["t320"] = P.sb([128, 256], BF16, "t320", st)
                X["lT"] = P.sb([128, 3, 128], BF16, "lT", st)
                X["sg"] = P.sb([128, 512], F32, "sg", st)
                X["a"] = P.sb([128, 512], F32, "a_", st)
                X["ex"] = [P.sb([128, 512], F32, "ex%d" % i, st) for i in range(2)]
                X["kk"] = P.sb([128, 512], F32, "kk", st)
                X["tmp"] = P.sb([128, 512], F32, "tmp", st)
                X["s8"] = P.sb([128, 8], F32, "s8", st)
                X["bs"] = P.sb([128, 8], F32, "bs", st)
                X["kt"] = P.sb([128, 512], F32, "kt", st)
                X["beta"] = P.sb([128, 512], F32, "beta", st)
                X["tmb"] = P.sb([128, 4, 512], BF16, "tmb", st)
                X["fm"] = P.sb([64, 8, 4, 128], BF16, "fm", st)
                X["Vb"] = [P.sb([128, 512], BF16, "Vb%d" % i, st) for i in range(2)]
                X["KH"] = [P.sb([128, 512], BF16, "KH%d" % i, st) for i in range(2)]
                X["BH"] = [P.sb([128, 512], BF16, "BH%d" % i, st) for i in range(2)]
                X["eLC"] = [P.sb([64, 8], F32, "eLC%d" % i, st) for i in range(2)]
                X["g"] = [P.sb([128, 512], F32, "g%d" % i, st) for i in range(2)]
                X["bv"] = [P.sb([128, 512], F32, "bv%d" % i, st) for i in range(2)]
                yt = [P.sb([128, 512], F32, "y%d" % i, st) for i in range(2)]
                X["y"] = yt
                X["yh"] = [[Buf(yt[i].t, "yh") for _ in range(2)] for i in range(2)]
                X["yo"] = P.sb([128, 512], F32, "yo", st)
                X["oc"] = P.sb([128, 512], F32, "oc", st)
                X["s8c"] = P.sb([128, 8], F32, "s8c", st)
                X["s8d"] = P.sb([128, 8], F32, "s8d", st)
                X["yrw"] = X["oc"]
                X["prp"] = P.ps([128, 512], F32, "prp", st)
                X["pT"] = P.ps([128, 1024], BF16, "pTd", st)
                X["uq"] = newsem()
                X["yq"] = newsem()
                X["sq"] = newsem()
                X["rq"] = newsem()
                X["hg"] = []
                for hg in range(2):
                    G = {}
                    G["AG"] = P.sb([128, 4, 4, 128], BF16, "AG", st)
                    G["PN"] = [P.sb([128, 4, 128], BF16, "PN%d" % i, st) for i in range(2)]
                    G["PX"] = [P.sb([128, 4, 128], BF16, "PX%d" % i, st) for i in range(2)]
                    G["ZN"] = P.sb([128, 4, 128], BF16, "ZN", st)
                    G["ZX"] = P.sb([128, 4, 128], BF16, "ZX", st)
                    G["RHS"] = P.sb([128, 4, 64], BF16, "RHS", st)
                    G["U"] = P.sb([128, 4, 64], BF16, "U", st)
                    G["Hf"] = P.sb([64, 4, 64], F32, "Hf", st)
                    G["Hb"] = P.sb([64, 4, 64], BF16, "Hb", st)
                    G["ps"] = P.ps([128, 512], F32, "pinv", st)
                    X["hg"].append(G)
                SD.append(X)
            scrY = [Buf(None, "scrY%d" % c) for c in range(nt)]

            def tile_of(d, k):
                return k if d == 0 else nt - 1 - k

            def delayed(gen, n):
                for _ in range(n):
                    yield
                yield from gen

            def is_second(d, c):
                return (c >= half) if d == 0 else (c < half)

            def prepA(d, k):
                X = SD[d]
                c = tile_of(d, k)
                par = k % 2
                comb = is_second(d, c)
                u, t320, lT, sg, a_, ex = X["u"], X["t320"], X["lT"], X["sg"], X["a"], X["ex"]
                kk, tmp, s8, kt, beta, tmb = X["kk"], X["tmp"], X["s8"], X["kt"], X["beta"], X["tmb"]
                prp, pT = X["prp"], X["pT"]
                P.dma("sp", u[:], scr_u[c], (), [u], sem=X["uq"])
                yield
                r_ = u[:, 0:512]
                k_ = u[:, 512:1024]
                v_ = u[:, 1024:1536]
                P.act(t320[:, 0:64], u[:, 1536 + 64 * d:1600 + 64 * d], AF.Tanh, [u], [t320])
                P.act(t320[:, 64:128], u[:, 1664:1728], AF.Copy, [u], [t320])
                if comb:
                    P.act(t320[:, 128:256], u[:, 1728:1856], AF.Sigmoid, [u], [t320])
                yield
                P.tr(pT[0:64, 0:128], t320[:, 0:64], identb[:], [t320, identb], [pT], sig=False)
                P.tr(pT[0:64, 128:256], t320[:, 64:128], identb[:], [t320, identb], [pT], sig=not comb)
                if comb:
                    P.tr(pT[:, 256:384], t320[:, 128:256], identb[:], [t320, identb], [pT])
                yield
                if comb:
                    P.cp("dve", lT[:].rearrange("p a b -> p (a b)"), pT[:, 0:384], [pT], [lT])
                else:
                    P.cp("dve", lT[0:64, 0:2, :].rearrange("p a b -> p (a b)"), pT[0:64, 0:256], [pT], [lT])
                yield
                P.mm(prp[:], lT[0:64, 0, :], wup[:, d, :], True, True, [lT, wup], [prp])
                yield
                P.tt("dve", tmp[:], prp[:], vb[:, W0F + d, :], ALU.add, [prp, vb], [tmp])
                yield
                P.act(sg[:], tmp[:], AF.Sigmoid, [tmp], [sg])
                P.mm(prp[:], lT[0:64, 1, :], wup[:, 2, :], True, True, [lT, wup], [prp])
                yield
                P.tt("dve", tmp[:], prp[:], vb[:, A0, :], ALU.add, [prp, vb], [tmp])
                yield
                P.act(a_[:], tmp[:], AF.Sigmoid, [tmp], [a_])
                P.tt("pool", kk[:], k_, vb[:, KK, :], ALU.mult, [u, vb], [kk])
                yield
                P.act(tmp[:], kk[:], AF.Square, [kk], [tmp])
                yield
                P.red("dve", s8[:], tmp[:].rearrange("p (a b) -> p a b", a=8), [tmp], [s8])
                yield
                P.act(s8[:], s8[:], AF.Ln, [s8], [s8])
                P.act(s8[:], s8[:], AF.Exp, [s8], [s8], scale=-0.5)
                yield
                P.ts("dve", s8[:], s8[:], 1e12, None, ALU.min, None, [s8], [s8])
                yield
                kk3 = kk[:].rearrange("p (a b) -> p a b", a=8)
                P.tt("dve", kk3, kk3, s8[:].unsqueeze(2).to_broadcast([128, 8, 64]), ALU.mult, [kk, s8], [kk])
                P.stt("dve", tmp[:], a_[:], -1.0, vb[:, KA, :], ALU.add, ALU.mult, [a_, vb], [tmp])
                yield
                P.stt("dve", kt[:], tmp[:], 1.0, k_, ALU.add, ALU.mult, [tmp, u], [kt])
                P.tt("pool", beta[:], kk[:], a_[:], ALU.mult, [kk, a_], [beta])
                yield
                m1, m2, m4 = (0, 1, 2) if d == 0 else (2, 3, 0)
                P.mm(prp[:], cum[:, m1, :], sg[:], True, True, [cum, sg], [prp])
                yield
                P.act(ex[0][:], prp[:], AF.Exp, [prp], [ex[0]], scale=-DS)
                yield
                P.mm(prp[:], cum[:, m2, :], sg[:], True, True, [cum, sg], [prp])
                P.tt("dve", tmb[:, 0, :], kk[:], ex[0][:], ALU.mult, [kk, ex[0]], [tmb])
                yield
                P.act(ex[1][:], prp[:], AF.Exp, [prp], [ex[1]], scale=-DS)
                P.act(ex[0][:], prp[:], AF.Exp, [prp], [ex[0]], scale=DS)
                yield
                P.mm(prp[:], cum[:, m4, :], sg[:], True, True, [cum, sg], [prp])
                P.tt("pool", tmb[:, 1, :], r_, ex[1][:], ALU.mult, [u, ex[1]], [tmb])
                P.tt("dve", tmb[:, 2, :], kt[:], ex[0][:], ALU.mult, [kt, ex[0]], [tmb])
                yield
                P.tt("pool", tmb[:, 3, :], beta[:], ex[0][:], ALU.mult, [beta, ex[0]], [tmb])
                P.act(ex[1][:], prp[:], AF.Exp, [prp], [ex[1]], scale=-DS)
                yield
                for h in range(8):
                    P.mm(prp[0:64, h:h + 1], sg[:, h * 64:(h + 1) * 64], ones1[:], True, True, [sg, ones1], [prp], sig=(h == 7))
                P.tt("dve", X["KH"][par][:], kt[:], ex[1][:], ALU.mult, [kt, ex[1]], [X["KH"][par]])
                yield
                P.act(X["eLC"][par][:], prp[0:64, 0:8], AF.Exp, [prp], [X["eLC"][par]], scale=-DS)
                P.stt("dve", X["BH"][par][:], beta[:], -1.0, ex[1][:], ALU.mult, ALU.mult, [beta, ex[1]], [X["BH"][par]])
                yield
                P.cp("act", X["Vb"][par][:], v_, [u], [X["Vb"][par]])
                if comb:
                    P.tt("pool", tmp[:], r_, kt[:], ALU.mult, [u, kt], [tmp])
                    yield
                    P.tt("pool", tmp[:], tmp[:], vb[:, RK, :], ALU.mult, [tmp, vb], [tmp])
                    yield
                    P.red("dve", X["bs"][:], tmp[:].rearrange("p (a b) -> p a b", a=8), [tmp], [X["bs"]])

            def commit(d, k):
                X = SD[d]
                c = tile_of(d, k)
                par = k % 2
                comb = is_second(d, c)
                tmb, fm, pT, prp = X["tmb"], X["fm"], X["pT"], X["prp"]
                for hp in range(4):
                    for hh in range(2):
                        h = hp * 2 + hh
                        for m in range(4):
                            P.tr(pT[0:64, (hh * 4 + m) * 128:(hh * 4 + m + 1) * 128], tmb[:, m, h * 64:(h + 1) * 64],
                                 identb[:], [tmb, identb], [pT], sig=(hh == 1 and m == 3))
                    yield
                    P.cp("act" if hp % 2 else "dve", fm[:, hp * 2:hp * 2 + 2, :, :].rearrange("p a b c -> p (a b c)"),
                         pT[0:64, :], [pT], [fm])
                if comb:
                    P.mm(prp[:], X["lT"][:, 2, :], gup[:], True, True, [X["lT"], gup], [prp])
                    P.cp("act", X["g"][par][:], prp[:], [prp], [X["g"][par]])
                    P.tt("pool", X["bv"][par][:].rearrange("p (a b) -> p a b", a=8),
                         X["u"][:, 1024:1536].rearrange("p (a b) -> p a b", a=8),
                         X["bs"][:].unsqueeze(2).to_broadcast([128, 8, 64]), ALU.mult, [X["u"], X["bs"]], [X["bv"][par]])

            def invchain(d, hg, k):
                X = SD[d]
                G = X["hg"][hg]
                c = tile_of(d, k)
                par = k % 2
                fm, Vb, KH, BH, eLC = X["fm"], X["Vb"][par], X["KH"][par], X["BH"][par], X["eLC"][par]
                AG, PNs, PXs, ZN, ZX, RHS, U = G["AG"], G["PN"], G["PX"], G["ZN"], G["ZX"], G["RHS"], G["U"]
                Hf, Hb, ps = G["Hf"], G["Hb"], G["ps"]
                yb = X["yh"][par][hg]
                ytile = X["y"][par]
                hs = [hg * 4 + i for i in range(4)]
                if k == 0:
                    P.op("pool", lambda e: e.memset(Hf[:], 0.0), (), [Hf])
                    P.op("pool", lambda e: e.memset(Hb[:], 0.0), (), [Hb])
                else:
                    joint = (c % seg_t == 0) if d == 0 else (c % seg_t == seg_t - 1)
                    if joint:
                        P.ts("dve", Hf[:], Hf[:], flg[0:64, 0:1], None, ALU.mult, None, [Hf, flg], [Hf])
                        P.cp("act", Hb[:], Hf[:], [Hf], [Hb])
                for i, h in enumerate(hs):
                    P.mm(ps[:, 0:256], fm[:, h, 3, :], fm[:, h, 0:2, :].rearrange("p a b -> p (a b)"), True, True, [fm], [ps], sig=False)
                    P.mm(ps[:, 256:512], fm[:, h, 2, :], fm[:, h, 0:2, :].rearrange("p a b -> p (a b)"), True, True, [fm], [ps])
                    yield
                    P.tt("dve", AG[:, i, :, :].rearrange("p a b -> p (a b)"), ps[:], amask[:, d, :], ALU.mult, [ps, amask], [AG])
                for i, h in enumerate(hs):
                    P.mm(ps[:, i * 128:(i + 1) * 128], fm[:, h, 0, :], fm[:, h, 3, :], True, True, [fm], [ps], sig=(i == 3))
                yield
                curN = PNs[0]
                P.tt("dve", curN[:], ps[:].rearrange("p (a b) -> p a b", a=4),
                     nmask[:, d, :].unsqueeze(1).to_broadcast([128, 4, 128]), ALU.mult, [ps, nmask], [curN])
                P.tt("pool", ZX[:], AG[:, :, 0, :], identb[:].unsqueeze(1).to_broadcast([128, 4, 128]), ALU.add, [AG, identb], [ZX])
                yield
                for i, h in enumerate(hs):
                    P.mm(ps[:, i * 64:(i + 1) * 64], fm[:, h, 0, :], Hb[:, i, :], True, False, [fm, Hb], [ps], sig=False)
                    P.mm(ps[:, i * 64:(i + 1) * 64], AG[:, i, 2, :], Vb[:, h * 64:(h + 1) * 64], False, True, [AG, Vb], [ps], sig=False)
                for i, h in enumerate(hs):
                    P.mm(ps[:, 256 + i * 64:256 + (i + 1) * 64], fm[:, h, 1, :], Hb[:, i, :], True, False, [fm, Hb], [ps], sig=False)
                    P.mm(ps[:, 256 + i * 64:256 + (i + 1) * 64], AG[:, i, 3, :], Vb[:, h * 64:(h + 1) * 64], False, True, [AG, Vb], [ps], sig=(i == 3))
                yield
                P.cp("act", RHS[:].rearrange("p a b -> p (a b)"), ps[:, 0:256], [ps], [RHS])
                P.cp("act", ytile[:, hg * 256:(hg + 1) * 256], ps[:, 256:512], [ps], [yb])
                yield
                for i, h in enumerate(hs):
                    P.mm(ps[0:64, i * 64:(i + 1) * 64], KH[:, h * 64:(h + 1) * 64], Vb[:, h * 64:(h + 1) * 64], True, True, [KH, Vb], [ps], sig=(i == 3))
                P.tt("pool", Hf[:], Hf[:], eLC[:, hg * 4:(hg + 1) * 4].unsqueeze(2).to_broadcast([64, 4, 64]), ALU.mult, [Hf, eLC], [Hf])
                yield
                P.tt("dve", Hf[:], Hf[:], ps[0:64, 0:256].rearrange("p (a b) -> p a b", a=4), ALU.add, [Hf, ps], [Hf])
                yield
                curX_ap = lambda i: AG[:, i, 0, :]
                curX_buf = AG
                for lvl in range(1, 7):
                    both = lvl < 6
                    need_side = "N" if (lvl % 2 == 1) else "X"
                    newN = PNs[lvl % 2]
                    newX = PXs[lvl % 2]
                    doX = both or need_side == "X"
                    doN = both or need_side == "N"
                    if doX:
                        for i in range(4):
                            P.mm(ps[:, i * 128:(i + 1) * 128], curN[:, i, :], curX_ap(i), True, True, [curN, curX_buf], [ps], sig=(i == 3))
                        yield
                        P.cp("act", newX[:].rearrange("p a b -> p (a b)"), ps[:], [ps], [newX])
                    if doN:
                        for i in range(4):
                            P.mm(ps[:, i * 128:(i + 1) * 128], curX_ap(i), curN[:, i, :], True, True, [curN, curX_buf], [ps], sig=(i == 3))
                        yield
                        P.cp("dve", newN[:].rearrange("p a b -> p (a b)"), ps[:], [ps], [newN])
                    curN = newN
                    curX_buf = newX
                    curX_ap = (lambda nx: (lambda i: nx[:, i, :]))(newX)
                    if need_side == "N":
                        for i in range(4):
                            P.mm(ps[:, i * 128:(i + 1) * 128], ZX[:, i, :], newN[:, i, :], True, False, [ZX, newN], [ps], sig=False)
                            P.mm(ps[:, i * 128:(i + 1) * 128], ZX[:, i, :], identb[:], False, True, [ZX, identb], [ps], sig=(i == 3))
                        yield
                        P.cp("act", ZN[:].rearrange("p a b -> p (a b)"), ps[:], [ps], [ZN])
                    else:
                        for i in range(4):
                            P.mm(ps[:, i * 128:(i + 1) * 128], ZN[:, i, :], newX[:, i, :], True, False, [ZN, newX], [ps], sig=False)
                            P.mm(ps[:, i * 128:(i + 1) * 128], ZN[:, i, :], identb[:], False, True, [ZN, identb], [ps], sig=(i == 3))
                        yield
                        P.cp("dve", ZX[:].rearrange("p a b -> p (a b)"), ps[:], [ps], [ZX])
                for i in range(4):
                    P.mm(ps[:, i * 64:(i + 1) * 64], ZX[:, i, :], RHS[:, i, :], True, True, [ZX, RHS], [ps], sig=(i == 3))
                yield
                P.cp("dve", U[:].rearrange("p a b -> p (a b)"), ps[:, 0:256], [ps], [U])
                yield
                for i, h in enumerate(hs):
                    P.mm(ps[:, i * 64:(i + 1) * 64], AG[:, i, 1, :], U[:, i, :], True, True, [AG, U], [ps], sig=False)
                for i, h in enumerate(hs):
                    P.mm(ps[0:64, 256 + i * 64:256 + (i + 1) * 64], BH[:, h * 64:(h + 1) * 64], U[:, i, :], True, True, [BH, U], [ps], sig=(i == 3))
                yield
                P.tt("dve", ytile[:, hg * 256:(hg + 1) * 256], ytile[:, hg * 256:(hg + 1) * 256], ps[:, 0:256], ALU.add, [ps, yb], [yb])
                P.tt("dve", Hf[:], Hf[:], ps[0:64, 256:512].rearrange("p (a b) -> p a b", a=4), ALU.add, [Hf, ps], [Hf])
                yield
                P.cp("act", Hb[:], Hf[:], [Hf], [Hb])

            def combine(d, k):
                X = SD[d]
                c = tile_of(d, k)
                par = k % 2
                y = X["y"][par]
                ybs = X["yh"][par]
                if not is_second(d, c):
                    P.dma("act", scr_y[c], y[:], ybs, [scrY[c]], sem=X["sq"])
                    return
                yo, oc, s8, s8b, yrw = X["yo"], X["oc"], X["s8c"], X["s8d"], X["yrw"]
                g_, bv = X["g"][par], X["bv"][par]
                P.dma("sp", yo[:], scr_y[c], [scrY[c]], [yo], sem=X["yq"])
                yield
                P.tt("dve", yo[:], yo[:], y[:], ALU.add, [yo] + ybs, [yo])
                yield
                y3 = yo[:].rearrange("p (a b) -> p a b", a=8)
                oc3 = oc[:].rearrange("p (a b) -> p a b", a=8)
                P.red("dve", s8[:], y3, [yo], [s8])
                yield
                P.ts("dve", s8[:], s8[:], 1.0 / 64, None, ALU.mult, None, [s8], [s8])
                yield
                P.tt("dve", oc3, y3, s8[:].unsqueeze(2).to_broadcast([128, 8, 64]), ALU.subtract, [yo, s8], [oc])
                yield
                P.act(yo[:], oc[:], AF.Square, [oc], [yo])
                yield
                P.red("dve", s8b[:], yo[:].rearrange("p (a b) -> p a b", a=8), [yo], [s8b])
                yield
                P.ts("dve", s8b[:], s8b[:], 1.0 / 64, GN_EPS, ALU.mult, ALU.add, [s8b], [s8b])
                yield
                P.act(s8b[:], s8b[:], AF.Ln, [s8b], [s8b])
                yield
                P.act(s8b[:], s8b[:], AF.Exp, [s8b], [s8b], scale=-0.5)
                yield
                P.tt("dve", oc3, oc3, s8b[:].unsqueeze(2).to_broadcast([128, 8, 64]), ALU.mult, [oc, s8b], [oc])
                yield
                P.tt("pool", oc[:], oc[:], vb[:, LNW, :], ALU.mult, [oc, vb], [oc])
                yield
                P.tt("pool", oc[:], oc[:], vb[:, LNB, :], ALU.add, [oc, vb], [oc])
                yield
                P.tt("pool", oc[:], oc[:], bv[:], ALU.add, [oc, bv], [oc])
                yield
                P.tt("pool", yrw[:], oc[:], g_[:], ALU.mult, [oc, g_], [yrw])
                P.dma("act", scr_yrw[c], yrw[:], [yrw], (), sem=X["rq"])

            def prep_commit(d, k):
                yield from prepA(d, k)
                yield from commit(d, k)

            rr([prep_commit(0, 0), prep_commit(1, 0)])
            for k in range(nt):
                streams = []
                if k + 1 < nt:
                    streams += [prep_commit(0, k + 1), prep_commit(1, k + 1)]
                streams += [invchain(d, hg, k) for d in range(2) for hg in range(2)]
                if k >= 1:
                    streams += [combine(0, k - 1), combine(1, k - 1)]
                rr(streams)
            rr([combine(0, nt - 1), combine(1, nt - 1)])
            P.barrier()
            P.emit()

        bts = boundary_tiles(nt, seg_t)
        with contextlib.ExitStack() as st:
            alloc_stages(st, 4)
            gv = alloc_gv(st, (0,))
            vb, vb2, cum, amask, nmask = small_consts(st, True, rw=False)
            wna = P.sb([128, 8, 1536], BF16, "wna", st)
            load_w(wna, None, w_in, 8, 1536, col0=0)
            tabG = P.sb([128, 8, 16, 64], BF16, "tabG", st)
            tabT = P.sb([128, 8, 10, 64], BF16, "tabT", st)
            valid = P.sb([128, nb_t, 7, 2], F32, "valid", st)
            P.dma("sp", valid[:], c_val, (), [valid], sem=cq)
            with contextlib.ExitStack() as st_t:
                gm = P.sb([128, 16, 64], F32, "gm", st_t)
                tmk = P.sb([128, 10, 64], F32, "tmk", st_t)
                P.dma("sp", gm[:], c_gm, (), [gm], sem=cq)
                P.dma("sp", tmk[:], c_tm, (), [tmk], sem=cq)
                for h in range(8):
                    sgb = stages[h % len(stages)]
                    P.dma("sp" if h % 2 else "act", sgb[:, 0:1024].rearrange("p (a b) -> p a b", a=16), rbG[:, h, :, :], (), [sgb], sem=wq)
                    P.act(sgb[:, 0:1024], sgb[:, 0:1024], AF.Exp, [sgb], [sgb])
                    P.tt("dve", tabG[:, h, :, :], sgb[:, 0:1024].rearrange("p (a b) -> p a b", a=16), gm[:], ALU.mult, [sgb, gm], [tabG])
                    P.dma("act" if h % 2 else "sp", sgb[:, 1024:1664].rearrange("p (a b) -> p a b", a=10), rbT[:, h, :, :], (), [sgb], sem=wq)
                    P.act(sgb[:, 1024:1664], sgb[:, 1024:1664], AF.Exp, [sgb], [sgb])
                    P.tt("dve", tabT[:, h, :, :], sgb[:, 1024:1664].rearrange("p (a b) -> p a b", a=10), tmk[:], ALU.mult, [sgb, tmk], [tabT])
                P.barrier()
                P.emit()
            xt = [P.sb([128, D], F32, "xt%d" % i, st) for i in range(2)]
            ss = P.sb([128, 1], F32, "ss", st)
            nb = P.sb([128, D], BF16, "nb", st)
            nT = P.sb([128, 8, 128], BF16, "nT", st)
            NR = 5
            qTr = [P.sb([64, 8, 128], BF16, "qTr%d" % i, st) for i in range(NR)]
            KR = 8
            kTr = [P.sb([64, 8, 128], BF16, "kTr%d" % i, st) for i in range(KR)]
            v1r = [P.sb([128, 8, 65], BF16, "v1r%d" % i, st) for i in range(KR)]
            zna = P.sb([128, 1536], F32, "zna", st)
            tmp2 = P.sb([128, 512], F32, "tmp2", st)
            qk = P.sb([128, 2, 512], BF16, "qk", st)
            s16 = P.sb([128, 16], F32, "s16", st)
            ESs = [P.sb([128, 896], F32, "ES%d" % i, st) for i in range(2)]
            PTs = [P.sb([128, 896], BF16, "PT%d" % i, st) for i in range(2)]
            yas = [P.sb([128, 512], F32, "ya%d" % i, st) for i in range(2)]
            rden = P.sb([128, 8], F32, "rden", st)
            pTb = P.ps([128, 1024], BF16, "pTb", st)
            pss = [P.ps([128, 512], F32, "ps%d" % i, st) for i in range(7)]
            yaq = [newsem() for _ in range(2)]
            for r_ in v1r:
                P.op("pool", lambda e, r_=r_: e.memset(r_[:], 1.0), (), [r_])

            znab = [zna, P.sb([128, 1536], F32, "znab", st)]
            pTx = Buf(pTb.t, "pTx")
            pTq = Buf(pTb.t, "pTq")

            def xz(c):
                x_ = xt[c % 2]
                zc = znab[c % 2]
                P.dma("sp", x_[:], xs[c * 128:(c + 1) * 128, :], (), [x_])
                yield
                P.act(nb[:], x_[:], AF.Square, [x_], [nb, ss], accum=ss[:])
                yield
                P.ts("dve", ss[:], ss[:], 1.0 / D, EPS, ALU.mult, ALU.add, [ss], [ss])
                yield
                P.act(ss[:], ss[:], AF.Ln, [ss], [ss])
                yield
                P.act(ss[:], ss[:], AF.Exp, [ss], [ss], scale=-0.5)
                yield
                P.stt("dve", nb[:], x_[:], ss[:], gv[:, 0, :], ALU.mult, ALU.mult, [x_, ss, gv], [nb])
                yield
                for r in range(2):
                    for j in range(4):
                        k = r * 4 + j
                        P.tr(pTb[:, j * 128:(j + 1) * 128], nb[:, k * 128:(k + 1) * 128], identb[:], [nb, identb], [pTx], sig=(j == 3))
                    yield
                    P.cp("act", nT[:, r * 4:(r + 1) * 4, :].rearrange("p a b -> p (a b)"), pTb[:, 0:512], [pTx], [nT])
                    yield
                for g0 in range(3):
                    pz = pss[4]
                    for k in range(8):
                        P.mm(pz[:], nT[:, k, :], wna[:, k, g0 * 512:(g0 + 1) * 512], k == 0, k == 7, [nT, wna], [pz])
                    yield
                    P.cp("act", zc[:, g0 * 512:(g0 + 1) * 512], pz[:], [pz], [zc])
                    yield

            def qk_(c):
                zc = znab[c % 2]
                P.act(tmp2[:], zc[:, 0:512], AF.Square, [zc], [tmp2])
                yield
                P.red("dve", s16[:, 0:8], tmp2[:].rearrange("p (a b) -> p a b", a=8), [tmp2], [s16])
                yield
                P.act(tmp2[:], zc[:, 512:1024], AF.Square, [zc], [tmp2])
                yield
                P.red("dve", s16[:, 8:16], tmp2[:].rearrange("p (a b) -> p a b", a=8), [tmp2], [s16])
                yield
                P.ts("dve", s16[:], s16[:], 1.0 / 64, EPS, ALU.mult, ALU.add, [s16], [s16])
                yield
                P.act(s16[:], s16[:], AF.Ln, [s16], [s16])
                yield
                P.act(s16[:], s16[:], AF.Exp, [s16], [s16], scale=-0.5)
                yield
                for w_ in range(2):
                    z3 = zc[:, w_ * 512:(w_ + 1) * 512].rearrange("p (a b) -> p a b", a=8)
                    t3 = tmp2[:].rearrange("p (a b) -> p a b", a=8)
                    P.tt("dve", t3, z3, s16[:, w_ * 8:(w_ + 1) * 8].unsqueeze(2).to_broadcast([128, 8, 64]), ALU.mult, [zc, s16], [tmp2])
                    yield
                    gsrc = vb[:, QG, :] if w_ == 0 else vb2[:]
                    P.tt("pool", qk[:, w_, :], tmp2[:], gsrc, ALU.mult, [tmp2, vb, vb2], [qk])
                    yield
                v1 = v1r[c % KR]
                P.cp("act", v1[:, :, 0:64], zc[:, 1024:1536].rearrange("p (a b) -> p a b", a=8), [zc], [v1])
                qT = qTr[c % NR]
                kT = kTr[c % KR]
                for w_, dst in ((0, qT), (1, kT)):
                    for r in range(2):
                        for j in range(4):
                            h = r * 4 + j
                            P.tr(pTb[0:64, 512 + j * 128:512 + (j + 1) * 128], qk[:, w_, h * 64:(h + 1) * 64], identb[:], [qk, identb], [pTq], sig=(j == 3))
                        yield
                        P.cp("dve" if (w_ + r) % 2 else "act", dst[:, r * 4:(r + 1) * 4, :].rearrange("p a b -> p (a b)"), pTb[0:64, 512:1024], [pTq], [dst])
                        yield

            yaT = [[Buf(yas[i].t, "yaT") for _ in range(2)] for i in range(2)]
            rdens = [P.sb([128, 4], F32, "rden%d" % i, st) for i in range(2)]

            def attn_half(i, hh):
                js, kind, idx0 = na_slots(i, nt, seg_t)
                slots = [(s, j) for s, j in enumerate(js) if 0 <= j < nt]
                qT = qTr[i % NR]
                po = pss[5 + hh]
                ya = yas[i % 2]
                yat = yaT[i % 2][hh]
                ES, PT = ESs[hh], PTs[hh]
                pa_, pb_ = pss[hh * 2], pss[hh * 2 + 1]
                rd = rdens[hh]
                s_lo = slots[0][0]
                s_hi = slots[-1][0]
                n_ = s_hi - s_lo + 1
                for hl in range(4):
                    h = hh * 4 + hl
                    for s, j in slots:
                        bank = pa_ if s < 4 else pb_
                        P.mm(bank[:, (s % 4) * 128:(s % 4 + 1) * 128], kTr[j % KR][:, h, :], qT[:, h, :], True, True,
                             [kTr[j % KR], qT], [bank], sig=True)
                    yield
                    a1_ = min(s_hi, 3)
                    if s_lo <= 3:
                        P.act(ES[:, s_lo * 128:(a1_ + 1) * 128], pa_[:, s_lo * 128:(a1_ + 1) * 128], AF.Exp, [pa_], [ES], scale=0.125)
                    if s_hi >= 4:
                        b0_ = max(s_lo, 4)
                        P.act(ES[:, b0_ * 128:(s_hi + 1) * 128], pb_[:, (b0_ - 4) * 128:(s_hi - 3) * 128], AF.Exp, [pb_], [ES], scale=0.125)
                    yield
                    es4 = ES[:, s_lo * 128:(s_hi + 1) * 128].rearrange("p (a b) -> p a b", b=64)
                    pt4 = PT[:, s_lo * 128:(s_hi + 1) * 128].rearrange("p (a b) -> p a b", b=64)
                    if kind == "T":
                        P.tt("dve", pt4, es4, tabT[:, h, 2 * s_lo:2 * s_hi + 2, :], ALU.mult, [ES, tabT], [PT])
                    else:
                        P.tt("dve", es4, es4, tabG[:, h, 2 * s_lo + 1:2 * s_hi + 3, :], ALU.mult, [ES, tabG], [ES])
                        yield
                        bi = bts.index(i)
                        vv = valid[:, bi, s_lo:s_hi + 1, :].rearrange("p a b -> p (a b)").unsqueeze(2).to_broadcast([128, 2 * n_, 64])
                        P.tt("pool", pt4, es4, vv, ALU.mult, [ES, valid], [PT])
                    yield
                    for s, j in slots:
                        P.mm(po[:, hl * 65:(hl + 1) * 65], PT[:, s * 128:(s + 1) * 128], v1r[j % KR][:, h, :],
                             s == s_lo, s == s_hi, [PT, v1r[j % KR]], [po], sig=(s == s_hi))
                yield
                po3 = po[:, 0:260].rearrange("p (a b) -> p a b", a=4)
                P.rcp(rd[:], po3[:, :, 64], [po], [rd])
                yield
                P.tt("dve", ya[:, hh * 256:(hh + 1) * 256].rearrange("p (a b) -> p a b", a=4), po3[:, :, 0:64],
                     rd[:].unsqueeze(2).to_broadcast([128, 4, 64]), ALU.mult, [po, rd], [yat])

            def attn_store(i):
                P.dma("act", scr_ya[i], yas[i % 2][:], yaT[i % 2], ())

            LAG = 4
            rr([xz(0)])
            for it in range(nt + LAG):
                streams = [xz(it + 1) if it + 1 < nt else None, qk_(it) if it < nt else None]
                if it - LAG >= 0:
                    streams += [attn_half(it - LAG, 0), attn_half(it - LAG, 1)]
                rr(streams)
                if it - LAG >= 0:
                    attn_store(it - LAG)
            P.barrier()
            P.emit()

        with contextlib.ExitStack() as st:
            alloc_stages(st, 2)
            gv = alloc_gv(st, (0,))
            wgt = P.sb([128, 8, 2048], BF16, "wgt", st)
            wao = P.sb([128, 4, D], BF16, "wao", st)
            wbo = P.sb([128, 4, D], BF16, "wbo", st)
            wo = P.sb([128, 8, D], BF16, "wo", st)
            load_w(wgt, None, w_in, 8, 2048, col0=3392)
            load_w(wao, None, w_a_out, 4, D)
            load_w(wbo, None, w_b_out, 4, D)
            load_w(wo, None, w_o, 8, D)
            NP = 3
            xt = [P.sb([128, D], F32, "xt%d" % i, st) for i in range(NP)]
            yl = [P.sb([128, 2, 512], F32, "yl%d" % i, st) for i in range(NP)]
            ssC = [P.sb([128, 1], F32, "ss%d" % i, st) for i in range(NP)]
            nbC = [P.sb([128, D], BF16, "nb%d" % i, st) for i in range(NP)]
            nTC = [P.sb([128, 8, 128], BF16, "nT%d" % i, st) for i in range(NP)]
            ylb = [P.sb([128, 2, 512], BF16, "ylb%d" % i, st) for i in range(NP)]
            ylT = [P.sb([128, 8, 128], BF16, "ylT%d" % i, st) for i in range(NP)]
            gtsC = [P.sb([128, 2048], F32, "gts%d" % i, st) for i in range(NP)]
            mrgC = [P.sb([128, D], F32, "mrg%d" % i, st) for i in range(NP)]
            mrbC = [P.sb([128, D], BF16, "mrb%d" % i, st) for i in range(NP)]
            mTC = [P.sb([128, 8, 128], BF16, "mT%d" % i, st) for i in range(NP)]
            hhC = [P.sb([128, D], F32, "hh%d" % i, st) for i in range(NP)]
            pTs = P.ps([128, 1024], BF16, "pTC", st)
            pTC = [pTs] * NP
            pzC = [[P.ps([128, 512], F32, "pzC%d_%d" % (i, j), st) for j in range(3 if i == 0 else 2)] for i in range(NP)]
            xqC = [newsem() for _ in range(NP)]
            yqC = [newsem() for _ in range(NP)]
            hqC = [newsem() for _ in range(NP)]

            def tileC(i):
                p = i % NP
                x_, yl_, ss, nb, nT, pT = xt[p], yl[p], ssC[p], nbC[p], nTC[p], pTC[p]
                gts, mrg, mrb, mT, h_ = gtsC[p], mrgC[p], mrbC[p], mTC[p], hhC[p]
                pzs = pzC[p]
                cnt = [0]

                def pz_():
                    cnt[0] += 1
                    return pzs[cnt[0] % len(pzs)]
                P.dma("sp", x_[:], xs[i * 128:(i + 1) * 128, :], (), [x_], sem=xqC[p])
                P.dma("sp", yl_[:, 0, :], scr_ya[i], (), [yl_], sem=yqC[p])
                P.dma("sp", yl_[:, 1, :], scr_yrw[i], (), [yl_], sem=yqC[p])
                yield
                P.act(nb[:], x_[:], AF.Square, [x_], [nb, ss], accum=ss[:])
                yield
                P.ts("dve", ss[:], ss[:], 1.0 / D, EPS, ALU.mult, ALU.add, [ss], [ss])
                yield
                P.act(ss[:], ss[:], AF.Ln, [ss], [ss])
                yield
                P.act(ss[:], ss[:], AF.Exp, [ss], [ss], scale=-0.5)
                yield
                P.stt("dve", nb[:], x_[:], ss[:], gv[:, 0, :], ALU.mult, ALU.mult, [x_, ss, gv], [nb])
                P.cp("pool", ylb[p][:], yl_[:], [yl_], [ylb[p]])
                yield
                for k in range(8):
                    P.tr(pT[:, k * 128:(k + 1) * 128], nb[:, k * 128:(k + 1) * 128], identb[:], [nb, identb], [pT], sig=(k == 7))
                P.cp("act", nT[:].rearrange("p a b -> p (a b)"), pT[:], [pT], [nT])
                yield
                for k in range(8):
                    P.tr(pT[:, k * 128:(k + 1) * 128], ylb[p][:, k // 4, (k % 4) * 128:(k % 4 + 1) * 128], identb[:], [ylb[p], identb], [pT], sig=(k == 7))
                P.cp("dve", ylT[p][:].rearrange("p a b -> p (a b)"), pT[:], [pT], [ylT[p]])
                for g0 in range(4):
                    pz = pz_()
                    for k in range(8):
                        P.mm(pz[:], nT[:, k, :], wgt[:, k, g0 * 512:(g0 + 1) * 512], k == 0, k == 7, [nT, wgt], [pz])
                    yield
                    P.act(gts[:, g0 * 512:(g0 + 1) * 512], pz[:], AF.Sigmoid, [pz], [gts])
                for g0 in range(2):
                    pz = pz_()
                    for k in range(4):
                        P.mm(pz[:], ylT[p][:, k, :], wao[:, k, g0 * 512:(g0 + 1) * 512], k == 0, k == 3, [ylT[p], wao], [pz])
                    yield
                    P.tt("dve", mrg[:, g0 * 512:(g0 + 1) * 512], pz[:], gts[:, g0 * 512:(g0 + 1) * 512], ALU.mult, [pz, gts], [mrg])
                    pz2 = pz_()
                    for k in range(4):
                        P.mm(pz2[:], ylT[p][:, 4 + k, :], wbo[:, k, g0 * 512:(g0 + 1) * 512], k == 0, k == 3, [ylT[p], wbo], [pz2])
                    yield
                    P.tt("dve", gts[:, 1024 + g0 * 512:1024 + (g0 + 1) * 512], pz2[:], gts[:, 1024 + g0 * 512:1024 + (g0 + 1) * 512], ALU.mult, [pz2, gts], [gts])
                    yield
                    P.tt("pool", mrb[:, g0 * 512:(g0 + 1) * 512], mrg[:, g0 * 512:(g0 + 1) * 512], gts[:, 1024 + g0 * 512:1024 + (g0 + 1) * 512], ALU.add, [mrg, gts], [mrb])
                yield
                for k in range(8):
                    P.tr(pT[:, k * 128:(k + 1) * 128], mrb[:, k * 128:(k + 1) * 128], identb[:], [mrb, identb], [pT], sig=(k == 7))
                P.cp("act", mT[:].rearrange("p a b -> p (a b)"), pT[:], [pT], [mT])
                for g0 in range(2):
                    pz = pz_()
                    for k in range(8):
                        P.mm(pz[:], mT[:, k, :], wo[:, k, g0 * 512:(g0 + 1) * 512], k == 0, k == 7, [mT, wo], [pz])
                    yield
                    P.tt("dve", h_[:, g0 * 512:(g0 + 1) * 512], pz[:], x_[:, g0 * 512:(g0 + 1) * 512], ALU.add, [pz, x_], [h_])
                P.dma("act", scr_h[i], h_[:], [h_], (), sem=hqC[p])

            rr([chain_gens(tileC, range(0, nt, 3)), delayed_start(chain_gens(tileC, range(1, nt, 3)), 10),
                delayed_start(chain_gens(tileC, range(2, nt, 3)), 20)])
            P.barrier()
            P.emit()

        with contextlib.ExitStack() as st:
            gv = alloc_gv(st, (1, 2))
            wf1 = P.sb([128, 8, 4096], BF16, "wf1", st)
            wf2 = P.sb([128, 32, D], BF16, "wf2", st)
            wpg = P.sb([128, 8, D], BF16, "wpg", st)
            wpl = P.sb([128, 2, D], BF16, "wpl", st)
            with contextlib.ExitStack() as st_w:
                alloc_stages(st_w, 6)
                load_w(wf1, None, w_ff1, 8, 4096)
                load_w(wf2, None, w_ff2, 32, D)
                load_w(wpg, None, w_pgate, 8, D)
                load_w(wpl, None, w_ple, 2, D)
                P.barrier()
                P.emit()
            ht = [P.sb([128, D], F32, "ht%d" % i, st) for i in range(2)]
            pt_ = [P.sb([128, 256], F32, "pt%d" % i, st) for i in range(2)]
            ss3 = [P.sb([128, 1], F32, "ss%d" % i, st) for i in range(2)]
            nb3 = [P.sb([128, D], BF16, "nb%d" % i, st) for i in range(2)]
            nT3 = [P.sb([128, 8, 128], BF16, "nT%d" % i, st) for i in range(2)]
            hT = [P.sb([128, 32, 128], BF16, "hT%d" % i, st) for i in range(2)]
            rl3 = [P.sb([128, 512], F32, "rl%d" % i, st) for i in range(2)]
            pb3 = [P.sb([128, 256], BF16, "pb%d" % i, st) for i in range(2)]
            pT2 = [P.sb([128, 2, 128], BF16, "pT2%d" % i, st) for i in range(2)]
            gt3 = [P.sb([128, D], F32, "gt%d" % i, st) for i in range(2)]
            pT3 = [P.ps([128, 1024], BF16, "pT3%d" % i, st) for i in range(2)]
            pz3 = [[P.ps([128, 512], F32, "pz3%d_%d" % (i, j), st) for j in range(3)] for i in range(2)]
            hq3 = [newsem() for _ in range(2)]
            pq3 = [newsem() for _ in range(2)]
            oq3 = [newsem() for _ in range(2)]

            def norm3(p, src, grow):
                ss, nb, nT, pT = ss3[p], nb3[p], nT3[p], pT3[p]
                P.act(nb[:], src[:], AF.Square, [src], [nb, ss], accum=ss[:])
                yield
                P.ts("dve", ss[:], ss[:], 1.0 / D, EPS, ALU.mult, ALU.add, [ss], [ss])
                yield
                P.act(ss[:], ss[:], AF.Ln, [ss], [ss])
                yield
                P.act(ss[:], ss[:], AF.Exp, [ss], [ss], scale=-0.5)
                yield
                P.stt("dve", nb[:], src[:], ss[:], gv[:, grow, :], ALU.mult, ALU.mult, [src, ss, gv], [nb])
                yield
                for k in range(8):
                    P.tr(pT[:, k * 128:(k + 1) * 128], nb[:, k * 128:(k + 1) * 128], identb[:], [nb, identb], [pT], sig=(k == 7))
                yield
                P.cp("act", nT[:].rearrange("p a b -> p (a b)"), pT[:], [pT], [nT])
                yield

            def tile3(c):
                p = c % 2
                h_, p_, nT, pT, rl, gt_ = ht[p], pt_[p], nT3[p], pT3[p], rl3[p], gt3[p]
                pzs = pz3[p]
                cnt = [0]

                def pz_():
                    cnt[0] += 1
                    return pzs[cnt[0] % 3]
                P.dma("sp", h_[:], scr_h[c], (), [h_], sem=hq3[p])
                P.dma("sp", p_[:], pp[c * 128:(c + 1) * 128, :], (), [p_], sem=pq3[p])
                yield
                yield from norm3(p, h_, 0)
                for f4 in range(8):
                    pz = pz_()
                    for f in range(4):
                        fc = f4 * 4 + f
                        for k in range(8):
                            P.mm(pz[:, f * 128:(f + 1) * 128], wf1[:, k, fc * 128:(fc + 1) * 128], nT[:, k, :], k == 0, k == 7,
                                 [wf1, nT], [pz], sig=(k == 7 and f == 3))
                    yield
                    P.act(rl[:], pz[:], AF.Relu, [pz], [rl])
                    yield
                    P.tt("dve" if f4 % 2 else "pool", hT[p][:, f4 * 4:(f4 + 1) * 4, :].rearrange("p a b -> p (a b)"), rl[:], rl[:], ALU.mult, [rl], [hT[p]])
                for g0 in range(2):
                    pz = pz_()
                    for k in range(32):
                        P.mm(pz[:], hT[p][:, k, :], wf2[:, k, g0 * 512:(g0 + 1) * 512], k == 0, k == 31, [hT[p], wf2], [pz])
                    yield
                    P.tt("dve", h_[:, g0 * 512:(g0 + 1) * 512], pz[:], h_[:, g0 * 512:(g0 + 1) * 512], ALU.add, [pz, h_], [h_])
                yield
                yield from norm3(p, h_, 1)
                P.cp("pool", pb3[p][:], p_[:], [p_], [pb3[p]])
                yield
                for k in range(2):
                    P.tr(pT[:, k * 128:(k + 1) * 128], pb3[p][:, k * 128:(k + 1) * 128], identb[:], [pb3[p], identb], [pT], sig=(k == 1))
                yield
                P.cp("act", pT2[p][:].rearrange("p a b -> p (a b)"), pT[:, 0:256], [pT], [pT2[p]])
                for g0 in range(2):
                    pz = pz_()
                    for k in range(8):
                        P.mm(pz[:], nT[:, k, :], wpg[:, k, g0 * 512:(g0 + 1) * 512], k == 0, k == 7, [nT, wpg], [pz])
                    yield
                    P.act(gt_[:, g0 * 512:(g0 + 1) * 512], pz[:], AF.Sigmoid, [pz], [gt_])
                    pz2 = pz_()
                    for k in range(2):
                        P.mm(pz2[:], pT2[p][:, k, :], wpl[:, k, g0 * 512:(g0 + 1) * 512], k == 0, k == 1, [pT2[p], wpl], [pz2])
                    yield
                    P.tt("dve", gt_[:, g0 * 512:(g0 + 1) * 512], pz2[:], gt_[:, g0 * 512:(g0 + 1) * 512], ALU.mult, [pz2, gt_], [gt_])
                    yield
                    P.tt("pool", gt_[:, g0 * 512:(g0 + 1) * 512], gt_[:, g0 * 512:(g0 + 1) * 512], h_[:, g0 * 512:(g0 + 1) * 512], ALU.add, [gt_, h_], [gt_])
                P.dma("act", yout[c * 128:(c + 1) * 128, :], gt_[:], [gt_], (), sem=oq3[p])

            rr([chain_gens(tile3, range(0, nt, 2)), delayed_start(chain_gens(tile3, range(1, nt, 2)), 24)])
            P.barrier()
            P.emit()
    return nc


NT_FULL = 64
SEG_T_FULL = 16
_CACHE = {}


def make_in_maps(super_x, super_p, prompt_flags, W, nt, seg_t):
    rbG, rbT = expand_rel_bias(np.asarray(W["rel_bias"][0], np.float32))
    t8 = lambda v: np.tile(np.asarray(v, np.float32).reshape(-1), 8)
    vec512 = np.stack([W["w0_f"][0], W["w0_b"][0], W["a0"][0], W["k_k"][0], W["k_a"][0],
                       np.asarray(W["r_k"][0]).reshape(-1), W["ln_x_w"][0], W["ln_x_b"][0], t8(W["q_gain"][0])]).astype(np.float32)
    vec512b = t8(W["k_gain"][0])[None, :].astype(np.float32)
    gvec = np.stack([W["g_mix"][0], W["g_ffn"][0], W["g_ple"][0]]).astype(np.float32)
    w_up = np.stack([W["w_up_f"][0], W["w_up_b"][0], W["a_up"][0]]).astype(np.float32)
    shared = dict(w_in=W["w_in"][0], conv_w=W["conv_w"][0], vec512=vec512, vec512b=vec512b, gvec=gvec, w_up=w_up,
                  g_up=W["g_up"][0], w_a_out=W["w_a_out"][0], w_b_out=W["w_b_out"][0], w_o=W["w_o"][0],
                  w_ff1=W["w_ff1"][0], w_ff2=W["w_ff2"][0], w_ple=W["w_ple"][0], w_pgate=W["w_pgate"][0],
                  rbG=rbG, rbT=rbT)
    shared = {k: np.ascontiguousarray(np.asarray(v, np.float32)) for k, v in shared.items()}
    consts = {True: host_consts(nt, seg_t, True), False: host_consts(nt, seg_t, False)}
    maps = []
    for x, p, pf in zip(super_x, super_p, prompt_flags):
        m = dict(shared)
        hc = consts[bool(pf)]
        m.update(xs=np.ascontiguousarray(x, np.float32), pp=np.ascontiguousarray(p, np.float32),
                 flag=np.full((128, 1), 1.0 if pf else 0.0, np.float32),
                 c_ident=hc["ident"], c_cum=hc["cum"], c_amask=hc["amask"], c_nmask=hc["nmask"],
                 c_gm=hc["gm"], c_tm=hc["tm"], c_val=hc["val"])
        maps.append(m)
    return maps


def kernel(**inputs):
    nt, seg_t = NT_FULL, SEG_T_FULL
    xp = np.asarray(inputs["x_prompt"], np.float32)
    xsm = np.asarray(inputs["x_sample"], np.float32)
    pq = np.asarray(inputs["p_prompt"], np.float32)[0]
    psm = np.asarray(inputs["p_sample"], np.float32)[0]
    sx = [xp[0], xp[1]] + [xsm[4 * i:4 * i + 4].reshape(8192, D) for i in range(4)]
    sp_ = [pq[0], pq[1]] + [psm[4 * i:4 * i + 4].reshape(8192, 256) for i in range(4)]
    fl = [True, True, False, False, False, False]
    sx += [sx[4], sx[5]]
    sp_ += [sp_[4], sp_[5]]
    fl += [False, False]
    W = {k: np.asarray(v) for k, v in inputs.items() if k not in ("x_prompt", "x_sample", "p_prompt", "p_sample")}
    maps = make_in_maps(sx, sp_, fl, W, nt, seg_t)
    if "nc" not in _CACHE:
        _CACHE["nc"] = build(nt, seg_t)
    res = run_bass_kernel_spmd(_CACHE["nc"], maps, core_ids=list(range(NCORE)))
    outs = [np.asarray(r["yout"], np.float32) for r in res.results]
    y_prompt = np.stack([outs[0], outs[1]])
    y_sample = np.concatenate([outs[2 + i].reshape(4, 2048, D) for i in range(4)], 0)
    return (y_prompt, y_sample)
```
